# Optimizing a Trainium2 kernel written in Bass

```python
import math
import jax, jax.numpy as jnp
from jax import lax
import numpy as np

D_MODEL = 2048
BATCH = 8
SEQ = 4096
DEPTH = 4

SSM_EXPAND = 2
SSM_DI = SSM_EXPAND * D_MODEL
SSM_HEADDIM = 64
SSM_HEADS = SSM_DI // SSM_HEADDIM
SSM_GROUPS = 8
SSM_STATE = 128
SSM_CONV = 4
SSM_CONV_DIM = SSM_DI + 2 * SSM_GROUPS * SSM_STATE
SSM_IN = SSM_DI + SSM_CONV_DIM + SSM_HEADS

GDN_NK = D_MODEL // 128
GDN_NV = 2 * GDN_NK
GDN_DK = 128
GDN_DV = 128
GDN_CONV = 4
GDN_QKV = 2 * GDN_NK * GDN_DK + GDN_NV * GDN_DV
GDN_IN = GDN_QKV + GDN_NV * GDN_DV + 2 * GDN_NV

FFN_DIM = 256 * math.ceil(8 * D_MODEL / 3 / 256)
FFN_CONV = 3

CHUNK = 64
N_MOD = 6
EPS = 1e-6

kernel_name = "hybrid_ssd_gdn_convffn_adaln"


def rmsnorm(x, w, eps=EPS):
    xf = x.astype(jnp.float32)
    xf = xf * lax.rsqrt(jnp.mean(xf * xf, axis=-1, keepdims=True) + eps)
    return xf.astype(x.dtype) * w


def causal_dwconv(x, w, b=None):
    K, C = w.shape
    y = lax.conv_general_dilated(x, w[:, None, :].astype(x.dtype), window_strides=(1,),
                                 padding=[(K - 1, 0)], dimension_numbers=("NWC", "WIO", "NWC"),
                                 feature_group_count=C)
    if b is not None:
        y = y + b
    return y


def to_chunks(t, nc):
    return jnp.moveaxis(t.reshape(t.shape[0], nc, CHUNK, *t.shape[2:]), 1, 0)


def ssd_chunked(xs, dt, A, Bm, Cm):
    b, L, H, P = xs.shape
    G, N = Bm.shape[2], Bm.shape[3]
    Hg = H // G
    nc = L // CHUNK
    xdt = (xs * dt[..., None]).reshape(b, L, G, Hg, P)
    a = (dt * A).reshape(b, L, G, Hg)
    causal = jnp.tril(jnp.ones((CHUNK, CHUNK), dtype=bool))

    def step(state, inp):
        xc, ac, Bc, Cc = inp
        acum = jnp.cumsum(ac, axis=1)
        seg = acum[:, :, None] - acum[:, None]
        Lm = jnp.exp(jnp.where(causal[None, :, :, None, None], seg, -jnp.inf))
        CB = jnp.einsum("bign,bjgn->bijg", Cc, Bc)
        y_diag = jnp.einsum("bijgh,bjghp->bighp", CB[..., None] * Lm, xc)
        y_off = jnp.einsum("bign,bghpn->bighp", Cc, state) * jnp.exp(acum)[..., None]
        decay_end = jnp.exp(acum[:, -1:] - acum)
        state = (state * jnp.exp(acum[:, -1])[..., None, None]
                 + jnp.einsum("bjgn,bjgh,bjghp->bghpn", Bc, decay_end, xc))
        return state, y_diag + y_off

    state0 = jnp.zeros((b, G, Hg, P, N), jnp.float32)
    _, y = lax.scan(step, state0, (to_chunks(xdt, nc), to_chunks(a, nc),
                                   to_chunks(Bm, nc), to_chunks(Cm, nc)))
    return jnp.moveaxis(y, 0, 1).reshape(b, L, H, P)


def gated_delta_chunked(q, k, v, g, beta):
    b, L, H, DK = k.shape
    DV = v.shape[-1]
    nc = L // CHUNK
    incl = jnp.tril(jnp.ones((CHUNK, CHUNK), dtype=bool))
    strict = jnp.tril(jnp.ones((CHUNK, CHUNK), dtype=bool), -1)
    eye = jnp.eye(CHUNK, dtype=jnp.float32)

    def step(S, inp):
        qc, kc, vc, gc, bc = inp
        gcum = jnp.cumsum(gc, axis=1)
        gh = jnp.swapaxes(gcum, 1, 2)
        seg = gh[..., :, None] - gh[..., None, :]
        decay = jnp.exp(jnp.where(incl, seg, -jnp.inf))
        kb = kc * bc[..., None]
        Amat = jnp.where(strict, jnp.einsum("bihd,bjhd->bhij", kb, kc) * decay, 0.0)
        rhs = jnp.concatenate([vc * bc[..., None], kb * jnp.exp(gcum)[..., None]], axis=-1)
        sol = lax.linalg.triangular_solve(eye + Amat, jnp.swapaxes(rhs, 1, 2), left_side=True,
                                          lower=True, unit_diagonal=True)
        u, w = sol[..., :DV], sol[..., DV:]
        v_new = u - jnp.einsum("bhid,bhde->bhie", w, S)
        attn = jnp.einsum("bihd,bjhd->bhij", qc, kc) * decay
        o = (jnp.einsum("bihd,bhde->bhie", qc * jnp.exp(gcum)[..., None], S)
             + jnp.einsum("bhij,bhje->bhie", attn, v_new))
        g_last = gh[..., -1]
        S = (S * jnp.exp(g_last)[..., None, None]
             + jnp.einsum("bjhd,bhj,bhje->bhde", kc, jnp.exp(g_last[..., None] - gh), v_new))
        return S, o

    S0 = jnp.zeros((b, H, DK, DV), jnp.float32)
    _, o = lax.scan(step, S0, (to_chunks(q, nc), to_chunks(k, nc), to_chunks(v, nc),
                               to_chunks(g, nc), to_chunks(beta, nc)))
    return jnp.transpose(o, (1, 0, 3, 2, 4)).reshape(b, L, H, DV)


def mamba2_mixer(h, w_in, conv_w, conv_b, dt_bias, A_log, Dskip, norm_w, w_out):
    b, L, _ = h.shape
    proj = h @ w_in
    z = proj[..., :SSM_DI]
    xBC = proj[..., SSM_DI:SSM_DI + SSM_CONV_DIM]
    dt = proj[..., SSM_DI + SSM_CONV_DIM:]
    xBC = jax.nn.silu(causal_dwconv(xBC, conv_w, conv_b)).astype(jnp.float32)
    GN = SSM_GROUPS * SSM_STATE
    xs = xBC[..., :SSM_DI].reshape(b, L, SSM_HEADS, SSM_HEADDIM)
    Bm = xBC[..., SSM_DI:SSM_DI + GN].reshape(b, L, SSM_GROUPS, SSM_STATE)
    Cm = xBC[..., SSM_DI + GN:].reshape(b, L, SSM_GROUPS, SSM_STATE)
    dt = jax.nn.softplus(dt.astype(jnp.float32) + dt_bias.astype(jnp.float32))
    A = -jnp.exp(A_log.astype(jnp.float32))
    y = ssd_chunked(xs, dt, A, Bm, Cm) + Dskip.astype(jnp.float32)[:, None] * xs
    yg = (y.reshape(b, L, SSM_DI) * jax.nn.silu(z.astype(jnp.float32))).reshape(b, L, SSM_GROUPS, -1)
    yg = yg * lax.rsqrt(jnp.mean(yg * yg, axis=-1, keepdims=True) + 1e-5)
    yg = yg.reshape(b, L, SSM_DI).astype(h.dtype) * norm_w
    return yg @ w_out


def gdn_mixer(h, w_in, conv_w, dt_bias, A_log, norm_w, w_out):
    b, L, _ = h.shape
    proj = h @ w_in
    qkv = jax.nn.silu(causal_dwconv(proj[..., :GDN_QKV], conv_w)).astype(jnp.float32)
    z = proj[..., GDN_QKV:GDN_QKV + GDN_NV * GDN_DV].reshape(b, L, GDN_NV, GDN_DV)
    ba = proj[..., GDN_QKV + GDN_NV * GDN_DV:].astype(jnp.float32)
    bta, a = ba[..., :GDN_NV], ba[..., GDN_NV:]
    nqk = GDN_NK * GDN_DK
    q = qkv[..., :nqk].reshape(b, L, GDN_NK, GDN_DK)
    k = qkv[..., nqk:2 * nqk].reshape(b, L, GDN_NK, GDN_DK)
    v = qkv[..., 2 * nqk:].reshape(b, L, GDN_NV, GDN_DV)
    q = q * lax.rsqrt(jnp.sum(q * q, axis=-1, keepdims=True) + 1e-6) * (GDN_DK ** -0.5)
    k = k * lax.rsqrt(jnp.sum(k * k, axis=-1, keepdims=True) + 1e-6)
    rep = GDN_NV // GDN_NK
    q = jnp.repeat(q, rep, axis=2)
    k = jnp.repeat(k, rep, axis=2)
    beta = jax.nn.sigmoid(bta)
    g = -jnp.exp(A_log.astype(jnp.float32)) * jax.nn.softplus(a + dt_bias.astype(jnp.float32))
    o = gated_delta_chunked(q, k, v, g, beta)
    o = o * lax.rsqrt(jnp.mean(o * o, axis=-1, keepdims=True) + EPS)
    o = (o.astype(h.dtype) * norm_w) * jax.nn.silu(z)
    return o.reshape(b, L, GDN_NV * GDN_DV) @ w_out


def conv_ffn(h, w_up, conv_w, conv_b, w_down):
    u = causal_dwconv(h @ w_up, conv_w, conv_b)
    gate, val = u[..., :FFN_DIM], u[..., FFN_DIM:]
    return (jax.nn.silu(gate) * val) @ w_down


def setup_inputs(seed: int = 0) -> dict:
    key = jax.random.key(seed)
    ks = list(jax.random.split(key, 32))
    nA = (DEPTH + 1) // 2
    nB = DEPTH // 2

    def nrm(shape, scale):
        return scale * jax.random.normal(ks.pop(), shape, jnp.float32)

    def dt_bias_init(shape):
        u = jax.random.uniform(ks.pop(), shape, jnp.float32)
        dt = jnp.exp(u * (math.log(0.1) - math.log(0.001)) + math.log(0.001))
        return dt + jnp.log(-jnp.expm1(-dt))

    def a_log_init(shape):
        return jnp.log(jax.random.uniform(ks.pop(), shape, jnp.float32, 1.0, 16.0))

    d = D_MODEL
    return {
        "x": nrm((BATCH, SEQ, d), 1.0),
        "c": nrm((BATCH, d), 1.0),
        "ada_w": nrm((DEPTH, d, N_MOD * d), 0.5 * d ** -0.5),
        "ada_b": nrm((DEPTH, N_MOD * d), 0.01),
        "norm_mix_w": 1.0 + nrm((DEPTH, d), 0.02),
        "norm_ffn_w": 1.0 + nrm((DEPTH, d), 0.02),
        "ssm_w_in": nrm((nA, d, SSM_IN), d ** -0.5),
        "ssm_conv_w": nrm((nA, SSM_CONV, SSM_CONV_DIM), SSM_CONV ** -0.5),
        "ssm_conv_b": nrm((nA, SSM_CONV_DIM), 0.01),
        "ssm_dt_bias": dt_bias_init((nA, SSM_HEADS)),
        "ssm_A_log": a_log_init((nA, SSM_HEADS)),
        "ssm_D": 1.0 + nrm((nA, SSM_HEADS), 0.1),
        "ssm_norm_w": 1.0 + nrm((nA, SSM_DI), 0.02),
        "ssm_w_out": nrm((nA, SSM_DI, d), SSM_DI ** -0.5),
        "gdn_w_in": nrm((nB, d, GDN_IN), d ** -0.5),
        "gdn_conv_w": nrm((nB, GDN_CONV, GDN_QKV), GDN_CONV ** -0.5),
        "gdn_dt_bias": dt_bias_init((nB, GDN_NV)),
        "gdn_A_log": a_log_init((nB, GDN_NV)),
        "gdn_norm_w": 1.0 + nrm((nB, GDN_DV), 0.02),
        "gdn_w_out": nrm((nB, GDN_NV * GDN_DV, d), (GDN_NV * GDN_DV) ** -0.5),
        "ffn_w_up": nrm((DEPTH, d, 2 * FFN_DIM), d ** -0.5),
        "ffn_conv_w": nrm((DEPTH, FFN_CONV, 2 * FFN_DIM), FFN_CONV ** -0.5),
        "ffn_conv_b": nrm((DEPTH, 2 * FFN_DIM), 0.01),
        "ffn_w_down": nrm((DEPTH, FFN_DIM, d), FFN_DIM ** -0.5),
        "final_norm_w": 1.0 + nrm((d,), 0.02),
    }


def reference(x, c, ada_w, ada_b, norm_mix_w, norm_ffn_w,
              ssm_w_in, ssm_conv_w, ssm_conv_b, ssm_dt_bias, ssm_A_log, ssm_D, ssm_norm_w, ssm_w_out,
              gdn_w_in, gdn_conv_w, gdn_dt_bias, gdn_A_log, gdn_norm_w, gdn_w_out,
              ffn_w_up, ffn_conv_w, ffn_conv_b, ffn_w_down, final_norm_w):
    cs = jax.nn.silu(c)
    for i in range(DEPTH):
        mod = cs @ ada_w[i] + ada_b[i]
        sh_m, sc_m, g_m, sh_f, sc_f, g_f = [m[:, None, :] for m in jnp.split(mod, N_MOD, axis=-1)]
        h = rmsnorm(x, norm_mix_w[i]) * (1.0 + sc_m) + sh_m
        j = i // 2
        if i % 2 == 0:
            y = mamba2_mixer(h, ssm_w_in[j], ssm_conv_w[j], ssm_conv_b[j], ssm_dt_bias[j],
                             ssm_A_log[j], ssm_D[j], ssm_norm_w[j], ssm_w_out[j])
        else:
            y = gdn_mixer(h, gdn_w_in[j], gdn_conv_w[j], gdn_dt_bias[j], gdn_A_log[j],
                          gdn_norm_w[j], gdn_w_out[j])
        x = x + (g_m * y).astype(x.dtype)
        h = rmsnorm(x, norm_ffn_w[i]) * (1.0 + sc_f) + sh_f
        x = x + (g_f * conv_ffn(h, ffn_w_up[i], ffn_conv_w[i], ffn_conv_b[i], ffn_w_down[i])).astype(x.dtype)
    return rmsnorm(x, final_norm_w)
```

```python
import numpy as np
from contextlib import ExitStack
import ml_dtypes
import concourse.bass as bass
import concourse.mybir as mybir
from concourse.bass_utils import run_bass_kernel_spmd

F32 = mybir.dt.float32
BF16 = mybir.dt.bfloat16
AF = mybir.ActivationFunctionType
ALU = mybir.AluOpType

D = 2048
L = 4096
DEPTH = 4
KC = D // 128
SSM_DI = 4096
SSM_CONV_DIM = 6144
SSM_IN = 10304
GDN_QKV = 8192
GDN_IN = 12352
FFN = 5632
FC = FFN // 128
EPS = 1e-6
HALF = 2048


class Sched:
    def __init__(self, nc, es, n_dma=28):
        self.nc = nc
        self.eng = {"pe": nc.tensor, "act": nc.scalar, "dve": nc.vector,
                    "pool": nc.gpsimd, "sp": nc.sync}
        self.sems = {k: es.enter_context(nc.semaphore("sem_" + k))
                     for k in ("pe", "act", "dve", "pool")}
        self.cnt = {k: 0 for k in self.sems}
        self.dsems = [es.enter_context(nc.semaphore("dsem%d" % i)) for i in range(n_dma)]
        self.dcnt = [0] * n_dma
        self.dnext = 0
        self.waited = {k: {} for k in self.eng}
        self.lastw = {}
        self.readers = {}

    def _sem(self, k):
        return self.sems[k] if isinstance(k, str) else self.dsems[k]

    def _deps(self, reads, writes):
        deps = {}

        def add(tok):
            if tok is None:
                return
            k, v = tok
            if deps.get(k, 0) < v:
                deps[k] = v
        for r in reads:
            add(self.lastw.get(r))
        for w in writes:
            add(self.lastw.get(w))
            for k, v in self.readers.get(w, {}).items():
                add((k, v))
        return deps

    def _wait(self, e, deps):
        for k, v in deps.items():
            if e == "pe" and k == "pe":
                continue
            if self.waited[e].get(k, 0) >= v:
                continue
            self.eng[e].wait_ge(self._sem(k), v)
            self.waited[e][k] = v

    def _commit(self, tok, reads, writes):
        k, v = tok
        for r in reads:
            d = self.readers.setdefault(r, {})
            if d.get(k, 0) < v:
                d[k] = v
        for w in writes:
            self.lastw[w] = tok
            self.readers[w] = {}

    def op(self, e, fn, reads=(), writes=()):
        self._wait(e, self._deps(reads, writes))
        ins = fn(self.eng[e])
        self.cnt[e] += 1
        ins.then_inc(self.sems[e], 1)
        self._commit((e, self.cnt[e]), reads, writes)

    def dma(self, q, out, in_, reads=(), writes=(), **kw):
        self._wait(q, self._deps(reads, writes))
        i = self.dnext
        self.dnext = (i + 1) % len(self.dsems)
        if self.dcnt[i] > 0 and self.waited[q].get(i, 0) < self.dcnt[i] * 16:
            self.eng[q].wait_ge(self.dsems[i], self.dcnt[i] * 16)
            self.waited[q][i] = self.dcnt[i] * 16
        self.dcnt[i] += 1
        self.eng[q].dma_start(out=out, in_=in_, **kw).then_inc(self.dsems[i], 16)
        self._commit((i, self.dcnt[i] * 16), reads, writes)

    def barrier(self, engines=("pe", "act", "dve", "pool", "sp")):
        for e in engines:
            for k in self.sems:
                if k != e and self.cnt[k] > self.waited[e].get(k, 0):
                    self.eng[e].wait_ge(self.sems[k], self.cnt[k])
                    self.waited[e][k] = self.cnt[k]
            if e in self.sems and self.cnt[e] > self.waited[e].get(e, 0) and e != "pe":
                self.eng[e].wait_ge(self.sems[e], self.cnt[e])
                self.waited[e][e] = self.cnt[e]
            for i in range(len(self.dsems)):
                v = self.dcnt[i] * 16
                if v > self.waited[e].get(i, 0):
                    self.eng[e].wait_ge(self.dsems[i], v)
                    self.waited[e][i] = v
        self.lastw = {}
        self.readers = {}


class Stager:
    def __init__(self, S):
        self.S = S
        self.groups = []

    def begin(self):
        self.groups.append([])

    def op(self, *a, **k):
        self.groups[-1].append((a, k))

    def flush(self):
        n = max(len(g) for g in self.groups) if self.groups else 0
        for i in range(n):
            for g in self.groups:
                if i < len(g):
                    a, k = g[i]
                    self.S.op(*a, **k)
        self.groups = []


class Background:
    def __init__(self, S):
        self.S = S
        self.stages = []

    def add(self, stager):
        groups = stager.groups
        stager.groups = []
        n = max(len(g) for g in groups) if groups else 0
        for i in range(n):
            self.stages.append([g[i] for g in groups if i < len(g)])

    def add_dma(self, *a, **k):
        self.stages.append([("dma", a, k)])

    def pump(self, n=1):
        for _ in range(n):
            if not self.stages:
                return
            for item in self.stages.pop(0):
                if len(item) == 3:
                    self.S.dma(*item[1], **item[2])
                else:
                    self.S.op(*item[0], **item[1])

    def drain(self):
        while self.stages:
            self.pump(1)


def bcast(ap, axis, n):
    v = ap.unsqueeze(axis)
    shp = list(v.shape)
    shp[axis] = n
    return v.broadcast_to(shp)


class Ctx:
    pass


class NCProxy:
    def __init__(self, nc):
        object.__setattr__(self, "_nc", nc)
        object.__setattr__(self, "_n", 0)

    def __getattr__(self, k):
        return getattr(self._nc, k)

    def sbuf_tensor(self, name, shape, dt):
        object.__setattr__(self, "_n", self._n + 1)
        return self._nc.sbuf_tensor("%s_u%d" % (name, self._n), shape, dt)


def build_program(n_layers=DEPTH, do_mixer=True, do_ffn=True):
    nc = bass.Bass("TRN2", target_bir_lowering=False)
    c = Ctx()
    c.nc = NCProxy(nc)

    def din(name, shape, dt=F32):
        return nc.dram_tensor(name, list(shape), dt, kind="ExternalInput").ap()

    c.x = din("x", [L, D])
    c.cvec = din("cvec", [128, KC])
    c.ada_w = din("ada_w", [DEPTH, D, 6 * D])
    c.ada_b = din("ada_b", [DEPTH, 1, 6 * D])
    c.nw = din("nw", [128, DEPTH * 2 * KC + KC])
    c.ffn_w_up = din("ffn_w_up", [DEPTH, D, 2 * FFN])
    c.ffn_cw = din("ffn_cw", [DEPTH, 128, 2 * FC * 4])
    c.ffn_w_down = din("ffn_w_down", [DEPTH, FFN, D])
    c.ssm_w_in = din("ssm_w_in", [2, D, SSM_IN])
    c.ssm_cw = din("ssm_cw", [2, 128, 48 * 5])
    c.ssm_hp = din("ssm_hp", [2, 64, 2])
    c.ssm_vec = din("ssm_vec", [2, 128, 64])
    c.ssm_w_out = din("ssm_w_out", [2, SSM_DI, D])
    c.gdn_w_in = din("gdn_w_in", [2, D, GDN_IN])
    c.gdn_cw = din("gdn_cw", [2, 128, 64 * 5])
    c.gdn_hp = din("gdn_hp", [2, 64, 2])
    c.gdn_nw = din("gdn_nw", [2, 128, 1])
    c.gdn_nwx = din("gdn_nwx", [2, 128, 32])
    c.gdn_w_out = din("gdn_w_out", [2, 4096, D])
    c.lmask_d = din("lmask", [128, 7 * 128], BF16)
    c.cb_d = din("cbias", [128, 2])
    c.maskT_d = din("maskT", [128, 128])
    c.maskS_d = din("maskS", [128, 128])
    c.lmT0_d = din("lmT0", [128, 128], BF16)
    c.identb_d = din("identb", [128, 128], BF16)
    c.ident_f = din("ident_f", [128, 128])
    c.ones_b = din("ones_b", [128, 128], BF16)
    c.out = nc.dram_tensor("out", [L, D], F32, kind="ExternalOutput").ap()
    c.xres = nc.dram_tensor("xres", [D, L], F32, kind="Internal").ap()
    c.gbuf = nc.dram_tensor("gbuf", [FFN, L], BF16, kind="Internal").ap()
    c.pbuf = nc.dram_tensor("pbuf", [GDN_IN, L], BF16, kind="Internal").ap()
    c.rawbuf = nc.dram_tensor("rawbuf", [64, L], F32, kind="Internal").ap()
    c.dtd = nc.dram_tensor("dtd", [64, L], F32, kind="Internal").ap()
    c.acd = nc.dram_tensor("acd", [64, L], F32, kind="Internal").ap()
    c.ybuf = nc.dram_tensor("ybuf", [4096, L], BF16, kind="Internal").ap()

    with ExitStack() as es:
        S = Sched(nc, es)
        c.S = S
        c.ps = es.enter_context(nc.psum_tensor("ps", [128, 8, 512], F32))
        c.identf = es.enter_context(nc.sbuf_tensor("identf", [128, 128], F32))
        c.onesb = es.enter_context(nc.sbuf_tensor("onesb", [128, 128], BF16))
        c.nwt = es.enter_context(nc.sbuf_tensor("nwt", [128, DEPTH * 2 * KC + KC], F32))
        c.modv = es.enter_context(nc.sbuf_tensor("modv", [128, DEPTH, 6 * KC], F32))
        c.gain = es.enter_context(nc.sbuf_tensor("gain", [128, DEPTH, 2, KC], F32))
        c.epsc = es.enter_context(nc.sbuf_tensor("epsc", [128, 1], F32))
        S.dma("sp", c.identf[:], c.ident_f, writes=["identf"])
        S.dma("sp", c.onesb[:], c.ones_b, writes=["onesb"])
        S.dma("sp", c.nwt[:], c.nw, writes=["nwt"])
        S.op("dve", lambda e: e.memset(c.epsc[:], EPS), writes=["epsc"])
        c.cb = es.enter_context(nc.sbuf_tensor("cb", [128, 2], F32))
        S.dma("sp", c.cb[:], c.cb_d, writes=["cb"])

        phase_transpose_in(c)
        phase_mods(c, n_layers)
        for l in range(n_layers):
            if do_mixer:
                if l % 2 == 0:
                    phase_ssm_layer(c, l)
                else:
                    phase_gdn_layer(c, l)
            if do_ffn:
                phase_ffn_up(c, l)
                phase_ffn_down(c, l)
        phase_final(c)
        S.barrier()
    return nc


def psb(c, b):
    return c.ps[:, b, :]


def phase_transpose_in(c):
    nc, S = c.nc, c.S
    with ExitStack() as es:
        xin = [es.enter_context(nc.sbuf_tensor("ti_x%d" % i, [128, 4, D], F32)) for i in range(2)]
        xT = [es.enter_context(nc.sbuf_tensor("ti_t%d" % i, [128, KC, 512], F32)) for i in range(2)]
        xv = c.x.rearrange("(n j p) d -> n p j d", j=4, p=128)
        xr = c.xres.rearrange("(kc p) t -> p kc t", p=128)
        NT = L // 512
        S.dma("sp", xin[0][:], xv[0], writes=["ti_x0"])
        for n in range(NT):
            b = n % 2
            if n + 1 < NT:
                S.dma("sp", xin[1 - b][:], xv[n + 1], writes=["ti_x%d" % (1 - b)])
            for kc in range(KC):
                bank = kc % 8

                def mm(e, kc=kc, bank=bank, b=b):
                    ins = None
                    for j in range(4):
                        ins = e.transpose(out=c.ps[:, bank, j * 128:(j + 1) * 128],
                                          in_=xin[b][:, j, kc * 128:(kc + 1) * 128],
                                          identity=c.identf[:])
                    return ins
                S.op("pe", mm, reads=["ti_x%d" % b, "identf"], writes=["ps%d" % bank])
                eng = "act" if kc % 2 == 0 else "dve"
                if eng == "act":
                    S.op("act", lambda e, kc=kc, bank=bank, b=b: e.copy(out=xT[b][:, kc, :], in_=psb(c, bank)),
                         reads=["ps%d" % bank], writes=["ti_t%d_%d" % (b, kc)])
                else:
                    S.op("dve", lambda e, kc=kc, bank=bank, b=b: e.tensor_copy(out=xT[b][:, kc, :], in_=psb(c, bank)),
                         reads=["ps%d" % bank], writes=["ti_t%d_%d" % (b, kc)])
            S.dma("sp", xr[:, :, n * 512:(n + 1) * 512], xT[b][:],
                  reads=["ti_t%d_%d" % (b, kc) for kc in range(KC)], writes=["xres_%d" % n])
        S.barrier()


def phase_mods(c, n_layers):
    nc, S = c.nc, c.S
    with ExitStack() as es:
        cs = es.enter_context(nc.sbuf_tensor("md_cs", [128, KC], F32))
        cs2 = es.enter_context(nc.sbuf_tensor("md_cs2", [128, KC, 2], F32))
        wt = [es.enter_context(nc.sbuf_tensor("md_w%d" % i, [128, KC, 512], F32)) for i in range(2)]
        brow = es.enter_context(nc.sbuf_tensor("md_b", [1, 6 * D], F32))
        mrow = es.enter_context(nc.sbuf_tensor("md_m", [1, 6 * D], F32))
        S.dma("sp", cs[:], c.cvec, writes=["md_cs"])
        S.op("act", lambda e: e.activation(out=cs[:], in_=cs[:], func=AF.Silu),
             reads=["md_cs"], writes=["md_cs"])
        S.op("dve", lambda e: e.tensor_copy(out=cs2[:], in_=bcast(cs[:], 2, 2)),
             reads=["md_cs"], writes=["md_cs2"])
        NG = 6 * D // 512
        for l in range(n_layers):
            wv = c.ada_w[l].rearrange("(kc p) n -> p kc n", p=128)
            S.dma("sp", brow[:], c.ada_b[l], writes=["md_b"])
            S.dma("sp", wt[0][:], wv[:, :, 0:512], writes=["md_w0"])
            for g in range(NG):
                b = g % 2
                if g + 1 < NG:
                    S.dma("sp" if g % 2 == 0 else "act", wt[1 - b][:], wv[:, :, (g + 1) * 512:(g + 2) * 512],
                          writes=["md_w%d" % (1 - b)])
                bank = g % 4

                def mm(e, b=b, bank=bank):
                    ins = None
                    for kc in range(KC):
                        ins = e.matmul(c.ps[0:2, bank, :], lhsT=cs2[:, kc, :], rhs=wt[b][:, kc, :],
                                       start=(kc == 0), stop=(kc == KC - 1))
                    return ins
                S.op("pe", mm, reads=["md_cs2", "md_w%d" % b], writes=["ps%d" % bank])
                S.op("dve", lambda e, g=g, bank=bank: e.tensor_tensor(
                    out=mrow[0:1, g * 512:(g + 1) * 512], in0=c.ps[0:1, bank, :],
                    in1=brow[0:1, g * 512:(g + 1) * 512], op=ALU.add),
                    reads=["ps%d" % bank, "md_b"], writes=["md_m%d" % g])

            def tr(e):
                ins = None
                for j in range(6 * KC):
                    ins = e.transpose(out=c.ps[:, 4, j:j + 1], in_=mrow[0:1, j * 128:(j + 1) * 128],
                                      identity=c.identf[0:1, 0:1])
                return ins
            S.op("pe", tr, reads=["md_m%d" % g for g in range(NG)] + ["identf"], writes=["ps4"])
            S.op("dve", lambda e, l=l: e.tensor_copy(out=c.modv[:, l, :], in_=c.ps[:, 4, 0:6 * KC]),
                 reads=["ps4"], writes=["modv"])
            for which, off in ((0, 1 * KC), (1, 4 * KC)):
                S.op("dve", lambda e, l=l, which=which, off=off: e.scalar_tensor_tensor(
                    out=c.gain[:, l, which, :], in0=c.modv[:, l, off:off + KC], scalar=1.0,
                    in1=c.nwt[:, (l * 2 + which) * KC:(l * 2 + which + 1) * KC],
                    op0=ALU.add, op1=ALU.mult),
                    reads=["modv", "nwt"], writes=["gain"])
        S.barrier()


def emit_norm(c, tiles, hT, hname, t0, ntok, gain_ap, shift_ap, tag):
    nc, S = c.nc, c.S
    TW = 256
    xr = c.xres.rearrange("(kc p) t -> p kc t", p=128)
    xt, sq, rs = tiles["xt"], tiles["sq"], tiles["rs"]
    nt = ntok // TW
    S.dma("sp", xt[0][:], xr[:, :, t0:t0 + TW], reads=["xres_%d" % (t0 // 512)], writes=[tag + "xt0"])
    for i in range(nt):
        b = i % 2
        if i + 1 < nt:
            tt = t0 + (i + 1) * TW
            S.dma("sp", xt[1 - b][:], xr[:, :, tt:tt + TW], reads=["xres_%d" % (tt // 512)],
                  writes=[tag + "xt%d" % (1 - b)])
        S.op("act", lambda e, b=b: e.activation(out=sq[:], in_=xt[b][:], func=AF.Square),
             reads=[tag + "xt%d" % b], writes=[tag + "sq"])
        bank = 7

        def mm(e):
            ins = None
            for kc in range(KC):
                ins = e.matmul(c.ps[:, bank, 0:TW], lhsT=c.onesb[:], rhs=sq[:, kc, :],
                               start=(kc == 0), stop=(kc == KC - 1))
            return ins
        S.op("pe", mm, reads=[tag + "sq", "onesb"], writes=["ps7"])
        S.op("act", lambda e: e.activation(out=rs[:], in_=c.ps[:, bank, 0:TW], func=AF.Sqrt,
                                            scale=1.0 / D, bias=c.epsc[:]),
             reads=["ps7", "epsc"], writes=[tag + "rs"])
        S.op("dve", lambda e: e.reciprocal(out=rs[:], in_=rs[:]), reads=[tag + "rs"], writes=[tag + "rs"])
        S.op("dve", lambda e, b=b: e.tensor_tensor(out=xt[b][:], in0=xt[b][:], in1=bcast(rs[:], 1, KC),
                                                   op=ALU.mult),
             reads=[tag + "xt%d" % b, tag + "rs"], writes=[tag + "xt%d" % b])
        for kc in range(KC):
            eng = "act" if kc % 2 == 0 else "pool"
            o = hT[:, kc, i * TW:(i + 1) * TW]
            if eng == "act":
                S.op("act", lambda e, kc=kc, b=b, o=o: e.activation(
                    out=o, in_=xt[b][:, kc, :], func=AF.Identity,
                    scale=gain_ap[:, kc:kc + 1], bias=shift_ap[:, kc:kc + 1]),
                    reads=[tag + "xt%d" % b, "gain", "modv"], writes=[hname + "_%d" % kc])
            else:
                S.op("pool", lambda e, kc=kc, b=b, o=o: e.tensor_scalar(
                    out=o, in0=xt[b][:, kc, :], scalar1=gain_ap[:, kc:kc + 1],
                    scalar2=shift_ap[:, kc:kc + 1], op0=ALU.mult, op1=ALU.add),
                    reads=[tag + "xt%d" % b, "gain", "modv"], writes=[hname + "_%d" % kc])


def alloc_norm_tiles(c, es, tag):
    nc = c.nc
    return {
        "xt": [es.enter_context(nc.sbuf_tensor(tag + "xt%d" % i, [128, KC, 256], F32)) for i in range(2)],
        "sq": es.enter_context(nc.sbuf_tensor(tag + "sq", [128, KC, 256], BF16)),
        "rs": es.enter_context(nc.sbuf_tensor(tag + "rs", [128, 256], F32)),
    }


def phase_ffn_up(c, l):
    nc, S = c.nc, c.S
    N = HALF
    with ExitStack() as es:
        hT = es.enter_context(nc.sbuf_tensor("fu_h", [128, KC, N], BF16))
        nt = alloc_norm_tiles(c, es, "fu_")
        cw = es.enter_context(nc.sbuf_tensor("fu_cw", [128, 2 * FC, 4], F32))
        halo = es.enter_context(nc.sbuf_tensor("fu_halo", [128, 2 * FC, 2], F32))
        wg = [es.enter_context(nc.sbuf_tensor("fu_wg%d" % i, [128, KC, 128], BF16)) for i in range(2)]
        wv = [es.enter_context(nc.sbuf_tensor("fu_wv%d" % i, [128, KC, 128], BF16)) for i in range(2)]
        pre = [[es.enter_context(nc.sbuf_tensor("fu_pre%d%d" % (i, j), [128, N + 2], F32)) for j in range(2)]
               for i in range(2)]
        acc = [es.enter_context(nc.sbuf_tensor("fu_acc%d" % j, [128, N], F32)) for j in range(2)]
        sg = es.enter_context(nc.sbuf_tensor("fu_sg", [128, N], F32))
        gout = [es.enter_context(nc.sbuf_tensor("fu_go%d" % i, [128, N], BF16)) for i in range(2)]
        S.dma("sp", cw[:], c.ffn_cw[l].rearrange("p (c k) -> p c k", k=4), writes=["fu_cw"])
        wsrc = c.ffn_w_up[l].rearrange("(kc p) n -> p kc n", p=128)
        gain_ap = c.gain[:, l, 1, :]
        shift_ap = c.modv[:, l, 3 * KC:4 * KC]
        for half in range(2):
            t0 = half * N
            emit_norm(c, nt, hT, "fu_h", t0, N, gain_ap, shift_ap, "fu_")

            def load_w(j, b):
                S.dma("pool", wg[b][:], wsrc[:, :, j * 128:(j + 1) * 128], writes=["fu_wg%d" % b])
                S.dma("pool", wv[b][:], wsrc[:, :, FFN + j * 128:FFN + (j + 1) * 128], writes=["fu_wv%d" % b])
            load_w(0, 0)
            for j in range(FC):
                b = j % 2
                if j + 1 < FC:
                    load_w(j + 1, 1 - b)
                for gv in range(2):
                    ch = gv * FC + j
                    if half == 0:
                        S.op("pool", lambda e, gv=gv, b=b: e.memset(pre[b][gv][:, 0:2], 0.0),
                             writes=["fu_pre%d%d_h" % (b, gv)])
                    else:
                        S.op("pool", lambda e, gv=gv, b=b, ch=ch: e.tensor_copy(out=pre[b][gv][:, 0:2], in_=halo[:, ch, :]),
                             reads=["fu_halo"], writes=["fu_pre%d%d_h" % (b, gv)])
                for st in range(2):
                    for gv in range(2):
                        w = wg[b] if gv == 0 else wv[b]
                        wname = ("fu_wg%d" if gv == 0 else "fu_wv%d") % b
                        for tt in range(2):
                            tok = (st * 2 + tt) * 512
                            bank = (st % 2) * 4 + gv * 2 + tt
                            def mm(e, w=w, tok=tok, bank=bank):
                                ins = None
                                for kc in range(KC):
                                    ins = e.matmul(c.ps[:, bank, :], lhsT=w[:, kc, :], rhs=hT[:, kc, tok:tok + 512],
                                                   start=(kc == 0), stop=(kc == KC - 1))
                                return ins
                            S.op("pe", mm, reads=[wname] + ["fu_h_%d" % kc for kc in range(KC)], writes=["ps%d" % bank])
                            S.op("act", lambda e, gv=gv, b=b, tok=tok, bank=bank: e.copy(
                                out=pre[b][gv][:, 2 + tok:2 + tok + 512], in_=psb(c, bank)),
                                reads=["ps%d" % bank], writes=["fu_pre%d%d_%d" % (b, gv, tok)])
                for gv in range(2):
                    ch = gv * FC + j
                    eng = "dve"
                    prn = ["fu_pre%d%d_%d" % (b, gv, t * 512) for t in range(4)] + ["fu_pre%d%d_h" % (b, gv)]
                    p_ = pre[b][gv]
                    a_ = acc[gv]
                    S.op(eng, lambda e, p_=p_, a_=a_, ch=ch: e.tensor_scalar(
                        out=a_[:], in0=p_[:, 0:N], scalar1=cw[:, ch, 0:1], scalar2=cw[:, ch, 3:4],
                        op0=ALU.mult, op1=ALU.add), reads=prn + ["fu_cw"], writes=["fu_acc%d" % gv])
                    for k in (1, 2):
                        S.op(eng, lambda e, p_=p_, a_=a_, ch=ch, k=k: e.scalar_tensor_tensor(
                            out=a_[:], in0=p_[:, k:k + N], scalar=cw[:, ch, k:k + 1], in1=a_[:],
                            op0=ALU.mult, op1=ALU.add), reads=prn + ["fu_cw", "fu_acc%d" % gv],
                            writes=["fu_acc%d" % gv])
                    if half == 0:
                        S.op("pool", lambda e, p_=p_, ch=ch: e.tensor_copy(out=halo[:, ch, :], in_=p_[:, N:N + 2]),
                             reads=prn, writes=["fu_halo"])
                S.op("act", lambda e: e.activation(out=sg[:], in_=acc[0][:], func=AF.Silu),
                     reads=["fu_acc0"], writes=["fu_sg"])
                S.op("dve", lambda e, b=b: e.tensor_tensor(out=gout[b][:], in0=sg[:], in1=acc[1][:], op=ALU.mult),
                     reads=["fu_sg", "fu_acc1"], writes=["fu_go%d" % b])
                S.dma("sp", c.gbuf[j * 128:(j + 1) * 128, t0:t0 + N], gout[b][:],
                      reads=["fu_go%d" % b], writes=["gbuf_%d_%d" % (j, half)])
        S.barrier()


def phase_down(c, wsrc_dram, nfc, src_buf, src_tag, gate_ap, rowscale_dram=None):
    nc, S = c.nc, c.S
    TW = 256
    OH = 8
    with ExitStack() as es:
        wd = es.enter_context(nc.sbuf_tensor("fd_w", [128, nfc, OH * 128], BF16))
        gT = [es.enter_context(nc.sbuf_tensor("fd_g%d" % i, [128, nfc, TW], BF16)) for i in range(2)]
        xo = [es.enter_context(nc.sbuf_tensor("fd_x%d" % i, [128, OH, TW], F32)) for i in range(2)]
        xn = [es.enter_context(nc.sbuf_tensor("fd_n%d" % i, [128, OH, TW], F32)) for i in range(2)]
        if rowscale_dram is not None:
            rsc = es.enter_context(nc.sbuf_tensor("fd_rs", [128, nfc], F32))
            S.dma("sp", rsc[:], rowscale_dram, writes=["fd_rs"])
        wsrc = wsrc_dram.rearrange("(fc p) n -> p fc n", p=128)
        gsrc = src_buf.rearrange("(fc p) t -> p fc t", p=128)
        xr = c.xres.rearrange("(kc p) t -> p kc t", p=128)
        NT = L // TW
        nq = 4
        qs = nfc // nq
        for oh in range(2):
            for q in range(nq):
                S.dma("pool", wd[:, q * qs:(q + 1) * qs, :], wsrc[:, q * qs:(q + 1) * qs, oh * 1024:(oh + 1) * 1024],
                      writes=["fd_w%d" % q])
            wnames = ["fd_w%d" % q for q in range(nq)]
            if rowscale_dram is not None:
                for fc in range(nfc):
                    S.op("act", lambda e, fc=fc: e.activation(out=wd[:, fc, :], in_=wd[:, fc, :], func=AF.Identity,
                                                              scale=rsc[:, fc:fc + 1]),
                         reads=["fd_rs", "fd_w%d" % (fc // qs)], writes=["fd_w%d" % (fc // qs)])

            def load(t, b):
                S.dma("sp", gT[b][:], gsrc[:, :, t * TW:(t + 1) * TW], writes=["fd_g%d" % b])
                S.dma("act", xo[b][:], xr[:, oh * OH:(oh + 1) * OH, t * TW:(t + 1) * TW],
                      reads=["xres_%d" % ((t * TW) // 512)], writes=["fd_x%d" % b])
            load(0, 0)
            for t in range(NT):
                b = t % 2
                if t + 1 < NT:
                    load(t + 1, 1 - b)
                for oc in range(OH):
                    bank = oc % 8

                    def mm(e, oc=oc, bank=bank, b=b):
                        ins = None
                        for fc in range(nfc):
                            ins = e.matmul(c.ps[:, bank, 0:TW], lhsT=wd[:, fc, oc * 128:(oc + 1) * 128],
                                           rhs=gT[b][:, fc, :], start=(fc == 0), stop=(fc == nfc - 1))
                        return ins
                    S.op("pe", mm, reads=wnames + ["fd_g%d" % b], writes=["ps%d" % bank])
                    col = oh * OH + oc
                    S.op("dve", lambda e, oc=oc, bank=bank, b=b, col=col: e.scalar_tensor_tensor(
                        out=xn[b][:, oc, :], in0=c.ps[:, bank, 0:TW], scalar=gate_ap[:, col:col + 1],
                        in1=xo[b][:, oc, :], op0=ALU.mult, op1=ALU.add),
                        reads=["ps%d" % bank, "fd_x%d" % b, "modv"], writes=["fd_n%d_%d" % (b, oc)])
                S.dma("sp", xr[:, oh * OH:(oh + 1) * OH, t * TW:(t + 1) * TW], xn[b][:],
                      reads=["fd_n%d_%d" % (b, oc) for oc in range(OH)], writes=["xres_%d" % ((t * TW) // 512)])
        S.barrier()


def phase_ffn_down(c, l):
    phase_down(c, c.ffn_w_down[l], FC, c.gbuf, "gbuf", c.modv[:, l, 5 * KC:6 * KC])


def phase_final(c):
    nc, S = c.nc, c.S
    TW = 256
    with ExitStack() as es:
        nt = alloc_norm_tiles(c, es, "fn_")
        hf = [es.enter_context(nc.sbuf_tensor("fn_h%d" % i, [128, KC, TW], F32)) for i in range(2)]
        ot = [es.enter_context(nc.sbuf_tensor("fn_o%d" % i, [128, 2, D], F32)) for i in range(2)]
        zero = es.enter_context(nc.sbuf_tensor("fn_z", [128, KC], F32))
        S.op("dve", lambda e: e.memset(zero[:], 0.0), writes=["fn_z"])
        fw = c.nwt[:, DEPTH * 2 * KC:DEPTH * 2 * KC + KC]
        ov = c.out.rearrange("(n j p) d -> n p j d", j=2, p=128)
        for t in range(L // TW):
            b = t % 2
            emit_norm_f32(c, nt, hf[b], "fn_h%d" % b, t * TW, fw, zero[:], "fn_")
            for j in range(2):
                for kc in range(KC):
                    bank = (kc // 4) % 4 + 4 * (j % 2)
                    if bank == 7:
                        bank = 3
                    q = kc % 4

                    def mm(e, j=j, kc=kc, bank=bank, q=q, b=b):
                        return e.transpose(out=c.ps[:, bank, q * 128:(q + 1) * 128],
                                           in_=hf[b][:, kc, j * 128:(j + 1) * 128], identity=c.identf[:])
                    S.op("pe", mm, reads=["fn_h%d_%d" % (b, kc), "identf"], writes=["ps%d" % bank])
                    if q == 3:
                        k0 = kc - 3
                        eng = "act" if (kc // 4) % 2 == 0 else "dve"
                        if eng == "act":
                            S.op("act", lambda e, j=j, k0=k0, bank=bank, b=b: e.copy(
                                out=ot[b][:, j, k0 * 128:(k0 + 4) * 128], in_=psb(c, bank)),
                                reads=["ps%d" % bank], writes=["fn_o%d_%d_%d" % (b, j, k0)])
                        else:
                            S.op("dve", lambda e, j=j, k0=k0, bank=bank, b=b: e.tensor_copy(
                                out=ot[b][:, j, k0 * 128:(k0 + 4) * 128], in_=psb(c, bank)),
                                reads=["ps%d" % bank], writes=["fn_o%d_%d_%d" % (b, j, k0)])
            S.dma("sp", ov[t], ot[b][:],
                  reads=["fn_o%d_%d_%d" % (b, j, k0) for j in range(2) for k0 in (0, 4, 8, 12)],
                  writes=["out_%d" % t])
        S.barrier()


def emit_norm_f32(c, tiles, hT, hname, t0, gain_ap, shift_ap, tag):
    nc, S = c.nc, c.S
    TW = 256
    xr = c.xres.rearrange("(kc p) t -> p kc t", p=128)
    xt, sq, rs = tiles["xt"], tiles["sq"], tiles["rs"]
    b = (t0 // TW) % 2
    S.dma("sp", xt[b][:], xr[:, :, t0:t0 + TW], reads=["xres_%d" % (t0 // 512)], writes=[tag + "xt%d" % b])
    S.op("act", lambda e: e.activation(out=sq[:], in_=xt[b][:], func=AF.Square),
         reads=[tag + "xt%d" % b], writes=[tag + "sq"])

    def mm(e):
        ins = None
        for kc in range(KC):
            ins = e.matmul(c.ps[:, 7, 0:TW], lhsT=c.onesb[:], rhs=sq[:, kc, :],
                           start=(kc == 0), stop=(kc == KC - 1))
        return ins
    S.op("pe", mm, reads=[tag + "sq", "onesb"], writes=["ps7"])
    S.op("act", lambda e: e.activation(out=rs[:], in_=c.ps[:, 7, 0:TW], func=AF.Sqrt,
                                        scale=1.0 / D, bias=c.epsc[:]),
         reads=["ps7", "epsc"], writes=[tag + "rs"])
    S.op("dve", lambda e: e.reciprocal(out=rs[:], in_=rs[:]), reads=[tag + "rs"], writes=[tag + "rs"])
    S.op("dve", lambda e: e.tensor_tensor(out=xt[b][:], in0=xt[b][:], in1=bcast(rs[:], 1, KC), op=ALU.mult),
         reads=[tag + "xt%d" % b, tag + "rs"], writes=[tag + "xt%d" % b])
    for kc in range(KC):
        eng = "act" if kc % 2 == 0 else "pool"
        if eng == "act":
            S.op("act", lambda e, kc=kc: e.activation(
                out=hT[:, kc, :], in_=xt[b][:, kc, :], func=AF.Identity,
                scale=gain_ap[:, kc:kc + 1], bias=shift_ap[:, kc:kc + 1]),
                reads=[tag + "xt%d" % b, "nwt", "fn_z"], writes=[hname + "_%d" % kc])
        else:
            S.op("pool", lambda e, kc=kc: e.tensor_scalar(
                out=hT[:, kc, :], in0=xt[b][:, kc, :], scalar1=gain_ap[:, kc:kc + 1],
                scalar2=None, op0=ALU.mult),
                reads=[tag + "xt%d" % b, "nwt"], writes=[hname + "_%d" % kc])


def cols128(v):
    v = np.asarray(v)
    return np.ascontiguousarray(v.reshape(-1, 128).T)


def host_inputs(inp, b):
    f32 = np.float32
    m = {}
    m["x"] = np.ascontiguousarray(inp["x"][b])
    m["cvec"] = cols128(inp["c"][b]).astype(f32)
    m["ada_w"] = inp["ada_w"]
    m["ada_b"] = np.ascontiguousarray(inp["ada_b"].reshape(DEPTH, 1, 6 * D))
    nw = []
    for l in range(DEPTH):
        nw.append(cols128(inp["norm_mix_w"][l]))
        nw.append(cols128(inp["norm_ffn_w"][l]))
    nw.append(cols128(inp["final_norm_w"]))
    m["nw"] = np.ascontiguousarray(np.concatenate(nw, axis=1)).astype(f32)
    m["ffn_w_up"] = inp["ffn_w_up"]
    cw = np.zeros((DEPTH, 128, 2 * FC, 4), f32)
    for l in range(DEPTH):
        for k in range(3):
            cw[l, :, :, k] = cols128(inp["ffn_conv_w"][l, k])
        cw[l, :, :, 3] = cols128(inp["ffn_conv_b"][l])
    m["ffn_cw"] = cw.reshape(DEPTH, 128, 2 * FC * 4)
    m["ffn_w_down"] = inp["ffn_w_down"]
    m["ssm_w_in"] = inp["ssm_w_in"]
    scw = np.zeros((2, 128, 48, 5), f32)
    for j in range(2):
        for k in range(4):
            scw[j, :, :, k] = cols128(inp["ssm_conv_w"][j, k])
        scw[j, :, :, 4] = cols128(inp["ssm_conv_b"][j])
    m["ssm_cw"] = scw.reshape(2, 128, 48 * 5)
    m["ssm_hp"] = np.ascontiguousarray(np.stack([inp["ssm_dt_bias"], inp["ssm_A_log"]], axis=-1)).astype(f32)
    sv = np.zeros((2, 128, 64), f32)
    for j in range(2):
        sv[j, :, 0:32] = cols128(np.repeat(inp["ssm_D"][j], 64))
        sv[j, :, 32:64] = cols128(inp["ssm_norm_w"][j])
    m["ssm_vec"] = sv
    m["ssm_w_out"] = inp["ssm_w_out"]
    m["gdn_w_in"] = inp["gdn_w_in"]
    gcw = np.zeros((2, 128, 64, 5), f32)
    for j in range(2):
        for k in range(4):
            gcw[j, :, :, k] = cols128(inp["gdn_conv_w"][j, k])
    m["gdn_cw"] = gcw.reshape(2, 128, 64 * 5)
    ghp = np.zeros((2, 64, 2), f32)
    ghp[:, 32:64, 0] = inp["gdn_dt_bias"]
    ghp[:, 32:64, 1] = inp["gdn_A_log"]
    m["gdn_hp"] = ghp
    m["gdn_nw"] = np.ascontiguousarray(inp["gdn_norm_w"].reshape(2, 128, 1)).astype(f32)
    m["gdn_nwx"] = np.ascontiguousarray(np.repeat(inp["gdn_norm_w"].reshape(2, 128, 1), 32, axis=2)).astype(f32)
    m["gdn_w_out"] = inp["gdn_w_out"]
    lm = np.zeros((128, 7, 128), f32)
    ii = np.arange(128)
    for lv in range(7):
        s_ = 1 << lv
        rowm = (ii % (2 * s_)) >= s_
        colm = (ii % (2 * s_)) < s_
        same = (ii[:, None] // (2 * s_)) == (ii[None, :] // (2 * s_))
        lm[:, lv, :] = (same & rowm[:, None] & colm[None, :]).astype(f32) + np.eye(128, dtype=f32)
    m["lmask"] = lm.reshape(128, 7 * 128).astype(ml_dtypes.bfloat16)
    m["lmT0"] = np.ascontiguousarray(lm[:, 0, :].T).astype(ml_dtypes.bfloat16)
    cbv = np.zeros((128, 2), f32)
    cbv[:, 0] = 128.0e-6
    cbv[:, 1] = 1.0e-6
    m["cbias"] = cbv
    m["maskT"] = np.triu(np.ones((128, 128), f32))
    m["maskS"] = np.triu(np.ones((128, 128), f32), 1)
    m["identb"] = np.eye(128).astype(ml_dtypes.bfloat16)
    m["ident_f"] = np.eye(128, dtype=f32)
    m["ones_b"] = np.ones((128, 128), dtype=ml_dtypes.bfloat16)
    return m


_NC_CACHE = {}


def kernel(**inputs):
    inp = {k: np.asarray(v) for k, v in inputs.items()}
    if "full" not in _NC_CACHE:
        _NC_CACHE["full"] = build_program()
    nc = _NC_CACHE["full"]
    B = inp["x"].shape[0]
    in_maps = [host_inputs(inp, b) for b in range(B)]
    res = run_bass_kernel_spmd(nc, in_maps, core_ids=list(range(B)))
    return np.stack([np.asarray(r["out"]) for r in res.results], axis=0).astype(np.float32)


def phase_inproj(c, l, wsrc_dram, chunks, cw_dram, ncw, gain_ap, shift_ap):
    nc, S = c.nc, c.S
    N = HALF
    with ExitStack() as es:
        hT = es.enter_context(nc.sbuf_tensor("ip_h", [128, KC, N], BF16))
        nt = alloc_norm_tiles(c, es, "ip_")
        cw = es.enter_context(nc.sbuf_tensor("ip_cw", [128, ncw, 5], F32))
        halo = es.enter_context(nc.sbuf_tensor("ip_halo", [128, ncw, 3], F32))
        w = [es.enter_context(nc.sbuf_tensor("ip_w%d" % i, [128, KC, 128], BF16)) for i in range(3)]
        pre = [es.enter_context(nc.sbuf_tensor("ip_pre%d" % i, [128, N + 3], F32)) for i in range(2)]
        acc = [es.enter_context(nc.sbuf_tensor("ip_acc%d" % i, [128, N], F32)) for i in range(2)]
        ob = [es.enter_context(nc.sbuf_tensor("ip_ob%d" % i, [128, N], BF16)) for i in range(2)]
        rawt = es.enter_context(nc.sbuf_tensor("ip_raw", [64, N], F32))
        S.dma("sp", cw[:], cw_dram.rearrange("p (c k) -> p c k", k=5), writes=["ip_cw"])
        wsrc = wsrc_dram.rearrange("(kc p) n -> p kc n", p=128)
        nchunks = len(chunks)
        for half in range(2):
            t0 = half * N
            emit_norm(c, nt, hT, "ip_h", t0, N, gain_ap, shift_ap, "ip_")

            def load_w(ci):
                col0, width = chunks[ci][0], chunks[ci][1]
                S.dma("pool", w[ci % 3][:, :, 0:width], wsrc[:, :, col0:col0 + width], writes=["ip_w%d" % (ci % 3)])
            load_w(0)
            if nchunks > 1:
                load_w(1)
            for ci, (col0, width, kind, dst, cwi) in enumerate(chunks):
                wb = ci % 3
                pb = ci % 2
                if ci + 2 < nchunks:
                    load_w(ci + 2)
                if kind == "conv":
                    if half == 0:
                        S.op("pool", lambda e, pb=pb: e.memset(pre[pb][:, 0:3], 0.0), writes=["ip_pre%d_h" % pb])
                    else:
                        S.op("pool", lambda e, pb=pb, cwi=cwi: e.tensor_copy(out=pre[pb][:, 0:3], in_=halo[:, cwi, :]),
                             reads=["ip_halo"], writes=["ip_pre%d_h" % pb])
                for t in range(4):
                    bank = (ci % 2) * 4 + t
                    tok = t * 512

                    def mm(e, wb=wb, width=width, bank=bank, tok=tok):
                        ins = None
                        for kc in range(KC):
                            ins = e.matmul(c.ps[0:width, bank, :], lhsT=w[wb][:, kc, 0:width], rhs=hT[:, kc, tok:tok + 512],
                                           start=(kc == 0), stop=(kc == KC - 1))
                        return ins
                    S.op("pe", mm, reads=["ip_w%d" % wb] + ["ip_h_%d" % kc for kc in range(KC)], writes=["ps%d" % bank])
                    if kind == "conv":
                        S.op("act", lambda e, pb=pb, bank=bank, tok=tok: e.copy(out=pre[pb][:, 3 + tok:3 + tok + 512], in_=psb(c, bank)),
                             reads=["ps%d" % bank], writes=["ip_pre%d_%d" % (pb, t)])
                    elif kind == "silu":
                        S.op("act", lambda e, pb=pb, bank=bank, tok=tok: e.activation(out=ob[pb][:, tok:tok + 512], in_=psb(c, bank), func=AF.Silu),
                             reads=["ps%d" % bank], writes=["ip_ob%d_%d" % (pb, t)])
                    else:
                        S.op("act", lambda e, bank=bank, tok=tok, width=width: e.copy(out=rawt[0:width, tok:tok + 512], in_=c.ps[0:width, bank, :]),
                             reads=["ps%d" % bank], writes=["ip_raw_%d" % t])
                if kind == "conv":
                    prn = ["ip_pre%d_%d" % (pb, t) for t in range(4)] + ["ip_pre%d_h" % pb]
                    p_, a_ = pre[pb], acc[pb]
                    S.op("dve", lambda e, p_=p_, a_=a_, cwi=cwi: e.tensor_scalar(
                        out=a_[:], in0=p_[:, 0:N], scalar1=cw[:, cwi, 0:1], scalar2=cw[:, cwi, 4:5],
                        op0=ALU.mult, op1=ALU.add), reads=prn + ["ip_cw"], writes=["ip_acc%d" % pb])
                    for k in (1, 2, 3):
                        S.op("dve", lambda e, p_=p_, a_=a_, cwi=cwi, k=k: e.scalar_tensor_tensor(
                            out=a_[:], in0=p_[:, k:k + N], scalar=cw[:, cwi, k:k + 1], in1=a_[:],
                            op0=ALU.mult, op1=ALU.add), reads=prn + ["ip_cw", "ip_acc%d" % pb], writes=["ip_acc%d" % pb])
                    if half == 0:
                        S.op("pool", lambda e, p_=p_, cwi=cwi: e.tensor_copy(out=halo[:, cwi, :], in_=p_[:, N:N + 3]),
                             reads=prn, writes=["ip_halo"])
                    S.op("act", lambda e, a_=a_, pb=pb: e.activation(out=ob[pb][:], in_=a_[:], func=AF.Silu),
                         reads=["ip_acc%d" % pb], writes=["ip_ob%d_%d" % (pb, t) for t in range(4)])
                if kind in ("conv", "silu"):
                    S.dma("sp", c.pbuf[dst:dst + 128, t0:t0 + N], ob[pb][:],
                          reads=["ip_ob%d_%d" % (pb, t) for t in range(4)], writes=["pbuf_%d_%d" % (dst, half)])
                else:
                    S.dma("sp", c.rawbuf[0:width, t0:t0 + N], rawt[0:width, :],
                          reads=["ip_raw_%d" % t for t in range(4)], writes=["rawbuf_%d" % half])
        S.barrier()


def psbf(c, bank):
    return c.ps[:, bank, :].bitcast(BF16)


def phase_ssd_pre(c, j):
    nc, S = c.nc, c.S
    with ExitStack() as es:
        raw = es.enter_context(nc.sbuf_tensor("sp_raw", [64, L], F32))
        dtt = es.enter_context(nc.sbuf_tensor("sp_dt", [64, L], F32))
        ac = es.enter_context(nc.sbuf_tensor("sp_ac", [64, L], F32))
        ones = es.enter_context(nc.sbuf_tensor("sp_one", [64, 128], F32))
        hp = es.enter_context(nc.sbuf_tensor("sp_hp", [64, 2], F32))
        an = es.enter_context(nc.sbuf_tensor("sp_an", [64, 1], F32))
        S.dma("sp", raw[:], c.rawbuf[0:64, :], writes=["raw"])
        S.dma("sp", hp[:], c.ssm_hp[j], writes=["hp"])
        S.op("pool", lambda e: e.memset(ones[:], 1.0), writes=["ones"])
        S.op("act", lambda e: e.activation(out=dtt[:], in_=raw[:], func=AF.Exp, bias=hp[:, 0:1]), reads=["raw", "hp"], writes=["dt"])
        S.op("act", lambda e: e.activation(out=dtt[:], in_=dtt[:], func=AF.Ln, bias=1.0), reads=["dt"], writes=["dt"])
        S.op("act", lambda e: e.activation(out=an[:], in_=hp[:, 1:2], func=AF.Exp), reads=["hp"], writes=["an"])
        S.op("dve", lambda e: e.tensor_scalar(out=an[:], in0=an[:], scalar1=-1.0, scalar2=None, op0=ALU.mult), reads=["an"], writes=["an"])
        S.op("dve", lambda e: e.tensor_scalar(out=raw[:], in0=dtt[:], scalar1=an[:, 0:1], scalar2=None, op0=ALU.mult),
             reads=["dt", "an"], writes=["raw"])
        for ch in range(L // 128):
            S.op("dve", lambda e, ch=ch: e.tensor_tensor_scan(out=ac[:, ch * 128:(ch + 1) * 128], data0=ones[:],
                                                            data1=raw[:, ch * 128:(ch + 1) * 128], initial=0.0,
                                                            op0=ALU.mult, op1=ALU.add),
                 reads=["raw", "ones"], writes=["ac%d" % ch])
        S.dma("sp", c.dtd[0:64, :], dtt[:], reads=["dt"], writes=["dtd"])
        S.dma("sp", c.acd[0:64, :], ac[:], reads=["ac%d" % ch for ch in range(L // 128)], writes=["acd"])
        S.barrier()


def phase_ssd_scan(c, l):
    nc, S = c.nc, c.S
    j = l // 2
    NCH = L // 128
    phase_ssd_pre(c, j)
    with ExitStack() as es:
        def sb(name, shape, dt=F32):
            return es.enter_context(nc.sbuf_tensor(name, shape, dt))
        St = sb("ss_S", [128, 8, 512])
        Sb = sb("ss_Sb", [128, 8, 512], BF16)
        maskT = sb("ss_mask", [128, 128])
        identb = sb("ss_idb", [128, 128], BF16)
        vec = sb("ss_vec", [128, 64])
        eps5 = sb("ss_eps", [128, 1])
        xsT = [sb("ss_xs%d" % i, [128, 32, 128], BF16) for i in range(2)]
        BC = [sb("ss_bc%d" % i, [128, 16, 128], BF16) for i in range(2)]
        zT = [sb("ss_z%d" % i, [128, 32, 128], BF16) for i in range(2)]
        dta = [sb("ss_dta%d" % i, [64, 2, 128]) for i in range(2)]
        Abc = [sb("ss_abc%d" % i, [128, 64, 128]) for i in range(1)]
        dtk = sb("ss_dtk", [128, 128])
        E = sb("ss_E", [128, 64])
        dend = sb("ss_dend", [128, 64])
        EL = sb("ss_EL", [128, 64])
        xdt = sb("ss_xdt", [128, 64, 64], BF16)
        xdtd = sb("ss_xdtd", [128, 64, 64], BF16)
        Btok = sb("ss_Btok", [128, 8, 128], BF16)
        CBm = sb("ss_CBm", [128, 8, 128])
        sg = [sb("ss_sg%d" % i, [128, 8, 128]) for i in range(2)]
        Mt = sb("ss_Mt", [128, 64, 128], BF16)
        ytok2 = [sb("ss_ytok%d" % i, [128, 32, 128]) for i in range(2)]
        tmpy = [sb("ss_tmpy%d" % i, [128, 8, 64]) for i in range(2)]
        tmpS = [sb("ss_tmpS%d" % i, [128, 8, 64]) for i in range(2)]
        yg = [sb("ss_yg%d" % i, [128, 4, 128]) for i in range(2)]
        sq = [sb("ss_sq%d" % i, [128, 4, 128], BF16) for i in range(2)]
        rstd = [sb("ss_rstd%d" % i, [128, 128]) for i in range(2)]
        ygT = [sb("ss_ygT%d" % i, [128, 32, 128], BF16) for i in range(1)]

        S.dma("sp", maskT[:], c.maskT_d, writes=["mask"])
        S.dma("sp", identb[:], c.identb_d, writes=["idb"])
        S.dma("sp", vec[:], c.ssm_vec[j], writes=["vec"])
        S.op("pool", lambda e: e.memset(eps5[:], 1e-5), writes=["eps5"])
        S.op("pool", lambda e: e.memset(St[:], 0.0), writes=["S%d" % g for g in range(8)])
        S.op("pool", lambda e: e.memset(Sb[:], 0.0), writes=["Sb%d" % g for g in range(8)])

        def cols(ch):
            return slice(ch * 128, (ch + 1) * 128)

        def load(ch):
            b = ch % 2
            S.dma("sp", xsT[b][:], c.pbuf[4096:8192, cols(ch)].rearrange("(cc p) t -> p cc t", p=128), writes=["xs%d" % b])
            S.dma("act", BC[b][:], c.pbuf[8192:10240, cols(ch)].rearrange("(cc p) t -> p cc t", p=128), writes=["bc%d" % b])
            S.dma("act", zT[b][:], c.pbuf[0:4096, cols(ch)].rearrange("(cc p) t -> p cc t", p=128), writes=["z%d" % b])
            S.dma("sp", dta[b][:, 0, :], c.dtd[0:64, cols(ch)], writes=["dta%d" % b])
            S.dma("sp", dta[b][:, 1, :], c.acd[0:64, cols(ch)], writes=["dta%d" % b])

        def load_abc(ch):
            src = bass.AP(tensor=c.acd.tensor, offset=ch * 128, ap=[[0, 128], [L, 64], [1, 128]])
            S.dma("sp", Abc[0][:], src, writes=["abc"])

        load(0)
        STG = Stager(S)
        TG = Stager(S)
        BG = Background(S)
        for ch in range(NCH):
            b = ch % 2
            if ch == 0:
                load_abc(0)
            ytok = ytok2[ch % 2]
            yb = ch % 2
            def t1(e, b=b):
                e.transpose(out=c.ps[:, 0, 0:64], in_=dta[b][0:64, 0, :], identity=c.identf[0:64, 0:64])
                return e.transpose(out=c.ps[:, 0, 64:128], in_=dta[b][0:64, 1, :], identity=c.identf[0:64, 0:64])
            S.op("pe", t1, reads=["dta%d" % b, "identf"], writes=["ps0"])
            S.op("dve", lambda e: e.tensor_copy(out=dtk[:], in_=c.ps[:, 0, 0:128]), reads=["ps0"], writes=["dtk"])
            S.op("act", lambda e: e.activation(out=E[:], in_=dtk[:, 64:128], func=AF.Exp), reads=["dtk"], writes=["E"])
            S.op("dve", lambda e: e.tensor_tensor(out=dend[:], in0=Abc[0][:, :, 127], in1=dtk[:, 64:128], op=ALU.subtract),
                 reads=["abc", "dtk"], writes=["dend"])
            S.op("act", lambda e: e.activation(out=dend[:], in_=dend[:], func=AF.Exp), reads=["dend"], writes=["dend"])
            S.op("act", lambda e: e.activation(out=EL[:], in_=Abc[0][:, :, 127], func=AF.Exp), reads=["abc"], writes=["EL"])
            BG.pump(4)
            for q4 in range(4):
                bank = 1 + q4 % 2

                def t2(e, q4=q4, bank=bank, b=b):
                    ins = None
                    for q in range(8):
                        ins = e.transpose(out=psbf(c, bank)[:, q * 128:(q + 1) * 128], in_=xsT[b][:, q4 * 8 + q, :], identity=identb[:])
                    return ins
                S.op("pe", t2, reads=["xs%d" % b, "idb"], writes=["ps%d" % bank])
                S.op("dve", lambda e, q4=q4, bank=bank: e.tensor_tensor(
                    out=xdt[:, q4 * 16:(q4 + 1) * 16, :], in0=psbf(c, bank).rearrange("p (h d) -> p h d", d=64),
                    in1=bcast(dtk[:, q4 * 16:(q4 + 1) * 16], 2, 64), op=ALU.mult),
                    reads=["ps%d" % bank, "dtk"], writes=["xdt%d" % q4])
                BG.pump(4)
            S.op("dve", lambda e: e.tensor_tensor(out=xdtd[:], in0=xdt[:], in1=bcast(dend[:], 2, 64), op=ALU.mult),
                 reads=["xdt%d" % q for q in range(4)] + ["dend"], writes=["xdtd"])
            def t3(e, b=b):
                ins = None
                for g in range(8):
                    ins = e.transpose(out=psbf(c, 3)[:, g * 128:(g + 1) * 128], in_=BC[b][:, g, :], identity=identb[:])
                return ins
            S.op("pe", t3, reads=["bc%d" % b, "idb"], writes=["ps3"])
            S.op("act", lambda e: e.copy(out=Btok[:], in_=psbf(c, 3).rearrange("p (g n) -> p g n", n=128)), reads=["ps3"], writes=["Btok"])
            BG.pump(4)
            for hb in range(2):
                bank = 1 + hb

                def t4(e, hb=hb, bank=bank, b=b):
                    ins = None
                    for gg in range(4):
                        g = hb * 4 + gg
                        ins = e.matmul(c.ps[:, bank, gg * 128:(gg + 1) * 128], lhsT=BC[b][:, g, :], rhs=BC[b][:, 8 + g, :], start=True, stop=True)
                    return ins
                S.op("pe", t4, reads=["bc%d" % b], writes=["ps%d" % bank])
                S.op("dve", lambda e, hb=hb, bank=bank: e.tensor_tensor(
                    out=CBm[:, hb * 4:(hb + 1) * 4, :], in0=c.ps[:, bank, :].rearrange("p (g i) -> p g i", i=128),
                    in1=bcast(maskT[:], 1, 4), op=ALU.mult), reads=["ps%d" % bank, "mask"], writes=["CBm%d" % hb])
            BG.pump(8)
            for g in range(8):
                STG.begin()
                k = g % 2
                STG.op("dve", lambda e, g=g, k=k: e.tensor_tensor(out=sg[k][:], in0=Abc[0][:, g * 8:(g + 1) * 8, :],
                                                               in1=bcast(dtk[:, 64 + g * 8:64 + (g + 1) * 8], 2, 128), op=ALU.subtract),
                     reads=["abc", "dtk"], writes=["sg%d" % k])
                STG.op("act", lambda e, k=k: e.activation(out=sg[k][:], in_=sg[k][:], func=AF.Relu, scale=-1.0), reads=["sg%d" % k], writes=["sg%d" % k])
                STG.op("act", lambda e, k=k: e.activation(out=sg[k][:], in_=sg[k][:], func=AF.Exp, scale=-1.0), reads=["sg%d" % k], writes=["sg%d" % k])
                STG.op("pool", lambda e, g=g, k=k: e.tensor_tensor(out=Mt[:, g * 8:(g + 1) * 8, :], in0=sg[k][:],
                                                                in1=bcast(CBm[:, g, :], 1, 8), op=ALU.mult),
                     reads=["sg%d" % k, "CBm%d" % (g // 4)], writes=["Mt%d" % g])
                if g % 2 == 1:
                    STG.flush()
                    BG.pump(4)
            if ch + 1 < NCH:
                load_abc(ch + 1)
            BG.drain()
            if ch + 1 < NCH:
                load(ch + 1)
            for g in range(8):
                STG.begin()
                k = g % 2
                bo, bd = k * 2, k * 2 + 1
                STG.op("pe", lambda e, g=g, bo=bo, b=b: e.matmul(c.ps[:, bo, :], lhsT=BC[b][:, 8 + g, :], rhs=Sb[:, g, :], start=True, stop=True),
                     reads=["bc%d" % b, "Sb%d" % g], writes=["ps%d" % bo])

                def t6(e, g=g, bd=bd):
                    ins = None
                    for hh in range(8):
                        h = g * 8 + hh
                        ins = e.matmul(c.ps[:, bd, hh * 64:(hh + 1) * 64], lhsT=Mt[:, h, :], rhs=xdt[:, h, :], start=True, stop=True)
                    return ins
                STG.op("pe", t6, reads=["Mt%d" % g, "xdt%d" % (g // 2)], writes=["ps%d" % bd])
                STG.op("dve", lambda e, g=g, k=k, bo=bo: e.tensor_tensor(
                    out=tmpy[k][:], in0=c.ps[:, bo, :].rearrange("p (h d) -> p h d", d=64),
                    in1=bcast(E[:, g * 8:(g + 1) * 8], 2, 64), op=ALU.mult), reads=["ps%d" % bo, "E"], writes=["tmpy%d" % k])
                STG.op("dve", lambda e, g=g, k=k, bd=bd: e.tensor_tensor(
                    out=ytok[:, g * 4:(g + 1) * 4, :].rearrange("p a b -> p (a b)"), in0=tmpy[k][:].rearrange("p h d -> p (h d)"),
                    in1=c.ps[:, bd, :], op=ALU.add), reads=["ps%d" % bd, "tmpy%d" % k], writes=["ytok%d_%d" % (yb, g)])
                if g % 2 == 1:
                    STG.flush()
            for g in range(8):
                STG.begin()
                k = g % 2
                bs = 4 + k
                STG.op("pe", lambda e, g=g, bs=bs: e.matmul(c.ps[:, bs, :], lhsT=Btok[:, g, :],
                                                         rhs=xdtd[:, g * 8:(g + 1) * 8, :].rearrange("p h d -> p (h d)"), start=True, stop=True),
                     reads=["Btok", "xdtd"], writes=["ps%d" % bs])
                STG.op("pool", lambda e, g=g, k=k: e.tensor_tensor(out=tmpS[k][:], in0=St[:, g, :].rearrange("p (h d) -> p h d", d=64),
                                                                in1=bcast(EL[:, g * 8:(g + 1) * 8], 2, 64), op=ALU.mult),
                     reads=["S%d" % g, "EL"], writes=["tmpS%d" % k])
                STG.op("dve", lambda e, g=g, k=k, bs=bs: e.tensor_tensor(out=St[:, g, :], in0=tmpS[k][:].rearrange("p h d -> p (h d)"),
                                                                      in1=c.ps[:, bs, :], op=ALU.add),
                     reads=["tmpS%d" % k, "ps%d" % bs], writes=["S%d" % g])
                STG.op("act", lambda e, g=g: e.copy(out=Sb[:, g, :], in_=St[:, g, :]), reads=["S%d" % g], writes=["Sb%d" % g])
                if g % 2 == 1:
                    STG.flush()
            ob = 0
            for G in range(8):
                TG.begin()
                k = G % 2
                bank = 4 + k

                def t8(e, G=G, bank=bank, ytok=ytok):
                    ins = None
                    for q in range(4):
                        ins = e.transpose(out=c.ps[:, bank, q * 128:(q + 1) * 128], in_=ytok[:, G * 4 + q, :], identity=c.identf[:])
                    return ins
                TG.op("pe", t8, reads=["ytok%d_%d" % (yb, G), "identf"], writes=["ps%d" % bank])
                for q in range(4):
                    cc = G * 4 + q
                    TG.op("dve", lambda e, q=q, cc=cc, k=k, bank=bank, b=b: e.scalar_tensor_tensor(
                        out=yg[k][:, q, :], in0=xsT[b][:, cc, :], scalar=vec[:, cc:cc + 1], in1=c.ps[:, bank, q * 128:(q + 1) * 128],
                        op0=ALU.mult, op1=ALU.add), reads=["xs%d" % b, "vec", "ps%d" % bank], writes=["yg%d_%d" % (k, q)])
                ygn = ["yg%d_%d" % (k, q) for q in range(4)]
                TG.op("pool", lambda e, G=G, k=k, b=b: e.tensor_tensor(out=yg[k][:], in0=yg[k][:], in1=zT[b][:, G * 4:(G + 1) * 4, :], op=ALU.mult),
                     reads=ygn + ["z%d" % b], writes=ygn)
                TG.op("act", lambda e, k=k: e.activation(out=sq[k][:], in_=yg[k][:], func=AF.Square), reads=ygn, writes=["sq%d" % k])
                bn = 6 + k

                def t8n(e, k=k, bn=bn):
                    ins = None
                    for q in range(4):
                        ins = e.matmul(c.ps[:, bn, 0:128], lhsT=c.onesb[:], rhs=sq[k][:, q, :], start=(q == 0), stop=(q == 3))
                    return ins
                TG.op("pe", t8n, reads=["sq%d" % k, "onesb"], writes=["ps%d" % bn])
                TG.op("act", lambda e, k=k, bn=bn: e.activation(out=rstd[k][:], in_=c.ps[:, bn, 0:128], func=AF.Ln, scale=1.0 / 512, bias=eps5[:]),
                     reads=["ps%d" % bn, "eps5"], writes=["rstd%d" % k])
                TG.op("act", lambda e, k=k: e.activation(out=rstd[k][:], in_=rstd[k][:], func=AF.Exp, scale=-0.5), reads=["rstd%d" % k], writes=["rstd%d" % k])
                TG.op("dve", lambda e, k=k, G=G, ob=ob: e.tensor_tensor(out=ygT[ob][:, G * 4:(G + 1) * 4, :], in0=yg[k][:], in1=bcast(rstd[k][:], 1, 4), op=ALU.mult),
                     reads=ygn + ["rstd%d" % k], writes=["ygT%d_%d" % (ob, G * 4 + q) for q in range(4)])
                if G % 2 == 1:
                    BG.add(TG)
            BG.add_dma("sp", c.ybuf[:, cols(ch)].rearrange("(cc p) t -> p cc t", p=128), ygT[ob][:],
                       reads=["ygT%d_%d" % (ob, cc) for cc in range(32)], writes=["ybuf%d" % ch])
        BG.drain()
        S.barrier()


def ssm_chunks():
    ch = []
    for i in range(32):
        ch.append((i * 128, 128, "silu", i * 128, 0))
    for i in range(48):
        ch.append((4096 + i * 128, 128, "conv", 4096 + i * 128, i))
    ch.append((10240, 64, "raw", 0, 0))
    return ch


def phase_ssm_layer(c, l):
    j = l // 2
    phase_inproj(c, l, c.ssm_w_in[j], ssm_chunks(), c.ssm_cw[j], 48, c.gain[:, l, 0, :], c.modv[:, l, 0:KC])
    phase_ssd_scan(c, l)
    phase_down(c, c.ssm_w_out[j], 32, c.ybuf, "ybuf", c.modv[:, l, 2 * KC:3 * KC], rowscale_dram=c.ssm_vec[j][:, 32:64])


def phase_gdn_pre(c, j):
    nc, S = c.nc, c.S
    with ExitStack() as es:
        raw = es.enter_context(nc.sbuf_tensor("gp_raw", [64, L], F32))
        t1 = es.enter_context(nc.sbuf_tensor("gp_t1", [64, L], F32))
        gc = es.enter_context(nc.sbuf_tensor("gp_gc", [64, L], F32))
        ones = es.enter_context(nc.sbuf_tensor("gp_one", [64, 128], F32))
        hp = es.enter_context(nc.sbuf_tensor("gp_hp", [64, 2], F32))
        an = es.enter_context(nc.sbuf_tensor("gp_an", [64, 1], F32))
        S.dma("sp", raw[:], c.rawbuf[0:64, :], writes=["raw"])
        S.dma("sp", hp[:], c.gdn_hp[j], writes=["hp"])
        S.op("pool", lambda e: e.memset(ones[:], 1.0), writes=["ones"])
        S.op("act", lambda e: e.activation(out=t1[0:32, :], in_=raw[0:32, :], func=AF.Sigmoid), reads=["raw"], writes=["beta"])
        S.op("act", lambda e: e.activation(out=t1[32:64, :], in_=raw[32:64, :], func=AF.Exp, bias=hp[32:64, 0:1]), reads=["raw", "hp"], writes=["sp"])
        S.op("act", lambda e: e.activation(out=t1[32:64, :], in_=t1[32:64, :], func=AF.Ln, bias=1.0), reads=["sp"], writes=["sp"])
        S.op("act", lambda e: e.activation(out=an[32:64, :], in_=hp[32:64, 1:2], func=AF.Exp), reads=["hp"], writes=["an"])
        S.op("dve", lambda e: e.tensor_scalar(out=an[32:64, :], in0=an[32:64, :], scalar1=-1.0, scalar2=None, op0=ALU.mult), reads=["an"], writes=["an"])
        S.op("dve", lambda e: e.tensor_scalar(out=raw[32:64, :], in0=t1[32:64, :], scalar1=an[32:64, 0:1], scalar2=None, op0=ALU.mult),
             reads=["sp", "an", "raw"], writes=["g"])
        for ch in range(L // 128):
            S.op("dve", lambda e, ch=ch: e.tensor_tensor_scan(out=gc[32:64, ch * 128:(ch + 1) * 128], data0=ones[32:64, :],
                                                            data1=raw[32:64, ch * 128:(ch + 1) * 128], initial=0.0,
                                                            op0=ALU.mult, op1=ALU.add),
                 reads=["g", "ones"], writes=["gc%d" % ch])
        S.dma("sp", c.dtd[0:32, :], t1[0:32, :], reads=["beta"], writes=["dtd"])
        S.dma("sp", c.acd[0:32, :], gc[32:64, :], reads=["gc%d" % ch for ch in range(L // 128)], writes=["acd"])
        S.barrier()


def phase_gdn_scan(c, l, HB=2):
    nc, S = c.nc, c.S
    j = l // 2
    NCH = L // 128
    NH = 32 // HB
    NK = 16 // HB
    NG = NH // 4
    phase_gdn_pre(c, j)
    with ExitStack() as es:
        def sb(name, shape, dt=F32):
            return es.enter_context(nc.sbuf_tensor(name, shape, dt))
        St = sb("gs_S", [128, NH, 128])
        Sb = sb("gs_Sb", [128, NH, 128], BF16)
        maskT = sb("gs_mask", [128, 128])
        lmask = sb("gs_lmask", [128, 7, 128], BF16)
        maskS = sb("gs_maskS", [128, 128])
        identb = sb("gs_idb", [128, 128], BF16)
        nwv = sb("gs_nw", [128, 1])
        qT = [sb("gs_q%d" % i, [128, NK, 128], BF16) for i in range(2)]
        kT = [sb("gs_k%d" % i, [128, NK, 128], BF16) for i in range(2)]
        vT = [sb("gs_v%d" % i, [128, NH, 128], BF16) for i in range(2)]
        zT = [sb("gs_z%d" % i, [128, NH, 128], BF16) for i in range(3)]
        dta = [sb("gs_dta%d" % i, [NH, 2, 128]) for i in range(2)]
        Gbc = sb("gs_gbc", [128, NH, 128])
        gk = sb("gs_gk", [128, 2 * NH])
        eg = sb("gs_eg", [128, NH])
        neg = sb("gs_neg", [128, NH])
        nbt = sb("gs_nbt", [128, NH])
        Wt = [sb("gs_W%d" % i, [128, 4, 128], BF16) for i in range(4)]
        lmT0 = sb("gs_lmT0", [128, 128], BF16)
        dl = sb("gs_dl", [128, NH])
        EGL = sb("gs_EGL", [128, NH])
        sqk = sb("gs_sqk", [128, NK, 128], BF16)
        rq = sb("gs_rq", [128, NK, 128])
        qn = sb("gs_qn", [128, NK, 128], BF16)
        kn = sb("gs_kn", [128, NK, 128], BF16)
        ktok = sb("gs_ktok", [128, NK, 128], BF16)
        vtok = sb("gs_vtok", [128, NH, 128], BF16)
        seg = [sb("gs_seg%d" % i, [128, 4, 128]) for i in range(2)]
        tm4 = [sb("gs_tm4%d" % i, [128, 4, 128]) for i in range(2)]
        dm4 = [sb("gs_dm4%d" % i, [128, 4, 128]) for i in range(2)]
        eg4 = [sb("gs_eg4%d" % i, [128, 4, 128]) for i in range(2)]
        At = sb("gs_At", [128, NH, 128], BF16)
        attnT = sb("gs_attn", [128, NH, 128], BF16)
        qg = sb("gs_qg", [128, NH, 128], BF16)
        Xn = sb("gs_Xn", [128, NH, 128], BF16)
        XT = sb("gs_XT", [128, NH, 128], BF16)
        tmpk = [sb("gs_tmpk%d" % i, [128, 4, 128]) for i in range(2)]
        vdn = sb("gs_vdn", [128, NH, 128], BF16)
        vnew = sb("gs_vnew", [128, NH, 128], BF16)
        o4 = [sb("gs_o4%d" % i, [128, 4, 128]) for i in range(4)]
        og = [sb("gs_og%d" % i, [128, 4, 128]) for i in range(2)]
        sq2 = [sb("gs_sq2%d" % i, [128, 4, 128], BF16) for i in range(2)]
        rstd = [sb("gs_rstd%d" % i, [128, 4, 128]) for i in range(2)]
        tmpS = [sb("gs_tmpS%d" % i, [128, 4, 128]) for i in range(2)]
        ygT = [sb("gs_ygT%d" % i, [128, NH, 128], BF16) for i in range(2)]

        S.dma("sp", maskT[:], c.maskT_d, writes=["mask"])
        S.dma("pool", lmask[:], c.lmask_d.rearrange("p (l i) -> p l i", i=128), writes=["lmask"])
        S.dma("sp", maskS[:], c.maskS_d, writes=["maskS"])
        S.dma("sp", lmT0[:], c.lmT0_d, writes=["lmT0"])
        STG = Stager(S)
        TG = Stager(S)
        BG = Background(S)
        S.dma("sp", identb[:], c.identb_d, writes=["idb"])
        S.dma("sp", nwv[:], c.gdn_nw[j], writes=["nw"])

        def cols(ch):
            return slice(ch * 128, (ch + 1) * 128)

        def g4(t, grp):
            return t[:, grp * 4:(grp + 1) * 4, :]

        for hb in range(HB):
            h0 = hb * NH
            k0 = hb * NK
            Sn = ["S%d" % g for g in range(NG)]
            Sbn = ["Sb%d" % g for g in range(NG)]
            S.op("pool", lambda e: e.memset(St[:], 0.0), writes=Sn)
            S.op("pool", lambda e: e.memset(Sb[:], 0.0), writes=Sbn)

            def load(ch):
                b = ch % 2
                rr = "(cc p) t -> p cc t"
                S.dma("sp", qT[b][:], c.pbuf[k0 * 128:(k0 + NK) * 128, cols(ch)].rearrange(rr, p=128), writes=["q%d" % b])
                S.dma("sp", kT[b][:], c.pbuf[2048 + k0 * 128:2048 + (k0 + NK) * 128, cols(ch)].rearrange(rr, p=128), writes=["k%d" % b])
                S.dma("act", vT[b][:], c.pbuf[4096 + h0 * 128:4096 + (h0 + NH) * 128, cols(ch)].rearrange(rr, p=128), writes=["v%d" % b])
                S.dma("act", zT[ch % 3][:], c.pbuf[8192 + h0 * 128:8192 + (h0 + NH) * 128, cols(ch)].rearrange(rr, p=128), writes=["z%d" % (ch % 3)])
                S.dma("sp", dta[b][:, 0, :], c.dtd[h0:h0 + NH, cols(ch)], writes=["dta%d" % b])
                S.dma("sp", dta[b][:, 1, :], c.acd[h0:h0 + NH, cols(ch)], writes=["dta%d" % b])

            load(0)
            for ch in range(NCH):
                b = ch % 2
                def gbc_load(ch_):
                    src = bass.AP(tensor=c.acd.tensor, offset=h0 * L + ch_ * 128, ap=[[0, 128], [L, NH], [1, 128]])
                    S.dma("sp", Gbc[:], src, writes=["gbc"])
                if ch == 0:
                    gbc_load(0)
                if ch + 1 < NCH:
                    load(ch + 1)
                def t1(e, b=b):
                    e.transpose(out=c.ps[:, 0, 0:NH], in_=dta[b][0:NH, 0, :], identity=c.identf[0:NH, 0:NH])
                    return e.transpose(out=c.ps[:, 0, NH:2 * NH], in_=dta[b][0:NH, 1, :], identity=c.identf[0:NH, 0:NH])
                S.op("pe", t1, reads=["dta%d" % b, "identf"], writes=["ps0"])
                S.op("dve", lambda e: e.tensor_copy(out=gk[:], in_=c.ps[:, 0, 0:2 * NH]), reads=["ps0"], writes=["gk"])
                S.op("act", lambda e: e.activation(out=eg[:], in_=gk[:, NH:2 * NH], func=AF.Exp), reads=["gk"], writes=["eg"])
                S.op("dve", lambda e: e.tensor_scalar(out=neg[:], in0=eg[:], scalar1=-1.0, scalar2=None, op0=ALU.mult), reads=["eg"], writes=["neg"])
                S.op("dve", lambda e: e.tensor_scalar(out=nbt[:], in0=gk[:, 0:NH], scalar1=-1.0, scalar2=None, op0=ALU.mult), reads=["gk"], writes=["nbt"])
                S.op("dve", lambda e: e.tensor_tensor(out=dl[:], in0=Gbc[:, :, 127], in1=gk[:, NH:2 * NH], op=ALU.subtract),
                     reads=["gbc", "gk"], writes=["dl"])
                S.op("act", lambda e: e.activation(out=dl[:], in_=dl[:], func=AF.Exp), reads=["dl"], writes=["dl"])
                S.op("act", lambda e: e.activation(out=EGL[:], in_=Gbc[:, :, 127], func=AF.Exp), reads=["gbc"], writes=["EGL"])
                BG.pump(3)
                for which, srcT, dst, scale, bias in ((0, qT[b], qn, 128.0, 128.0e-6), (1, kT[b], kn, 1.0, 1.0e-6)):
                    sname = ("q%d" if which == 0 else "k%d") % b
                    dname = "qn" if which == 0 else "kn"
                    S.op("act", lambda e, srcT=srcT: e.activation(out=sqk[:], in_=srcT[:], func=AF.Square), reads=[sname], writes=["sqk"])
                    nb = (NK * 128) // 512
                    for bb in range(nb):
                        bank = 1 + bb

                        def mmn(e, bb=bb, bank=bank):
                            ins = None
                            for q in range(4):
                                ins = e.matmul(c.ps[:, bank, q * 128:(q + 1) * 128], lhsT=c.onesb[:], rhs=sqk[:, bb * 4 + q, :], start=True, stop=True)
                            return ins
                        S.op("pe", mmn, reads=["sqk", "onesb"], writes=["ps%d" % bank])
                        S.op("act", lambda e, bb=bb, bank=bank, scale=scale, bias=bias: e.activation(
                            out=rq[:, bb * 4:(bb + 1) * 4, :], in_=c.ps[:, bank, :].rearrange("p (k i) -> p k i", i=128),
                            func=AF.Ln, scale=scale, bias=c.cb[:, which:which + 1]), reads=["ps%d" % bank, "cb"], writes=["rq%d" % bb])
                    rqn = ["rq%d" % bb for bb in range(nb)]
                    S.op("act", lambda e: e.activation(out=rq[:], in_=rq[:], func=AF.Exp, scale=-0.5), reads=rqn, writes=rqn)
                    BG.pump(3)
                    S.op("dve", lambda e, srcT=srcT, dst=dst: e.tensor_tensor(out=dst[:], in0=srcT[:], in1=rq[:], op=ALU.mult),
                         reads=rqn + [sname], writes=[dname])
                for bb in range((NK + 7) // 8):
                    bank = 3 + bb
                    nq = min(8, NK - bb * 8)

                    def t3(e, bb=bb, bank=bank, nq=nq):
                        ins = None
                        for q in range(nq):
                            ins = e.transpose(out=psbf(c, bank)[:, q * 128:(q + 1) * 128], in_=kn[:, bb * 8 + q, :], identity=identb[:])
                        return ins
                    S.op("pe", t3, reads=["kn", "idb"], writes=["ps%d" % bank])
                    S.op("act", lambda e, bb=bb, bank=bank, nq=nq: e.copy(
                        out=ktok[:, bb * 8:bb * 8 + nq, :], in_=psbf(c, bank)[:, 0:nq * 128].rearrange("p (g n) -> p g n", n=128)),
                        reads=["ps%d" % bank], writes=["ktok"])
                for bb in range(NH // 8):
                    bank = 4 + (bb % 2)

                    def t3v(e, bb=bb, bank=bank, b=b):
                        ins = None
                        for q in range(8):
                            ins = e.transpose(out=psbf(c, bank)[:, q * 128:(q + 1) * 128], in_=vT[b][:, bb * 8 + q, :], identity=identb[:])
                        return ins
                    S.op("pe", t3v, reads=["v%d" % b, "idb"], writes=["ps%d" % bank])
                    S.op("act", lambda e, bb=bb, bank=bank: e.copy(
                        out=vtok[:, bb * 8:(bb + 1) * 8, :], in_=psbf(c, bank).rearrange("p (g n) -> p g n", n=128)),
                        reads=["ps%d" % bank], writes=["vtok%d" % bb])
                BG.pump(5)
                for grp in range(NG):
                    STG.begin()
                    k2 = grp % 2
                    bkk, bqk = 1 + k2 * 2, 2 + k2 * 2

                    def t4(e, grp=grp, bkk=bkk, bqk=bqk):
                        ins = None
                        for r in range(2):
                            kh = grp * 2 + r
                            e.matmul(c.ps[:, bkk, r * 128:(r + 1) * 128], lhsT=kn[:, kh, :], rhs=kn[:, kh, :], start=True, stop=True)
                            ins = e.matmul(c.ps[:, bqk, r * 128:(r + 1) * 128], lhsT=kn[:, kh, :], rhs=qn[:, kh, :], start=True, stop=True)
                        return ins
                    STG.op("pe", t4, reads=["kn", "qn"], writes=["ps%d" % bkk, "ps%d" % bqk])
                    STG.op("dve", lambda e, grp=grp, k2=k2: e.tensor_tensor(out=seg[k2][:], in0=g4(Gbc, grp),
                                                                         in1=bcast(gk[:, NH + grp * 4:NH + grp * 4 + 4], 2, 128), op=ALU.subtract),
                         reads=["gbc", "gk"], writes=["seg%d" % k2])
                    STG.op("act", lambda e, k2=k2: e.activation(out=seg[k2][:], in_=seg[k2][:], func=AF.Relu, scale=-1.0), reads=["seg%d" % k2], writes=["seg%d" % k2])
                    STG.op("act", lambda e, k2=k2: e.activation(out=seg[k2][:], in_=seg[k2][:], func=AF.Exp, scale=-1.0), reads=["seg%d" % k2], writes=["seg%d" % k2])
                    STG.op("pool", lambda e, k2=k2: e.tensor_tensor(out=tm4[k2][:], in0=seg[k2][:], in1=bcast(maskS[:], 1, 4), op=ALU.mult),
                         reads=["seg%d" % k2, "maskS"], writes=["tm4%d" % k2])
                    STG.op("pool", lambda e, k2=k2: e.tensor_tensor(out=dm4[k2][:], in0=seg[k2][:], in1=bcast(maskT[:], 1, 4), op=ALU.mult),
                         reads=["seg%d" % k2, "mask"], writes=["dm4%d" % k2])

                    def v22(ap):
                        return ap.rearrange("p (k r) i -> p k r i", r=2)
                    kkv = bcast(c.ps[:, bkk, 0:256].rearrange("p (k i) -> p k i", i=128), 2, 2)
                    qkv = bcast(c.ps[:, bqk, 0:256].rearrange("p (k i) -> p k i", i=128), 2, 2)
                    def mkAt(e, grp=grp, k2=k2, bkk=bkk):
                        ins = None
                        for q in range(4):
                            h = grp * 4 + q
                            r = q // 2
                            ins = e.scalar_tensor_tensor(out=At[:, h, :], in0=c.ps[:, bkk, r * 128:(r + 1) * 128], scalar=nbt[:, h:h + 1],
                                                         in1=tm4[k2][:, q, :], op0=ALU.mult, op1=ALU.mult)
                        return ins
                    STG.op("dve", mkAt, reads=["ps%d" % bkk, "tm4%d" % k2, "nbt"], writes=["At%d" % grp])
                    STG.op("dve", lambda e, grp=grp: e.tensor_tensor(out=g4(At, grp), in0=g4(At, grp), in1=bcast(identb[:], 1, 4), op=ALU.add),
                         reads=["At%d" % grp, "idb"], writes=["At%d" % grp])
                    STG.op("dve", lambda e, grp=grp, k2=k2, qkv=qkv: e.tensor_tensor(out=v22(g4(attnT, grp)), in0=qkv, in1=v22(dm4[k2][:]), op=ALU.mult),
                         reads=["ps%d" % bqk, "dm4%d" % k2], writes=["attn%d" % grp])
                    STG.op("act", lambda e, grp=grp, k2=k2: e.activation(out=eg4[k2][:], in_=g4(Gbc, grp), func=AF.Exp), reads=["gbc"], writes=["eg4%d" % k2])
                    STG.op("pool", lambda e, grp=grp, k2=k2: e.tensor_tensor(out=v22(g4(qg, grp)), in0=bcast(qn[:, grp * 2:grp * 2 + 2, :], 2, 2),
                                                                          in1=v22(eg4[k2][:]), op=ALU.mult),
                         reads=["qn", "eg4%d" % k2], writes=["qg%d" % grp])
                    if grp % 2 == 1:
                        STG.flush()
                if ch + 1 < NCH:
                    gbc_load(ch + 1)
                BG.drain()
                vr = lambda ap: ap.rearrange("p (q i) -> p q i", i=128)
                for grp in range(NG):
                    S.op("pool", lambda e, grp=grp: e.tensor_tensor(out=g4(XT, grp), in0=g4(At, grp), in1=bcast(lmT0[:], 1, 4), op=ALU.mult),
                         reads=["At%d" % grp, "lmT0"], writes=["XT%d" % grp])
                for grp in range(NG):
                    def mt0(e, grp=grp):
                        ins = None
                        for q in range(4):
                            h = grp * 4 + q
                            ins = e.matmul(c.ps[:, grp % 4, q * 128:(q + 1) * 128], lhsT=XT[:, h, :], rhs=identb[:], start=True, stop=True)
                        return ins
                    S.op("pe", mt0, reads=["XT%d" % grp, "idb"], writes=["ps%d" % (grp % 4)])
                for grp in range(NG):
                    S.op("act", lambda e, grp=grp: e.copy(out=g4(Xn, grp), in_=vr(c.ps[:, grp % 4, :])),
                         reads=["ps%d" % (grp % 4)], writes=["Xn%d" % grp])
                for lv in range(1, 7):
                    for grp in range(NG):
                        def mp(e, grp=grp):
                            ins = None
                            for q in range(4):
                                h = grp * 4 + q
                                ins = e.matmul(c.ps[:, grp % 4, q * 128:(q + 1) * 128], lhsT=At[:, h, :], rhs=Xn[:, h, :], start=True, stop=True)
                            return ins
                        S.op("pe", mp, reads=["At%d" % grp, "Xn%d" % grp], writes=["ps%d" % (grp % 4)])
                    for grp in range(NG):
                        S.op("dve", lambda e, grp=grp, lv=lv: e.tensor_tensor(out=Wt[grp % 4][:], in0=vr(c.ps[:, grp % 4, :]),
                                                                             in1=bcast(lmask[:, lv, :], 1, 4), op=ALU.mult),
                             reads=["ps%d" % (grp % 4), "lmask"], writes=["W%d" % (grp % 4)])
                    for half in range(NG // 2):
                        for grp in (2 * half, 2 * half + 1):
                            k2 = grp % 2
                            px, pt = 4 + k2 * 2, 5 + k2 * 2
                            if lv < 6:
                                def mx(e, grp=grp, px=px):
                                    ins = None
                                    for q in range(4):
                                        h = grp * 4 + q
                                        ins = e.matmul(c.ps[:, px, q * 128:(q + 1) * 128], lhsT=XT[:, h, :], rhs=Wt[grp % 4][:, q, :], start=True, stop=True)
                                    return ins
                                S.op("pe", mx, reads=["XT%d" % grp, "W%d" % (grp % 4)], writes=["ps%d" % px])

                            def mt(e, grp=grp, pt=pt):
                                ins = None
                                for q in range(4):
                                    h = grp * 4 + q
                                    ins = e.matmul(c.ps[:, pt, q * 128:(q + 1) * 128], lhsT=Wt[grp % 4][:, q, :], rhs=XT[:, h, :], start=True, stop=True)
                                return ins
                            S.op("pe", mt, reads=["XT%d" % grp, "W%d" % (grp % 4)], writes=["ps%d" % pt])
                        for grp in (2 * half, 2 * half + 1):
                            k2 = grp % 2
                            px, pt = 4 + k2 * 2, 5 + k2 * 2
                            if lv < 6:
                                S.op("act", lambda e, grp=grp, px=px: e.copy(out=g4(Xn, grp), in_=vr(c.ps[:, px, :])),
                                     reads=["ps%d" % px], writes=["Xn%d" % grp])
                            S.op("act", lambda e, grp=grp, pt=pt: e.copy(out=g4(XT, grp), in_=vr(c.ps[:, pt, :])),
                                 reads=["ps%d" % pt], writes=["XT%d" % grp])
                BG.drain()
                ob = ch % 2
                for grp in range(NG):
                    STG.begin()
                    k2 = grp % 2
                    b1, b2, b3, b4 = k2 * 3, k2 * 3, k2 * 3 + 1, k2 * 3 + 2
                    bt = 6 + k2
                    o4i = grp % 4
                    zb = ch % 3
                    vr4 = lambda ap: ap.rearrange("p (q i) -> p q i", i=128)

                    def mks(e, grp=grp, b1=b1):
                        ins = None
                        for q in range(4):
                            h = grp * 4 + q
                            ins = e.matmul(c.ps[:, b1, q * 128:(q + 1) * 128], lhsT=kn[:, h // 2, :], rhs=Sb[:, h, :], start=True, stop=True)
                        return ins
                    STG.op("pe", mks, reads=["kn", "Sb%d" % grp], writes=["ps%d" % b1])
                    STG.op("dve", lambda e, grp=grp, k2=k2, b1=b1: e.tensor_tensor(out=tmpk[k2][:], in0=vr4(c.ps[:, b1, :]),
                                                                                in1=bcast(neg[:, grp * 4:grp * 4 + 4], 2, 128), op=ALU.mult),
                         reads=["ps%d" % b1, "neg"], writes=["tmpk%d" % k2])
                    STG.op("pool", lambda e, grp=grp, k2=k2: e.tensor_tensor(out=g4(vdn, grp), in0=tmpk[k2][:], in1=g4(vtok, grp), op=ALU.add),
                         reads=["tmpk%d" % k2, "vtok%d" % (grp // 2)], writes=["vdn%d" % grp])

                    def mtv(e, grp=grp, b2=b2):
                        ins = None
                        for q in range(4):
                            h = grp * 4 + q
                            ins = e.matmul(c.ps[:, b2, q * 128:(q + 1) * 128], lhsT=XT[:, h, :], rhs=vdn[:, h, :], start=True, stop=True)
                        return ins
                    STG.op("pe", mtv, reads=["XT%d" % grp, "vdn%d" % grp], writes=["ps%d" % b2])
                    STG.op("dve", lambda e, grp=grp, b2=b2: e.tensor_tensor(out=g4(vnew, grp), in0=vr4(c.ps[:, b2, :]),
                                                                         in1=bcast(gk[:, grp * 4:grp * 4 + 4], 2, 128), op=ALU.mult),
                         reads=["ps%d" % b2, "gk"], writes=["vnew%d" % grp])
                    STG.op("pool", lambda e, grp=grp: e.tensor_tensor(out=g4(vdn, grp), in0=g4(vnew, grp),
                                                                   in1=bcast(dl[:, grp * 4:grp * 4 + 4], 2, 128), op=ALU.mult),
                         reads=["vnew%d" % grp, "dl"], writes=["vdn%d" % grp])

                    def mo(e, grp=grp, b3=b3):
                        ins = None
                        for q in range(4):
                            h = grp * 4 + q
                            e.matmul(c.ps[:, b3, q * 128:(q + 1) * 128], lhsT=qg[:, h, :], rhs=Sb[:, h, :], start=True, stop=False)
                            ins = e.matmul(c.ps[:, b3, q * 128:(q + 1) * 128], lhsT=attnT[:, h, :], rhs=vnew[:, h, :], start=False, stop=True)
                        return ins
                    STG.op("pe", mo, reads=["qg%d" % grp, "Sb%d" % grp, "attn%d" % grp, "vnew%d" % grp], writes=["ps%d" % b3])
                    def mst(e, grp=grp, b4=b4):
                        ins = None
                        for q in range(4):
                            h = grp * 4 + q
                            ins = e.matmul(c.ps[:, b4, q * 128:(q + 1) * 128], lhsT=ktok[:, h // 2, :], rhs=vdn[:, h, :], start=True, stop=True)
                        return ins
                    STG.op("pe", mst, reads=["ktok", "vdn%d" % grp], writes=["ps%d" % b4])
                    STG.op("pool", lambda e, grp=grp, k2=k2: e.tensor_tensor(out=tmpS[k2][:], in0=g4(St, grp),
                                                                          in1=bcast(EGL[:, grp * 4:grp * 4 + 4], 2, 128), op=ALU.mult),
                         reads=["S%d" % grp, "EGL"], writes=["tmpS%d" % k2])
                    STG.op("dve", lambda e, grp=grp, k2=k2, b4=b4: e.tensor_tensor(out=g4(St, grp), in0=tmpS[k2][:], in1=vr4(c.ps[:, b4, :]), op=ALU.add),
                         reads=["tmpS%d" % k2, "ps%d" % b4], writes=["S%d" % grp])
                    STG.op("act", lambda e, grp=grp: e.copy(out=g4(Sb, grp), in_=g4(St, grp)), reads=["S%d" % grp], writes=["Sb%d" % grp])
                    STG.op("act", lambda e, o4i=o4i, b3=b3: e.copy(out=o4[o4i][:], in_=vr4(c.ps[:, b3, :])), reads=["ps%d" % b3], writes=["o4%d" % o4i])
                    TG.begin()

                    def mtr(e, o4i=o4i, bt=bt):
                        ins = None
                        for q in range(4):
                            ins = e.transpose(out=c.ps[:, bt, q * 128:(q + 1) * 128], in_=o4[o4i][:, q, :], identity=c.identf[:])
                        return ins
                    TG.op("pe", mtr, reads=["o4%d" % o4i, "identf"], writes=["ps%d" % bt])
                    TG.op("act", lambda e, k2=k2, bt=bt: e.copy(out=og[k2][:], in_=vr4(c.ps[:, bt, :])), reads=["ps%d" % bt], writes=["og%d" % k2])
                    TG.op("act", lambda e, k2=k2, bt=bt: e.activation(out=sq2[k2][:], in_=vr4(c.ps[:, bt, :]), func=AF.Square), reads=["ps%d" % bt], writes=["sq2%d" % k2])

                    def mn2(e, k2=k2, bt=bt):
                        ins = None
                        for q in range(4):
                            ins = e.matmul(c.ps[:, bt, q * 128:(q + 1) * 128], lhsT=c.onesb[:], rhs=sq2[k2][:, q, :], start=True, stop=True)
                        return ins
                    TG.op("pe", mn2, reads=["sq2%d" % k2, "onesb"], writes=["ps%d" % bt])
                    TG.op("act", lambda e, k2=k2, bt=bt: e.activation(out=rstd[k2][:], in_=vr4(c.ps[:, bt, :]), func=AF.Ln, scale=1.0 / 128, bias=c.cb[:, 1:2]),
                          reads=["ps%d" % bt, "cb"], writes=["rstd%d" % k2])
                    TG.op("act", lambda e, k2=k2: e.activation(out=rstd[k2][:], in_=rstd[k2][:], func=AF.Exp, scale=-0.5), reads=["rstd%d" % k2], writes=["rstd%d" % k2])
                    TG.op("dve", lambda e, k2=k2: e.tensor_tensor(out=og[k2][:], in0=og[k2][:], in1=rstd[k2][:], op=ALU.mult),
                          reads=["og%d" % k2, "rstd%d" % k2], writes=["og%d" % k2])
                    TG.op("pool", lambda e, grp=grp, k2=k2, zb=zb, ob=ob: e.tensor_tensor(out=g4(ygT[ob], grp), in0=og[k2][:], in1=g4(zT[zb], grp), op=ALU.mult),
                          reads=["og%d" % k2, "z%d" % zb], writes=["ygT%d_%d" % (ob, grp)])
                    if grp % 2 == 1:
                        STG.flush()
                        BG.add(TG)
                BG.add_dma("sp", c.ybuf[h0 * 128:(h0 + NH) * 128, cols(ch)].rearrange("(cc p) t -> p cc t", p=128), ygT[ob][:],
                           reads=["ygT%d_%d" % (ob, grp) for grp in range(NG)], writes=["ybuf%d_%d" % (hb, ch)])
            BG.drain()
        S.barrier()


def gdn_chunks():
    ch = []
    for i in range(64):
        ch.append((i * 128, 128, "conv", i * 128, i))
    for i in range(32):
        ch.append((8192 + i * 128, 128, "silu", 8192 + i * 128, 0))
    ch.append((12288, 64, "raw", 0, 0))
    return ch


def phase_gdn_layer(c, l):
    j = l // 2
    phase_inproj(c, l, c.gdn_w_in[j], gdn_chunks(), c.gdn_cw[j], 64, c.gain[:, l, 0, :], c.modv[:, l, 0:KC])
    phase_gdn_scan(c, l)
    phase_down(c, c.gdn_w_out[j], 32, c.ybuf, "ybuf", c.modv[:, l, 2 * KC:3 * KC], rowscale_dram=c.gdn_nwx[j])
```

```python
import numpy as np
from contextlib import ExitStack
import ml_dtypes
import concourse.bass as bass
import concourse.mybir as mybir
from concourse.bass_utils import run_bass_kernel_spmd

F32 = mybir.dt.float32
BF16 = mybir.dt.bfloat16
AF = mybir.ActivationFunctionType
ALU = mybir.AluOpType

D = 2048
L = 4096
DEPTH = 4
KC = D // 128
SSM_DI = 4096
SSM_CONV_DIM = 6144
SSM_IN = 10304
GDN_QKV = 8192
GDN_IN = 12352
FFN = 5632
FC = FFN // 128
EPS = 1e-6
HALF = 2048


class Sched:
    def __init__(self, nc, es, n_dma=28):
        self.nc = nc
        self.eng = {"pe": nc.tensor, "act": nc.scalar, "dve": nc.vector,
                    "pool": nc.gpsimd, "sp": nc.sync}
        self.sems = {k: es.enter_context(nc.semaphore("sem_" + k))
                     for k in ("pe", "act", "dve", "pool")}
        self.cnt = {k: 0 for k in self.sems}
        self.dsems = [es.enter_context(nc.semaphore("dsem%d" % i)) for i in range(n_dma)]
        self.dcnt = [0] * n_dma
        self.dnext = 0
        self.waited = {k: {} for k in self.eng}
        self.lastw = {}
        self.readers = {}

    def _sem(self, k):
        return self.sems[k] if isinstance(k, str) else self.dsems[k]

    def _deps(self, reads, writes):
        deps = {}

        def add(tok):
            if tok is None:
                return
            k, v = tok
            if deps.get(k, 0) < v:
                deps[k] = v
        for r in reads:
            add(self.lastw.get(r))
        for w in writes:
            add(self.lastw.get(w))
            for k, v in self.readers.get(w, {}).items():
                add((k, v))
        return deps

    def _wait(self, e, deps):
        for k, v in deps.items():
            if e == "pe" and k == "pe":
                continue
            if self.waited[e].get(k, 0) >= v:
                continue
            self.eng[e].wait_ge(self._sem(k), v)
            self.waited[e][k] = v

    def _commit(self, tok, reads, writes):
        k, v = tok
        for r in reads:
            d = self.readers.setdefault(r, {})
            if d.get(k, 0) < v:
                d[k] = v
        for w in writes:
            self.lastw[w] = tok
            self.readers[w] = {}

    def op(self, e, fn, reads=(), writes=()):
        self._wait(e, self._deps(reads, writes))
        ins = fn(self.eng[e])
        self.cnt[e] += 1
        ins.then_inc(self.sems[e], 1)
        self._commit((e, self.cnt[e]), reads, writes)

    def dma(self, q, out, in_, reads=(), writes=(), **kw):
        self._wait(q, self._deps(reads, writes))
        i = self.dnext
        self.dnext = (i + 1) % len(self.dsems)
        if self.dcnt[i] > 0 and self.waited[q].get(i, 0) < self.dcnt[i] * 16:
            self.eng[q].wait_ge(self.dsems[i], self.dcnt[i] * 16)
            self.waited[q][i] = self.dcnt[i] * 16
        self.dcnt[i] += 1
        self.eng[q].dma_start(out=out, in_=in_, **kw).then_inc(self.dsems[i], 16)
        self._commit((i, self.dcnt[i] * 16), reads, writes)

    def barrier(self, engines=("pe", "act", "dve", "pool", "sp")):
        for e in engines:
            for k in self.sems:
                if k != e and self.cnt[k] > self.waited[e].get(k, 0):
                    self.eng[e].wait_ge(self.sems[k], self.cnt[k])
                    self.waited[e][k] = self.cnt[k]
            if e in self.sems and self.cnt[e] > self.waited[e].get(e, 0) and e != "pe":
                self.eng[e].wait_ge(self.sems[e], self.cnt[e])
                self.waited[e][e] = self.cnt[e]
            for i in range(len(self.dsems)):
                v = self.dcnt[i] * 16
                if v > self.waited[e].get(i, 0):
                    self.eng[e].wait_ge(self.dsems[i], v)
                    self.waited[e][i] = v
        self.lastw = {}
        self.readers = {}


class Stager:
    def __init__(self, S):
        self.S = S
        self.groups = []

    def begin(self):
        self.groups.append([])

    def op(self, *a, **k):
        self.groups[-1].append((a, k))

    def flush(self):
        n = max(len(g) for g in self.groups) if self.groups else 0
        for i in range(n):
            for g in self.groups:
                if i < len(g):
                    a, k = g[i]
                    self.S.op(*a, **k)
        self.groups = []


class Background:
    def __init__(self, S):
        self.S = S
        self.stages = []

    def add(self, stager):
        groups = stager.groups
        stager.groups = []
        n = max(len(g) for g in groups) if groups else 0
        for i in range(n):
            self.stages.append([g[i] for g in groups if i < len(g)])

    def add_dma(self, *a, **k):
        self.stages.append([("dma", a, k)])

    def pump(self, n=1):
        for _ in range(n):
            if not self.stages:
                return
            for item in self.stages.pop(0):
                if len(item) == 3:
                    self.S.dma(*item[1], **item[2])
                else:
                    self.S.op(*item[0], **item[1])

    def drain(self):
        while self.stages:
            self.pump(1)


def bcast(ap, axis, n):
    v = ap.unsqueeze(axis)
    shp = list(v.shape)
    shp[axis] = n
    return v.broadcast_to(shp)


class Ctx:
    pass


class NCProxy:
    def __init__(self, nc):
        object.__setattr__(self, "_nc", nc)
        object.__setattr__(self, "_n", 0)

    def __getattr__(self, k):
        return getattr(self._nc, k)

    def sbuf_tensor(self, name, shape, dt):
        object.__setattr__(self, "_n", self._n + 1)
        return self._nc.sbuf_tensor("%s_u%d" % (name, self._n), shape, dt)


def build_program(n_layers=DEPTH, do_mixer=True, do_ffn=True):
    nc = bass.Bass("TRN2", target_bir_lowering=False)
    c = Ctx()
    c.nc = NCProxy(nc)

    def din(name, shape, dt=F32):
        return nc.dram_tensor(name, list(shape), dt, kind="ExternalInput").ap()

    c.x = din("x", [L, D])
    c.cvec = din("cvec", [128, KC])
    c.ada_w = din("ada_w", [DEPTH, D, 6 * D])
    c.ada_b = din("ada_b", [DEPTH, 1, 6 * D])
    c.nw = din("nw", [128, DEPTH * 2 * KC + KC])
    c.ffn_w_up = din("ffn_w_up", [DEPTH, D, 2 * FFN])
    c.ffn_cw = din("ffn_cw", [DEPTH, 128, 2 * FC * 4])
    c.ffn_w_down = din("ffn_w_down", [DEPTH, FFN, D])
    c.ssm_w_in = din("ssm_w_in", [2, D, SSM_IN])
    c.ssm_cw = din("ssm_cw", [2, 128, 48 * 5])
    c.ssm_hp = din("ssm_hp", [2, 64, 2])
    c.ssm_vec = din("ssm_vec", [2, 128, 64])
    c.ssm_w_out = din("ssm_w_out", [2, SSM_DI, D])
    c.gdn_w_in = din("gdn_w_in", [2, D, GDN_IN])
    c.gdn_cw = din("gdn_cw", [2, 128, 64 * 5])
    c.gdn_hp = din("gdn_hp", [2, 64, 2])
    c.gdn_nw = din("gdn_nw", [2, 128, 1])
    c.gdn_nwx = din("gdn_nwx", [2, 128, 32])
    c.gdn_w_out = din("gdn_w_out", [2, 4096, D])
    c.lmask_d = din("lmask", [128, 7 * 128], BF16)
    c.cb_d = din("cbias", [128, 2])
    c.maskT_d = din("maskT", [128, 128])
    c.maskS_d = din("maskS", [128, 128])
    c.lmT0_d = din("lmT0", [128, 128], BF16)
    c.identb_d = din("identb", [128, 128], BF16)
    c.ident_f = din("ident_f", [128, 128])
    c.ones_b = din("ones_b", [128, 128], BF16)
    c.out = nc.dram_tensor("out", [L, D], F32, kind="ExternalOutput").ap()
    c.xres = nc.dram_tensor("xres", [D, L], F32, kind="Internal").ap()
    c.gbuf = nc.dram_tensor("gbuf", [FFN, L], BF16, kind="Internal").ap()
    c.pbuf = nc.dram_tensor("pbuf", [GDN_IN, L], BF16, kind="Internal").ap()
    c.rawbuf = nc.dram_tensor("rawbuf", [64, L], F32, kind="Internal").ap()
    c.dtd = nc.dram_tensor("dtd", [64, L], F32, kind="Internal").ap()
    c.acd = nc.dram_tensor("acd", [64, L], F32, kind="Internal").ap()
    c.ybuf = nc.dram_tensor("ybuf", [4096, L], BF16, kind="Internal").ap()

    with ExitStack() as es:
        S = Sched(nc, es)
        c.S = S
        c.ps = es.enter_context(nc.psum_tensor("ps", [128, 8, 512], F32))
        c.identf = es.enter_context(nc.sbuf_tensor("identf", [128, 128], F32))
        c.onesb = es.enter_context(nc.sbuf_tensor("onesb", [128, 128], BF16))
        c.nwt = es.enter_context(nc.sbuf_tensor("nwt", [128, DEPTH * 2 * KC + KC], F32))
        c.modv = es.enter_context(nc.sbuf_tensor("modv", [128, DEPTH, 6 * KC], F32))
        c.gain = es.enter_context(nc.sbuf_tensor("gain", [128, DEPTH, 2, KC], F32))
        c.epsc = es.enter_context(nc.sbuf_tensor("epsc", [128, 1], F32))
        S.dma("sp", c.identf[:], c.ident_f, writes=["identf"])
        S.dma("sp", c.onesb[:], c.ones_b, writes=["onesb"])
        S.dma("sp", c.nwt[:], c.nw, writes=["nwt"])
        S.op("dve", lambda e: e.memset(c.epsc[:], EPS), writes=["epsc"])
        c.cb = es.enter_context(nc.sbuf_tensor("cb", [128, 2], F32))
        S.dma("sp", c.cb[:], c.cb_d, writes=["cb"])

        phase_transpose_in(c)
        phase_mods(c, n_layers)
        for l in range(n_layers):
            if do_mixer:
                if l % 2 == 0:
                    phase_ssm_layer(c, l)
                else:
                    phase_gdn_layer(c, l)
            if do_ffn:
                phase_ffn_up(c, l)
                phase_ffn_down(c, l)
        phase_final(c)
        S.barrier()
    return nc


def psb(c, b):
    return c.ps[:, b, :]


def phase_transpose_in(c):
    nc, S = c.nc, c.S
    with ExitStack() as es:
        xin = [es.enter_context(nc.sbuf_tensor("ti_x%d" % i, [128, 4, D], F32)) for i in range(2)]
        xT = [es.enter_context(nc.sbuf_tensor("ti_t%d" % i, [128, KC, 512], F32)) for i in range(2)]
        xv = c.x.rearrange("(n j p) d -> n p j d", j=4, p=128)
        xr = c.xres.rearrange("(kc p) t -> p kc t", p=128)
        NT = L // 512
        S.dma("sp", xin[0][:], xv[0], writes=["ti_x0"])
        for n in range(NT):
            b = n % 2
            if n + 1 < NT:
                S.dma("sp", xin[1 - b][:], xv[n + 1], writes=["ti_x%d" % (1 - b)])
            for kc in range(KC):
                bank = kc % 8

                def mm(e, kc=kc, bank=bank, b=b):
                    ins = None
                    for j in range(4):
                        ins = e.transpose(out=c.ps[:, bank, j * 128:(j + 1) * 128],
                                          in_=xin[b][:, j, kc * 128:(kc + 1) * 128],
                                          identity=c.identf[:])
                    return ins
                S.op("pe", mm, reads=["ti_x%d" % b, "identf"], writes=["ps%d" % bank])
                eng = "act" if kc % 2 == 0 else "dve"
                if eng == "act":
                    S.op("act", lambda e, kc=kc, bank=bank, b=b: e.copy(out=xT[b][:, kc, :], in_=psb(c, bank)),
                         reads=["ps%d" % bank], writes=["ti_t%d_%d" % (b, kc)])
                else:
                    S.op("dve", lambda e, kc=kc, bank=bank, b=b: e.tensor_copy(out=xT[b][:, kc, :], in_=psb(c, bank)),
                         reads=["ps%d" % bank], writes=["ti_t%d_%d" % (b, kc)])
            S.dma("sp", xr[:, :, n * 512:(n + 1) * 512], xT[b][:],
                  reads=["ti_t%d_%d" % (b, kc) for kc in range(KC)], writes=["xres_%d" % n])
        S.barrier()


def phase_mods(c, n_layers):
    nc, S = c.nc, c.S
    with ExitStack() as es:
        cs = es.enter_context(nc.sbuf_tensor("md_cs", [128, KC], F32))
        cs2 = es.enter_context(nc.sbuf_tensor("md_cs2", [128, KC, 2], F32))
        wt = [es.enter_context(nc.sbuf_tensor("md_w%d" % i, [128, KC, 512], F32)) for i in range(2)]
        brow = es.enter_context(nc.sbuf_tensor("md_b", [1, 6 * D], F32))
        mrow = es.enter_context(nc.sbuf_tensor("md_m", [1, 6 * D], F32))
        S.dma("sp", cs[:], c.cvec, writes=["md_cs"])
        S.op("act", lambda e: e.activation(out=cs[:], in_=cs[:], func=AF.Silu),
             reads=["md_cs"], writes=["md_cs"])
        S.op("dve", lambda e: e.tensor_copy(out=cs2[:], in_=bcast(cs[:], 2, 2)),
             reads=["md_cs"], writes=["md_cs2"])
        NG = 6 * D // 512
        for l in range(n_layers):
            wv = c.ada_w[l].rearrange("(kc p) n -> p kc n", p=128)
            S.dma("sp", brow[:], c.ada_b[l], writes=["md_b"])
            S.dma("sp", wt[0][:], wv[:, :, 0:512], writes=["md_w0"])
            for g in range(NG):
                b = g % 2
                if g + 1 < NG:
                    S.dma("sp" if g % 2 == 0 else "act", wt[1 - b][:], wv[:, :, (g + 1) * 512:(g + 2) * 512],
                          writes=["md_w%d" % (1 - b)])
                bank = g % 4

                def mm(e, b=b, bank=bank):
                    ins = None
                    for kc in range(KC):
                        ins = e.matmul(c.ps[0:2, bank, :], lhsT=cs2[:, kc, :], rhs=wt[b][:, kc, :],
                                       start=(kc == 0), stop=(kc == KC - 1))
                    return ins
                S.op("pe", mm, reads=["md_cs2", "md_w%d" % b], writes=["ps%d" % bank])
                S.op("dve", lambda e, g=g, bank=bank: e.tensor_tensor(
                    out=mrow[0:1, g * 512:(g + 1) * 512], in0=c.ps[0:1, bank, :],
                    in1=brow[0:1, g * 512:(g + 1) * 512], op=ALU.add),
                    reads=["ps%d" % bank, "md_b"], writes=["md_m%d" % g])

            def tr(e):
                ins = None
                for j in range(6 * KC):
                    ins = e.transpose(out=c.ps[:, 4, j:j + 1], in_=mrow[0:1, j * 128:(j + 1) * 128],
                                      identity=c.identf[0:1, 0:1])
                return ins
            S.op("pe", tr, reads=["md_m%d" % g for g in range(NG)] + ["identf"], writes=["ps4"])
            S.op("dve", lambda e, l=l: e.tensor_copy(out=c.modv[:, l, :], in_=c.ps[:, 4, 0:6 * KC]),
                 reads=["ps4"], writes=["modv"])
            for which, off in ((0, 1 * KC), (1, 4 * KC)):
                S.op("dve", lambda e, l=l, which=which, off=off: e.scalar_tensor_tensor(
                    out=c.gain[:, l, which, :], in0=c.modv[:, l, off:off + KC], scalar=1.0,
                    in1=c.nwt[:, (l * 2 + which) * KC:(l * 2 + which + 1) * KC],
                    op0=ALU.add, op1=ALU.mult),
                    reads=["modv", "nwt"], writes=["gain"])
        S.barrier()


def emit_norm(c, tiles, hT, hname, t0, ntok, gain_ap, shift_ap, tag):
    nc, S = c.nc, c.S
    TW = 256
    xr = c.xres.rearrange("(kc p) t -> p kc t", p=128)
    xt, sq, rs = tiles["xt"], tiles["sq"], tiles["rs"]
    nt = ntok // TW
    S.dma("sp", xt[0][:], xr[:, :, t0:t0 + TW], reads=["xres_%d" % (t0 // 512)], writes=[tag + "xt0"])
    for i in range(nt):
        b = i % 2
        if i + 1 < nt:
            tt = t0 + (i + 1) * TW
            S.dma("sp", xt[1 - b][:], xr[:, :, tt:tt + TW], reads=["xres_%d" % (tt // 512)],
                  writes=[tag + "xt%d" % (1 - b)])
        S.op("act", lambda e, b=b: e.activation(out=sq[:], in_=xt[b][:], func=AF.Square),
             reads=[tag + "xt%d" % b], writes=[tag + "sq"])
        bank = 7

        def mm(e):
            ins = None
            for kc in range(KC):
                ins = e.matmul(c.ps[:, bank, 0:TW], lhsT=c.onesb[:], rhs=sq[:, kc, :],
                               start=(kc == 0), stop=(kc == KC - 1))
            return ins
        S.op("pe", mm, reads=[tag + "sq", "onesb"], writes=["ps7"])
        S.op("act", lambda e: e.activation(out=rs[:], in_=c.ps[:, bank, 0:TW], func=AF.Sqrt,
                                            scale=1.0 / D, bias=c.epsc[:]),
             reads=["ps7", "epsc"], writes=[tag + "rs"])
        S.op("dve", lambda e: e.reciprocal(out=rs[:], in_=rs[:]), reads=[tag + "rs"], writes=[tag + "rs"])
        S.op("dve", lambda e, b=b: e.tensor_tensor(out=xt[b][:], in0=xt[b][:], in1=bcast(rs[:], 1, KC),
                                                   op=ALU.mult),
             reads=[tag + "xt%d" % b, tag + "rs"], writes=[tag + "xt%d" % b])
        for kc in range(KC):
            eng = "act" if kc % 2 == 0 else "pool"
            o = hT[:, kc, i * TW:(i + 1) * TW]
            if eng == "act":
                S.op("act", lambda e, kc=kc, b=b, o=o: e.activation(
                    out=o, in_=xt[b][:, kc, :], func=AF.Identity,
                    scale=gain_ap[:, kc:kc + 1], bias=shift_ap[:, kc:kc + 1]),
                    reads=[tag + "xt%d" % b, "gain", "modv"], writes=[hname + "_%d" % kc])
            else:
                S.op("pool", lambda e, kc=kc, b=b, o=o: e.tensor_scalar(
                    out=o, in0=xt[b][:, kc, :], scalar1=gain_ap[:, kc:kc + 1],
                    scalar2=shift_ap[:, kc:kc + 1], op0=ALU.mult, op1=ALU.add),
                    reads=[tag + "xt%d" % b, "gain", "modv"], writes=[hname + "_%d" % kc])


def alloc_norm_tiles(c, es, tag):
    nc = c.nc
    return {
        "xt": [es.enter_context(nc.sbuf_tensor(tag + "xt%d" % i, [128, KC, 256], F32)) for i in range(2)],
        "sq": es.enter_context(nc.sbuf_tensor(tag + "sq", [128, KC, 256], BF16)),
        "rs": es.enter_context(nc.sbuf_tensor(tag + "rs", [128, 256], F32)),
    }


def phase_ffn_up(c, l):
    nc, S = c.nc, c.S
    N = HALF
    with ExitStack() as es:
        hT = es.enter_context(nc.sbuf_tensor("fu_h", [128, KC, N], BF16))
        nt = alloc_norm_tiles(c, es, "fu_")
        cw = es.enter_context(nc.sbuf_tensor("fu_cw", [128, 2 * FC, 4], F32))
        halo = es.enter_context(nc.sbuf_tensor("fu_halo", [128, 2 * FC, 2], F32))
        wg = [es.enter_context(nc.sbuf_tensor("fu_wg%d" % i, [128, KC, 128], BF16)) for i in range(2)]
        wv = [es.enter_context(nc.sbuf_tensor("fu_wv%d" % i, [128, KC, 128], BF16)) for i in range(2)]
        pre = [[es.enter_context(nc.sbuf_tensor("fu_pre%d%d" % (i, j), [128, N + 2], F32)) for j in range(2)]
               for i in range(2)]
        acc = [es.enter_context(nc.sbuf_tensor("fu_acc%d" % j, [128, N], F32)) for j in range(2)]
        sg = es.enter_context(nc.sbuf_tensor("fu_sg", [128, N], F32))
        gout = [es.enter_context(nc.sbuf_tensor("fu_go%d" % i, [128, N], BF16)) for i in range(2)]
        S.dma("sp", cw[:], c.ffn_cw[l].rearrange("p (c k) -> p c k", k=4), writes=["fu_cw"])
        wsrc = c.ffn_w_up[l].rearrange("(kc p) n -> p kc n", p=128)
        gain_ap = c.gain[:, l, 1, :]
        shift_ap = c.modv[:, l, 3 * KC:4 * KC]
        for half in range(2):
            t0 = half * N
            emit_norm(c, nt, hT, "fu_h", t0, N, gain_ap, shift_ap, "fu_")

            def load_w(j, b):
                S.dma("pool", wg[b][:], wsrc[:, :, j * 128:(j + 1) * 128], writes=["fu_wg%d" % b])
                S.dma("pool", wv[b][:], wsrc[:, :, FFN + j * 128:FFN + (j + 1) * 128], writes=["fu_wv%d" % b])
            load_w(0, 0)
            for j in range(FC):
                b = j % 2
                if j + 1 < FC:
                    load_w(j + 1, 1 - b)
                for gv in range(2):
                    ch = gv * FC + j
                    if half == 0:
                        S.op("pool", lambda e, gv=gv, b=b: e.memset(pre[b][gv][:, 0:2], 0.0),
                             writes=["fu_pre%d%d_h" % (b, gv)])
                    else:
                        S.op("pool", lambda e, gv=gv, b=b, ch=ch: e.tensor_copy(out=pre[b][gv][:, 0:2], in_=halo[:, ch, :]),
                             reads=["fu_halo"], writes=["fu_pre%d%d_h" % (b, gv)])
                for st in range(2):
                    for gv in range(2):
                        w = wg[b] if gv == 0 else wv[b]
                        wname = ("fu_wg%d" if gv == 0 else "fu_wv%d") % b
                        for tt in range(2):
                            tok = (st * 2 + tt) * 512
                            bank = (st % 2) * 4 + gv * 2 + tt
                            def mm(e, w=w, tok=tok, bank=bank):
                                ins = None
                                for kc in range(KC):
                                    ins = e.matmul(c.ps[:, bank, :], lhsT=w[:, kc, :], rhs=hT[:, kc, tok:tok + 512],
                                                   start=(kc == 0), stop=(kc == KC - 1))
                                return ins
                            S.op("pe", mm, reads=[wname] + ["fu_h_%d" % kc for kc in range(KC)], writes=["ps%d" % bank])
                            S.op("act", lambda e, gv=gv, b=b, tok=tok, bank=bank: e.copy(
                                out=pre[b][gv][:, 2 + tok:2 + tok + 512], in_=psb(c, bank)),
                                reads=["ps%d" % bank], writes=["fu_pre%d%d_%d" % (b, gv, tok)])
                for gv in range(2):
                    ch = gv * FC + j
                    eng = "dve"
                    prn = ["fu_pre%d%d_%d" % (b, gv, t * 512) for t in range(4)] + ["fu_pre%d%d_h" % (b, gv)]
                    p_ = pre[b][gv]
                    a_ = acc[gv]
                    S.op(eng, lambda e, p_=p_, a_=a_, ch=ch: e.tensor_scalar(
                        out=a_[:], in0=p_[:, 0:N], scalar1=cw[:, ch, 0:1], scalar2=cw[:, ch, 3:4],
                        op0=ALU.mult, op1=ALU.add), reads=prn + ["fu_cw"], writes=["fu_acc%d" % gv])
                    for k in (1, 2):
                        S.op(eng, lambda e, p_=p_, a_=a_, ch=ch, k=k: e.scalar_tensor_tensor(
                            out=a_[:], in0=p_[:, k:k + N], scalar=cw[:, ch, k:k + 1], in1=a_[:],
                            op0=ALU.mult, op1=ALU.add), reads=prn + ["fu_cw", "fu_acc%d" % gv],
                            writes=["fu_acc%d" % gv])
                    if half == 0:
                        S.op("pool", lambda e, p_=p_, ch=ch: e.tensor_copy(out=halo[:, ch, :], in_=p_[:, N:N + 2]),
                             reads=prn, writes=["fu_halo"])
                S.op("act", lambda e: e.activation(out=sg[:], in_=acc[0][:], func=AF.Silu),
                     reads=["fu_acc0"], writes=["fu_sg"])
                S.op("dve", lambda e, b=b: e.tensor_tensor(out=gout[b][:], in0=sg[:], in1=acc[1][:], op=ALU.mult),
                     reads=["fu_sg", "fu_acc1"], writes=["fu_go%d" % b])
                S.dma("sp", c.gbuf[j * 128:(j + 1) * 128, t0:t0 + N], gout[b][:],
                      reads=["fu_go%d" % b], writes=["gbuf_%d_%d" % (j, half)])
        S.barrier()


def phase_down(c, wsrc_dram, nfc, src_buf, src_tag, gate_ap, rowscale_dram=None):
    nc, S = c.nc, c.S
    TW = 256
    OH = 8
    with ExitStack() as es:
        wd = es.enter_context(nc.sbuf_tensor("fd_w", [128, nfc, OH * 128], BF16))
        gT = [es.enter_context(nc.sbuf_tensor("fd_g%d" % i, [128, nfc, TW], BF16)) for i in range(2)]
        xo = [es.enter_context(nc.sbuf_tensor("fd_x%d" % i, [128, OH, TW], F32)) for i in range(2)]
        xn = [es.enter_context(nc.sbuf_tensor("fd_n%d" % i, [128, OH, TW], F32)) for i in range(2)]
        if rowscale_dram is not None:
            rsc = es.enter_context(nc.sbuf_tensor("fd_rs", [128, nfc], F32))
            S.dma("sp", rsc[:], rowscale_dram, writes=["fd_rs"])
        wsrc = wsrc_dram.rearrange("(fc p) n -> p fc n", p=128)
        gsrc = src_buf.rearrange("(fc p) t -> p fc t", p=128)
        xr = c.xres.rearrange("(kc p) t -> p kc t", p=128)
        NT = L // TW
        nq = 4
        qs = nfc // nq
        for oh in range(2):
            for q in range(nq):
                S.dma("pool", wd[:, q * qs:(q + 1) * qs, :], wsrc[:, q * qs:(q + 1) * qs, oh * 1024:(oh + 1) * 1024],
                      writes=["fd_w%d" % q])
            wnames = ["fd_w%d" % q for q in range(nq)]
            if rowscale_dram is not None:
                for fc in range(nfc):
                    S.op("act", lambda e, fc=fc: e.activation(out=wd[:, fc, :], in_=wd[:, fc, :], func=AF.Identity,
                                                              scale=rsc[:, fc:fc + 1]),
                         reads=["fd_rs", "fd_w%d" % (fc // qs)], writes=["fd_w%d" % (fc // qs)])

            def load(t, b):
                S.dma("sp", gT[b][:], gsrc[:, :, t * TW:(t + 1) * TW], writes=["fd_g%d" % b])
                S.dma("act", xo[b][:], xr[:, oh * OH:(oh + 1) * OH, t * TW:(t + 1) * TW],
                      reads=["xres_%d" % ((t * TW) // 512)], writes=["fd_x%d" % b])
            load(0, 0)
            for t in range(NT):
                b = t % 2
                if t + 1 < NT:
                    load(t + 1, 1 - b)
                for oc in range(OH):
                    bank = oc % 8

                    def mm(e, oc=oc, bank=bank, b=b):
                        ins = None
                        for fc in range(nfc):
                            ins = e.matmul(c.ps[:, bank, 0:TW], lhsT=wd[:, fc, oc * 128:(oc + 1) * 128],
                                           rhs=gT[b][:, fc, :], start=(fc == 0), stop=(fc == nfc - 1))
                        return ins
                    S.op("pe", mm, reads=wnames + ["fd_g%d" % b], writes=["ps%d" % bank])
                    col = oh * OH + oc
                    S.op("dve", lambda e, oc=oc, bank=bank, b=b, col=col: e.scalar_tensor_tensor(
                        out=xn[b][:, oc, :], in0=c.ps[:, bank, 0:TW], scalar=gate_ap[:, col:col + 1],
                        in1=xo[b][:, oc, :], op0=ALU.mult, op1=ALU.add),
                        reads=["ps%d" % bank, "fd_x%d" % b, "modv"], writes=["fd_n%d_%d" % (b, oc)])
                S.dma("sp", xr[:, oh * OH:(oh + 1) * OH, t * TW:(t + 1) * TW], xn[b][:],
                      reads=["fd_n%d_%d" % (b, oc) for oc in range(OH)], writes=["xres_%d" % ((t * TW) // 512)])
        S.barrier()


def phase_ffn_down(c, l):
    phase_down(c, c.ffn_w_down[l], FC, c.gbuf, "gbuf", c.modv[:, l, 5 * KC:6 * KC])


def phase_final(c):
    nc, S = c.nc, c.S
    TW = 256
    with ExitStack() as es:
        nt = alloc_norm_tiles(c, es, "fn_")
        hf = [es.enter_context(nc.sbuf_tensor("fn_h%d" % i, [128, KC, TW], F32)) for i in range(2)]
        ot = [es.enter_context(nc.sbuf_tensor("fn_o%d" % i, [128, 2, D], F32)) for i in range(2)]
        zero = es.enter_context(nc.sbuf_tensor("fn_z", [128, KC], F32))
        S.op("dve", lambda e: e.memset(zero[:], 0.0), writes=["fn_z"])
        fw = c.nwt[:, DEPTH * 2 * KC:DEPTH * 2 * KC + KC]
        ov = c.out.rearrange("(n j p) d -> n p j d", j=2, p=128)
        for t in range(L // TW):
            b = t % 2
            emit_norm_f32(c, nt, hf[b], "fn_h%d" % b, t * TW, fw, zero[:], "fn_")
            for j in range(2):
                for kc in range(KC):
                    bank = (kc // 4) % 4 + 4 * (j % 2)
                    if bank == 7:
                        bank = 3
                    q = kc % 4

                    def mm(e, j=j, kc=kc, bank=bank, q=q, b=b):
                        return e.transpose(out=c.ps[:, bank, q * 128:(q + 1) * 128],
                                           in_=hf[b][:, kc, j * 128:(j + 1) * 128], identity=c.identf[:])
                    S.op("pe", mm, reads=["fn_h%d_%d" % (b, kc), "identf"], writes=["ps%d" % bank])
                    if q == 3:
                        k0 = kc - 3
                        eng = "act" if (kc // 4) % 2 == 0 else "dve"
                        if eng == "act":
                            S.op("act", lambda e, j=j, k0=k0, bank=bank, b=b: e.copy(
                                out=ot[b][:, j, k0 * 128:(k0 + 4) * 128], in_=psb(c, bank)),
                                reads=["ps%d" % bank], writes=["fn_o%d_%d_%d" % (b, j, k0)])
                        else:
                            S.op("dve", lambda e, j=j, k0=k0, bank=bank, b=b: e.tensor_copy(
                                out=ot[b][:, j, k0 * 128:(k0 + 4) * 128], in_=psb(c, bank)),
                                reads=["ps%d" % bank], writes=["fn_o%d_%d_%d" % (b, j, k0)])
            S.dma("sp", ov[t], ot[b][:],
                  reads=["fn_o%d_%d_%d" % (b, j, k0) for j in range(2) for k0 in (0, 4, 8, 12)],
                  writes=["out_%d" % t])
        S.barrier()


def emit_norm_f32(c, tiles, hT, hname, t0, gain_ap, shift_ap, tag):
    nc, S = c.nc, c.S
    TW = 256
    xr = c.xres.rearrange("(kc p) t -> p kc t", p=128)
    xt, sq, rs = tiles["xt"], tiles["sq"], tiles["rs"]
    b = (t0 // TW) % 2
    S.dma("sp", xt[b][:], xr[:, :, t0:t0 + TW], reads=["xres_%d" % (t0 // 512)], writes=[tag + "xt%d" % b])
    S.op("act", lambda e: e.activation(out=sq[:], in_=xt[b][:], func=AF.Square),
         reads=[tag + "xt%d" % b], writes=[tag + "sq"])

    def mm(e):
        ins = None
        for kc in range(KC):
            ins = e.matmul(c.ps[:, 7, 0:TW], lhsT=c.onesb[:], rhs=sq[:, kc, :],
                           start=(kc == 0), stop=(kc == KC - 1))
        return ins
    S.op("pe", mm, reads=[tag + "sq", "onesb"], writes=["ps7"])
    S.op("act", lambda e: e.activation(out=rs[:], in_=c.ps[:, 7, 0:TW], func=AF.Sqrt,
                                        scale=1.0 / D, bias=c.epsc[:]),
         reads=["ps7", "epsc"], writes=[tag + "rs"])
    S.op("dve", lambda e: e.reciprocal(out=rs[:], in_=rs[:]), reads=[tag + "rs"], writes=[tag + "rs"])
    S.op("dve", lambda e: e.tensor_tensor(out=xt[b][:], in0=xt[b][:], in1=bcast(rs[:], 1, KC), op=ALU.mult),
         reads=[tag + "xt%d" % b, tag + "rs"], writes=[tag + "xt%d" % b])
    for kc in range(KC):
        eng = "act" if kc % 2 == 0 else "pool"
        if eng == "act":
            S.op("act", lambda e, kc=kc: e.activation(
                out=hT[:, kc, :], in_=xt[b][:, kc, :], func=AF.Identity,
                scale=gain_ap[:, kc:kc + 1], bias=shift_ap[:, kc:kc + 1]),
                reads=[tag + "xt%d" % b, "nwt", "fn_z"], writes=[hname + "_%d" % kc])
        else:
            S.op("pool", lambda e, kc=kc: e.tensor_scalar(
                out=hT[:, kc, :], in0=xt[b][:, kc, :], scalar1=gain_ap[:, kc:kc + 1],
                scalar2=None, op0=ALU.mult),
                reads=[tag + "xt%d" % b, "nwt"], writes=[hname + "_%d" % kc])


def cols128(v):
    v = np.asarray(v)
    return np.ascontiguousarray(v.reshape(-1, 128).T)


def host_inputs(inp, b):
    f32 = np.float32
    m = {}
    m["x"] = np.ascontiguousarray(inp["x"][b])
    m["cvec"] = cols128(inp["c"][b]).astype(f32)
    m["ada_w"] = inp["ada_w"]
    m["ada_b"] = np.ascontiguousarray(inp["ada_b"].reshape(DEPTH, 1, 6 * D))
    nw = []
    for l in range(DEPTH):
        nw.append(cols128(inp["norm_mix_w"][l]))
        nw.append(cols128(inp["norm_ffn_w"][l]))
    nw.append(cols128(inp["final_norm_w"]))
    m["nw"] = np.ascontiguousarray(np.concatenate(nw, axis=1)).astype(f32)
    m["ffn_w_up"] = inp["ffn_w_up"]
    cw = np.zeros((DEPTH, 128, 2 * FC, 4), f32)
    for l in range(DEPTH):
        for k in range(3):
            cw[l, :, :, k] = cols128(inp["ffn_conv_w"][l, k])
        cw[l, :, :, 3] = cols128(inp["ffn_conv_b"][l])
    m["ffn_cw"] = cw.reshape(DEPTH, 128, 2 * FC * 4)
    m["ffn_w_down"] = inp["ffn_w_down"]
    m["ssm_w_in"] = inp["ssm_w_in"]
    scw = np.zeros((2, 128, 48, 5), f32)
    for j in range(2):
        for k in range(4):
            scw[j, :, :, k] = cols128(inp["ssm_conv_w"][j, k])
        scw[j, :, :, 4] = cols128(inp["ssm_conv_b"][j])
    m["ssm_cw"] = scw.reshape(2, 128, 48 * 5)
    m["ssm_hp"] = np.ascontiguousarray(np.stack([inp["ssm_dt_bias"], inp["ssm_A_log"]], axis=-1)).astype(f32)
    sv = np.zeros((2, 128, 64), f32)
    for j in range(2):
        sv[j, :, 0:32] = cols128(np.repeat(inp["ssm_D"][j], 64))
        sv[j, :, 32:64] = cols128(inp["ssm_norm_w"][j])
    m["ssm_vec"] = sv
    m["ssm_w_out"] = inp["ssm_w_out"]
    m["gdn_w_in"] = inp["gdn_w_in"]
    gcw = np.zeros((2, 128, 64, 5), f32)
    for j in range(2):
        for k in range(4):
            gcw[j, :, :, k] = cols128(inp["gdn_conv_w"][j, k])
    m["gdn_cw"] = gcw.reshape(2, 128, 64 * 5)
    ghp = np.zeros((2, 64, 2), f32)
    ghp[:, 32:64, 0] = inp["gdn_dt_bias"]
    ghp[:, 32:64, 1] = inp["gdn_A_log"]
    m["gdn_hp"] = ghp
    m["gdn_nw"] = np.ascontiguousarray(inp["gdn_norm_w"].reshape(2, 128, 1)).astype(f32)
    m["gdn_nwx"] = np.ascontiguousarray(np.repeat(inp["gdn_norm_w"].reshape(2, 128, 1), 32, axis=2)).astype(f32)
    m["gdn_w_out"] = inp["gdn_w_out"]
    lm = np.zeros((128, 7, 128), f32)
    ii = np.arange(128)
    for lv in range(7):
        s_ = 1 << lv
        rowm = (ii % (2 * s_)) >= s_
        colm = (ii % (2 * s_)) < s_
        same = (ii[:, None] // (2 * s_)) == (ii[None, :] // (2 * s_))
        lm[:, lv, :] = (same & rowm[:, None] & colm[None, :]).astype(f32) + np.eye(128, dtype=f32)
    m["lmask"] = lm.reshape(128, 7 * 128).astype(ml_dtypes.bfloat16)
    m["lmT0"] = np.ascontiguousarray(lm[:, 0, :].T).astype(ml_dtypes.bfloat16)
    cbv = np.zeros((128, 2), f32)
    cbv[:, 0] = 128.0e-6
    cbv[:, 1] = 1.0e-6
    m["cbias"] = cbv
    m["maskT"] = np.triu(np.ones((128, 128), f32))
    m["maskS"] = np.triu(np.ones((128, 128), f32), 1)
    m["identb"] = np.eye(128).astype(ml_dtypes.bfloat16)
    m["ident_f"] = np.eye(128, dtype=f32)
    m["ones_b"] = np.ones((128, 128), dtype=ml_dtypes.bfloat16)
    return m


_NC_CACHE = {}


def kernel(**inputs):
    inp = {k: np.asarray(v) for k, v in inputs.items()}
    if "full" not in _NC_CACHE:
        _NC_CACHE["full"] = build_program()
    nc = _NC_CACHE["full"]
    B = inp["x"].shape[0]
    in_maps = [host_inputs(inp, b) for b in range(B)]
    res = run_bass_kernel_spmd(nc, in_maps, core_ids=list(range(B)))
    return np.stack([np.asarray(r["out"]) for r in res.results], axis=0).astype(np.float32)


def phase_inproj(c, l, wsrc_dram, chunks, cw_dram, ncw, gain_ap, shift_ap):
    nc, S = c.nc, c.S
    N = HALF
    with ExitStack() as es:
        hT = es.enter_context(nc.sbuf_tensor("ip_h", [128, KC, N], BF16))
        nt = alloc_norm_tiles(c, es, "ip_")
        cw = es.enter_context(nc.sbuf_tensor("ip_cw", [128, ncw, 5], F32))
        halo = es.enter_context(nc.sbuf_tensor("ip_halo", [128, ncw, 3], F32))
        w = [es.enter_context(nc.sbuf_tensor("ip_w%d" % i, [128, KC, 128], BF16)) for i in range(3)]
        pre = [es.enter_context(nc.sbuf_tensor("ip_pre%d" % i, [128, N + 3], F32)) for i in range(2)]
        acc = [es.enter_context(nc.sbuf_tensor("ip_acc%d" % i, [128, N], F32)) for i in range(2)]
        ob = [es.enter_context(nc.sbuf_tensor("ip_ob%d" % i, [128, N], BF16)) for i in range(2)]
        rawt = es.enter_context(nc.sbuf_tensor("ip_raw", [64, N], F32))
        S.dma("sp", cw[:], cw_dram.rearrange("p (c k) -> p c k", k=5), writes=["ip_cw"])
        wsrc = wsrc_dram.rearrange("(kc p) n -> p kc n", p=128)
        nchunks = len(chunks)
        for half in range(2):
            t0 = half * N
            emit_norm(c, nt, hT, "ip_h", t0, N, gain_ap, shift_ap, "ip_")

            def load_w(ci):
                col0, width = chunks[ci][0], chunks[ci][1]
                S.dma("pool", w[ci % 3][:, :, 0:width], wsrc[:, :, col0:col0 + width], writes=["ip_w%d" % (ci % 3)])
            load_w(0)
            if nchunks > 1:
                load_w(1)
            for ci, (col0, width, kind, dst, cwi) in enumerate(chunks):
                wb = ci % 3
                pb = ci % 2
                if ci + 2 < nchunks:
                    load_w(ci + 2)
                if kind == "conv":
                    if half == 0:
                        S.op("pool", lambda e, pb=pb: e.memset(pre[pb][:, 0:3], 0.0), writes=["ip_pre%d_h" % pb])
                    else:
                        S.op("pool", lambda e, pb=pb, cwi=cwi: e.tensor_copy(out=pre[pb][:, 0:3], in_=halo[:, cwi, :]),
                             reads=["ip_halo"], writes=["ip_pre%d_h" % pb])
                for t in range(4):
                    bank = (ci % 2) * 4 + t
                    tok = t * 512

                    def mm(e, wb=wb, width=width, bank=bank, tok=tok):
                        ins = None
                        for kc in range(KC):
                            ins = e.matmul(c.ps[0:width, bank, :], lhsT=w[wb][:, kc, 0:width], rhs=hT[:, kc, tok:tok + 512],
                                           start=(kc == 0), stop=(kc == KC - 1))
                        return ins
                    S.op("pe", mm, reads=["ip_w%d" % wb] + ["ip_h_%d" % kc for kc in range(KC)], writes=["ps%d" % bank])
                    if kind == "conv":
                        S.op("act", lambda e, pb=pb, bank=bank, tok=tok: e.copy(out=pre[pb][:, 3 + tok:3 + tok + 512], in_=psb(c, bank)),
                             reads=["ps%d" % bank], writes=["ip_pre%d_%d" % (pb, t)])
                    elif kind == "silu":
                        S.op("act", lambda e, pb=pb, bank=bank, tok=tok: e.activation(out=ob[pb][:, tok:tok + 512], in_=psb(c, bank), func=AF.Silu),
                             reads=["ps%d" % bank], writes=["ip_ob%d_%d" % (pb, t)])
                    else:
                        S.op("act", lambda e, bank=bank, tok=tok, width=width: e.copy(out=rawt[0:width, tok:tok + 512], in_=c.ps[0:width, bank, :]),
                             reads=["ps%d" % bank], writes=["ip_raw_%d" % t])
                if kind == "conv":
                    prn = ["ip_pre%d_%d" % (pb, t) for t in range(4)] + ["ip_pre%d_h" % pb]
                    p_, a_ = pre[pb], acc[pb]
                    S.op("dve", lambda e, p_=p_, a_=a_, cwi=cwi: e.tensor_scalar(
                        out=a_[:], in0=p_[:, 0:N], scalar1=cw[:, cwi, 0:1], scalar2=cw[:, cwi, 4:5],
                        op0=ALU.mult, op1=ALU.add), reads=prn + ["ip_cw"], writes=["ip_acc%d" % pb])
                    for k in (1, 2, 3):
                        S.op("dve", lambda e, p_=p_, a_=a_, cwi=cwi, k=k: e.scalar_tensor_tensor(
                            out=a_[:], in0=p_[:, k:k + N], scalar=cw[:, cwi, k:k + 1], in1=a_[:],
                            op0=ALU.mult, op1=ALU.add), reads=prn + ["ip_cw", "ip_acc%d" % pb], writes=["ip_acc%d" % pb])
                    if half == 0:
                        S.op("pool", lambda e, p_=p_, cwi=cwi: e.tensor_copy(out=halo[:, cwi, :], in_=p_[:, N:N + 3]),
                             reads=prn, writes=["ip_halo"])
                    S.op("act", lambda e, a_=a_, pb=pb: e.activation(out=ob[pb][:], in_=a_[:], func=AF.Silu),
                         reads=["ip_acc%d" % pb], writes=["ip_ob%d_%d" % (pb, t) for t in range(4)])
                if kind in ("conv", "silu"):
                    S.dma("sp", c.pbuf[dst:dst + 128, t0:t0 + N], ob[pb][:],
                          reads=["ip_ob%d_%d" % (pb, t) for t in range(4)], writes=["pbuf_%d_%d" % (dst, half)])
                else:
                    S.dma("sp", c.rawbuf[0:width, t0:t0 + N], rawt[0:width, :],
                          reads=["ip_raw_%d" % t for t in range(4)], writes=["rawbuf_%d" % half])
        S.barrier()


def psbf(c, bank):
    return c.ps[:, bank, :].bitcast(BF16)


def phase_ssd_pre(c, j):
    nc, S = c.nc, c.S
    with ExitStack() as es:
        raw = es.enter_context(nc.sbuf_tensor("sp_raw", [64, L], F32))
        dtt = es.enter_context(nc.sbuf_tensor("sp_dt", [64, L], F32))
        ac = es.enter_context(nc.sbuf_tensor("sp_ac", [64, L], F32))
        ones = es.enter_context(nc.sbuf_tensor("sp_one", [64, 128], F32))
        hp = es.enter_context(nc.sbuf_tensor("sp_hp", [64, 2], F32))
        an = es.enter_context(nc.sbuf_tensor("sp_an", [64, 1], F32))
        S.dma("sp", raw[:], c.rawbuf[0:64, :], writes=["raw"])
        S.dma("sp", hp[:], c.ssm_hp[j], writes=["hp"])
        S.op("pool", lambda e: e.memset(ones[:], 1.0), writes=["ones"])
        S.op("act", lambda e: e.activation(out=dtt[:], in_=raw[:], func=AF.Exp, bias=hp[:, 0:1]), reads=["raw", "hp"], writes=["dt"])
        S.op("act", lambda e: e.activation(out=dtt[:], in_=dtt[:], func=AF.Ln, bias=1.0), reads=["dt"], writes=["dt"])
        S.op("act", lambda e: e.activation(out=an[:], in_=hp[:, 1:2], func=AF.Exp), reads=["hp"], writes=["an"])
        S.op("dve", lambda e: e.tensor_scalar(out=an[:], in0=an[:], scalar1=-1.0, scalar2=None, op0=ALU.mult), reads=["an"], writes=["an"])
        S.op("dve", lambda e: e.tensor_scalar(out=raw[:], in0=dtt[:], scalar1=an[:, 0:1], scalar2=None, op0=ALU.mult),
             reads=["dt", "an"], writes=["raw"])
        for ch in range(L // 128):
            S.op("dve", lambda e, ch=ch: e.tensor_tensor_scan(out=ac[:, ch * 128:(ch + 1) * 128], data0=ones[:],
                                                            data1=raw[:, ch * 128:(ch + 1) * 128], initial=0.0,
                                                            op0=ALU.mult, op1=ALU.add),
                 reads=["raw", "ones"], writes=["ac%d" % ch])
        S.dma("sp", c.dtd[0:64, :], dtt[:], reads=["dt"], writes=["dtd"])
        S.dma("sp", c.acd[0:64, :], ac[:], reads=["ac%d" % ch for ch in range(L // 128)], writes=["acd"])
        S.barrier()


def phase_ssd_scan(c, l):
    nc, S = c.nc, c.S
    j = l // 2
    NCH = L // 128
    phase_ssd_pre(c, j)
    with ExitStack() as es:
        def sb(name, shape, dt=F32):
            return es.enter_context(nc.sbuf_tensor(name, shape, dt))
        St = sb("ss_S", [128, 8, 512])
        Sb = sb("ss_Sb", [128, 8, 512], BF16)
        maskT = sb("ss_mask", [128, 128])
        identb = sb("ss_idb", [128, 128], BF16)
        vec = sb("ss_vec", [128, 64])
        eps5 = sb("ss_eps", [128, 1])
        xsT = [sb("ss_xs%d" % i, [128, 32, 128], BF16) for i in range(2)]
        BC = [sb("ss_bc%d" % i, [128, 16, 128], BF16) for i in range(2)]
        zT = [sb("ss_z%d" % i, [128, 32, 128], BF16) for i in range(2)]
        dta = [sb("ss_dta%d" % i, [64, 2, 128]) for i in range(2)]
        Abc = [sb("ss_abc%d" % i, [128, 64, 128]) for i in range(1)]
        dtk = sb("ss_dtk", [128, 128])
        E = sb("ss_E", [128, 64])
        dend = sb("ss_dend", [128, 64])
        EL = sb("ss_EL", [128, 64])
        xdt = sb("ss_xdt", [128, 64, 64], BF16)
        xdtd = sb("ss_xdtd", [128, 64, 64], BF16)
        Btok = sb("ss_Btok", [128, 8, 128], BF16)
        CBm = sb("ss_CBm", [128, 8, 128])
        sg = [sb("ss_sg%d" % i, [128, 8, 128]) for i in range(2)]
        Mt = sb("ss_Mt", [128, 64, 128], BF16)
        ytok = sb("ss_ytok", [128, 32, 128])
        tmpy = [sb("ss_tmpy%d" % i, [128, 8, 64]) for i in range(2)]
        tmpS = [sb("ss_tmpS%d" % i, [128, 8, 64]) for i in range(2)]
        yg = [sb("ss_yg%d" % i, [128, 4, 128]) for i in range(2)]
        sq = [sb("ss_sq%d" % i, [128, 4, 128], BF16) for i in range(2)]
        rstd = [sb("ss_rstd%d" % i, [128, 128]) for i in range(2)]
        ygT = [sb("ss_ygT%d" % i, [128, 32, 128], BF16) for i in range(2)]

        S.dma("sp", maskT[:], c.maskT_d, writes=["mask"])
        S.dma("sp", identb[:], c.identb_d, writes=["idb"])
        S.dma("sp", vec[:], c.ssm_vec[j], writes=["vec"])
        S.op("pool", lambda e: e.memset(eps5[:], 1e-5), writes=["eps5"])
        S.op("pool", lambda e: e.memset(St[:], 0.0), writes=["S%d" % g for g in range(8)])
        S.op("pool", lambda e: e.memset(Sb[:], 0.0), writes=["Sb%d" % g for g in range(8)])

        def cols(ch):
            return slice(ch * 128, (ch + 1) * 128)

        def load(ch):
            b = ch % 2
            S.dma("sp", xsT[b][:], c.pbuf[4096:8192, cols(ch)].rearrange("(cc p) t -> p cc t", p=128), writes=["xs%d" % b])
            S.dma("act", BC[b][:], c.pbuf[8192:10240, cols(ch)].rearrange("(cc p) t -> p cc t", p=128), writes=["bc%d" % b])
            S.dma("act", zT[b][:], c.pbuf[0:4096, cols(ch)].rearrange("(cc p) t -> p cc t", p=128), writes=["z%d" % b])
            S.dma("sp", dta[b][:, 0, :], c.dtd[0:64, cols(ch)], writes=["dta%d" % b])
            S.dma("sp", dta[b][:, 1, :], c.acd[0:64, cols(ch)], writes=["dta%d" % b])

        def load_abc(ch):
            src = bass.AP(tensor=c.acd.tensor, offset=ch * 128, ap=[[0, 128], [L, 64], [1, 128]])
            S.dma("sp", Abc[0][:], src, writes=["abc"])

        load(0)
        STG = Stager(S)
        for ch in range(NCH):
            b = ch % 2
            if ch == 0:
                load_abc(0)
            if ch + 1 < NCH:
                load(ch + 1)
            def t1(e, b=b):
                e.transpose(out=c.ps[:, 0, 0:64], in_=dta[b][0:64, 0, :], identity=c.identf[0:64, 0:64])
                return e.transpose(out=c.ps[:, 0, 64:128], in_=dta[b][0:64, 1, :], identity=c.identf[0:64, 0:64])
            S.op("pe", t1, reads=["dta%d" % b, "identf"], writes=["ps0"])
            S.op("dve", lambda e: e.tensor_copy(out=dtk[:], in_=c.ps[:, 0, 0:128]), reads=["ps0"], writes=["dtk"])
            S.op("act", lambda e: e.activation(out=E[:], in_=dtk[:, 64:128], func=AF.Exp), reads=["dtk"], writes=["E"])
            S.op("dve", lambda e: e.tensor_tensor(out=dend[:], in0=Abc[0][:, :, 127], in1=dtk[:, 64:128], op=ALU.subtract),
                 reads=["abc", "dtk"], writes=["dend"])
            S.op("act", lambda e: e.activation(out=dend[:], in_=dend[:], func=AF.Exp), reads=["dend"], writes=["dend"])
            S.op("act", lambda e: e.activation(out=EL[:], in_=Abc[0][:, :, 127], func=AF.Exp), reads=["abc"], writes=["EL"])
            for q4 in range(4):
                bank = 1 + q4

                def t2(e, q4=q4, bank=bank, b=b):
                    ins = None
                    for q in range(8):
                        ins = e.transpose(out=psbf(c, bank)[:, q * 128:(q + 1) * 128], in_=xsT[b][:, q4 * 8 + q, :], identity=identb[:])
                    return ins
                S.op("pe", t2, reads=["xs%d" % b, "idb"], writes=["ps%d" % bank])
                S.op("dve", lambda e, q4=q4, bank=bank: e.tensor_tensor(
                    out=xdt[:, q4 * 16:(q4 + 1) * 16, :], in0=psbf(c, bank).rearrange("p (h d) -> p h d", d=64),
                    in1=bcast(dtk[:, q4 * 16:(q4 + 1) * 16], 2, 64), op=ALU.mult),
                    reads=["ps%d" % bank, "dtk"], writes=["xdt%d" % q4])
            S.op("dve", lambda e: e.tensor_tensor(out=xdtd[:], in0=xdt[:], in1=bcast(dend[:], 2, 64), op=ALU.mult),
                 reads=["xdt%d" % q for q in range(4)] + ["dend"], writes=["xdtd"])
            def t3(e, b=b):
                ins = None
                for g in range(8):
                    ins = e.transpose(out=psbf(c, 5)[:, g * 128:(g + 1) * 128], in_=BC[b][:, g, :], identity=identb[:])
                return ins
            S.op("pe", t3, reads=["bc%d" % b, "idb"], writes=["ps5"])
            S.op("act", lambda e: e.copy(out=Btok[:], in_=psbf(c, 5).rearrange("p (g n) -> p g n", n=128)), reads=["ps5"], writes=["Btok"])
            for hb in range(2):
                bank = 6 + hb

                def t4(e, hb=hb, bank=bank, b=b):
                    ins = None
                    for gg in range(4):
                        g = hb * 4 + gg
                        ins = e.matmul(c.ps[:, bank, gg * 128:(gg + 1) * 128], lhsT=BC[b][:, g, :], rhs=BC[b][:, 8 + g, :], start=True, stop=True)
                    return ins
                S.op("pe", t4, reads=["bc%d" % b], writes=["ps%d" % bank])
                S.op("dve", lambda e, hb=hb, bank=bank: e.tensor_tensor(
                    out=CBm[:, hb * 4:(hb + 1) * 4, :], in0=c.ps[:, bank, :].rearrange("p (g i) -> p g i", i=128),
                    in1=bcast(maskT[:], 1, 4), op=ALU.mult), reads=["ps%d" % bank, "mask"], writes=["CBm%d" % hb])
            for g in range(8):
                STG.begin()
                k = g % 2
                STG.op("dve", lambda e, g=g, k=k: e.tensor_tensor(out=sg[k][:], in0=Abc[0][:, g * 8:(g + 1) * 8, :],
                                                               in1=bcast(dtk[:, 64 + g * 8:64 + (g + 1) * 8], 2, 128), op=ALU.subtract),
                     reads=["abc", "dtk"], writes=["sg%d" % k])
                STG.op("act", lambda e, k=k: e.activation(out=sg[k][:], in_=sg[k][:], func=AF.Relu, scale=-1.0), reads=["sg%d" % k], writes=["sg%d" % k])
                STG.op("act", lambda e, k=k: e.activation(out=sg[k][:], in_=sg[k][:], func=AF.Exp, scale=-1.0), reads=["sg%d" % k], writes=["sg%d" % k])
                STG.op("pool", lambda e, g=g, k=k: e.tensor_tensor(out=Mt[:, g * 8:(g + 1) * 8, :], in0=sg[k][:],
                                                                in1=bcast(CBm[:, g, :], 1, 8), op=ALU.mult),
                     reads=["sg%d" % k, "CBm%d" % (g // 4)], writes=["Mt%d" % g])
                if g % 2 == 1:
                    STG.flush()
            if ch + 1 < NCH:
                load_abc(ch + 1)
            for g in range(8):
                STG.begin()
                k = g % 2
                bo, bd = k * 2, k * 2 + 1
                STG.op("pe", lambda e, g=g, bo=bo, b=b: e.matmul(c.ps[:, bo, :], lhsT=BC[b][:, 8 + g, :], rhs=Sb[:, g, :], start=True, stop=True),
                     reads=["bc%d" % b, "Sb%d" % g], writes=["ps%d" % bo])

                def t6(e, g=g, bd=bd):
                    ins = None
                    for hh in range(8):
                        h = g * 8 + hh
                        ins = e.matmul(c.ps[:, bd, hh * 64:(hh + 1) * 64], lhsT=Mt[:, h, :], rhs=xdt[:, h, :], start=True, stop=True)
                    return ins
                STG.op("pe", t6, reads=["Mt%d" % g, "xdt%d" % (g // 2)], writes=["ps%d" % bd])
                STG.op("dve", lambda e, g=g, k=k, bo=bo: e.tensor_tensor(
                    out=tmpy[k][:], in0=c.ps[:, bo, :].rearrange("p (h d) -> p h d", d=64),
                    in1=bcast(E[:, g * 8:(g + 1) * 8], 2, 64), op=ALU.mult), reads=["ps%d" % bo, "E"], writes=["tmpy%d" % k])
                STG.op("dve", lambda e, g=g, k=k, bd=bd: e.tensor_tensor(
                    out=ytok[:, g * 4:(g + 1) * 4, :].rearrange("p a b -> p (a b)"), in0=tmpy[k][:].rearrange("p h d -> p (h d)"),
                    in1=c.ps[:, bd, :], op=ALU.add), reads=["ps%d" % bd, "tmpy%d" % k], writes=["ytok%d" % g])
                if g % 2 == 1:
                    STG.flush()
            for g in range(8):
                STG.begin()
                k = g % 2
                bs = 4 + k
                STG.op("pe", lambda e, g=g, bs=bs: e.matmul(c.ps[:, bs, :], lhsT=Btok[:, g, :],
                                                         rhs=xdtd[:, g * 8:(g + 1) * 8, :].rearrange("p h d -> p (h d)"), start=True, stop=True),
                     reads=["Btok", "xdtd"], writes=["ps%d" % bs])
                STG.op("pool", lambda e, g=g, k=k: e.tensor_tensor(out=tmpS[k][:], in0=St[:, g, :].rearrange("p (h d) -> p h d", d=64),
                                                                in1=bcast(EL[:, g * 8:(g + 1) * 8], 2, 64), op=ALU.mult),
                     reads=["S%d" % g, "EL"], writes=["tmpS%d" % k])
                STG.op("dve", lambda e, g=g, k=k, bs=bs: e.tensor_tensor(out=St[:, g, :], in0=tmpS[k][:].rearrange("p h d -> p (h d)"),
                                                                      in1=c.ps[:, bs, :], op=ALU.add),
                     reads=["tmpS%d" % k, "ps%d" % bs], writes=["S%d" % g])
                STG.op("act", lambda e, g=g: e.copy(out=Sb[:, g, :], in_=St[:, g, :]), reads=["S%d" % g], writes=["Sb%d" % g])
                if g % 2 == 1:
                    STG.flush()
            ob = ch % 2
            for G in range(8):
                STG.begin()
                k = G % 2
                bank = 6 + k

                def t8(e, G=G, bank=bank):
                    ins = None
                    for q in range(4):
                        ins = e.transpose(out=c.ps[:, bank, q * 128:(q + 1) * 128], in_=ytok[:, G * 4 + q, :], identity=c.identf[:])
                    return ins
                STG.op("pe", t8, reads=["ytok%d" % G, "identf"], writes=["ps%d" % bank])
                for q in range(4):
                    cc = G * 4 + q
                    STG.op("dve", lambda e, q=q, cc=cc, k=k, bank=bank, b=b: e.scalar_tensor_tensor(
                        out=yg[k][:, q, :], in0=xsT[b][:, cc, :], scalar=vec[:, cc:cc + 1], in1=c.ps[:, bank, q * 128:(q + 1) * 128],
                        op0=ALU.mult, op1=ALU.add), reads=["xs%d" % b, "vec", "ps%d" % bank], writes=["yg%d_%d" % (k, q)])
                ygn = ["yg%d_%d" % (k, q) for q in range(4)]
                STG.op("pool", lambda e, G=G, k=k, b=b: e.tensor_tensor(out=yg[k][:], in0=yg[k][:], in1=zT[b][:, G * 4:(G + 1) * 4, :], op=ALU.mult),
                     reads=ygn + ["z%d" % b], writes=ygn)
                STG.op("act", lambda e, k=k: e.activation(out=sq[k][:], in_=yg[k][:], func=AF.Square), reads=ygn, writes=["sq%d" % k])
                bn = 4 + k

                def t8n(e, k=k, bn=bn):
                    ins = None
                    for q in range(4):
                        ins = e.matmul(c.ps[:, bn, 0:128], lhsT=c.onesb[:], rhs=sq[k][:, q, :], start=(q == 0), stop=(q == 3))
                    return ins
                STG.op("pe", t8n, reads=["sq%d" % k, "onesb"], writes=["ps%d" % bn])
                STG.op("act", lambda e, k=k, bn=bn: e.activation(out=rstd[k][:], in_=c.ps[:, bn, 0:128], func=AF.Ln, scale=1.0 / 512, bias=eps5[:]),
                     reads=["ps%d" % bn, "eps5"], writes=["rstd%d" % k])
                STG.op("act", lambda e, k=k: e.activation(out=rstd[k][:], in_=rstd[k][:], func=AF.Exp, scale=-0.5), reads=["rstd%d" % k], writes=["rstd%d" % k])
                STG.op("dve", lambda e, k=k, G=G, ob=ob: e.tensor_tensor(out=ygT[ob][:, G * 4:(G + 1) * 4, :], in0=yg[k][:], in1=bcast(rstd[k][:], 1, 4), op=ALU.mult),
                     reads=ygn + ["rstd%d" % k], writes=["ygT%d_%d" % (ob, G * 4 + q) for q in range(4)])
                if G % 2 == 1:
                    STG.flush()
            S.dma("sp", c.ybuf[:, cols(ch)].rearrange("(cc p) t -> p cc t", p=128), ygT[ob][:],
                  reads=["ygT%d_%d" % (ob, cc) for cc in range(32)], writes=["ybuf%d" % ch])
        S.barrier()


def ssm_chunks():
    ch = []
    for i in range(32):
        ch.append((i * 128, 128, "silu", i * 128, 0))
    for i in range(48):
        ch.append((4096 + i * 128, 128, "conv", 4096 + i * 128, i))
    ch.append((10240, 64, "raw", 0, 0))
    return ch


def phase_ssm_layer(c, l):
    j = l // 2
    phase_inproj(c, l, c.ssm_w_in[j], ssm_chunks(), c.ssm_cw[j], 48, c.gain[:, l, 0, :], c.modv[:, l, 0:KC])
    phase_ssd_scan(c, l)
    phase_down(c, c.ssm_w_out[j], 32, c.ybuf, "ybuf", c.modv[:, l, 2 * KC:3 * KC], rowscale_dram=c.ssm_vec[j][:, 32:64])


def phase_gdn_pre(c, j):
    nc, S = c.nc, c.S
    with ExitStack() as es:
        raw = es.enter_context(nc.sbuf_tensor("gp_raw", [64, L], F32))
        t1 = es.enter_context(nc.sbuf_tensor("gp_t1", [64, L], F32))
        gc = es.enter_context(nc.sbuf_tensor("gp_gc", [64, L], F32))
        ones = es.enter_context(nc.sbuf_tensor("gp_one", [64, 128], F32))
        hp = es.enter_context(nc.sbuf_tensor("gp_hp", [64, 2], F32))
        an = es.enter_context(nc.sbuf_tensor("gp_an", [64, 1], F32))
        S.dma("sp", raw[:], c.rawbuf[0:64, :], writes=["raw"])
        S.dma("sp", hp[:], c.gdn_hp[j], writes=["hp"])
        S.op("pool", lambda e: e.memset(ones[:], 1.0), writes=["ones"])
        S.op("act", lambda e: e.activation(out=t1[0:32, :], in_=raw[0:32, :], func=AF.Sigmoid), reads=["raw"], writes=["beta"])
        S.op("act", lambda e: e.activation(out=t1[32:64, :], in_=raw[32:64, :], func=AF.Exp, bias=hp[32:64, 0:1]), reads=["raw", "hp"], writes=["sp"])
        S.op("act", lambda e: e.activation(out=t1[32:64, :], in_=t1[32:64, :], func=AF.Ln, bias=1.0), reads=["sp"], writes=["sp"])
        S.op("act", lambda e: e.activation(out=an[32:64, :], in_=hp[32:64, 1:2], func=AF.Exp), reads=["hp"], writes=["an"])
        S.op("dve", lambda e: e.tensor_scalar(out=an[32:64, :], in0=an[32:64, :], scalar1=-1.0, scalar2=None, op0=ALU.mult), reads=["an"], writes=["an"])
        S.op("dve", lambda e: e.tensor_scalar(out=raw[32:64, :], in0=t1[32:64, :], scalar1=an[32:64, 0:1], scalar2=None, op0=ALU.mult),
             reads=["sp", "an", "raw"], writes=["g"])
        for ch in range(L // 128):
            S.op("dve", lambda e, ch=ch: e.tensor_tensor_scan(out=gc[32:64, ch * 128:(ch + 1) * 128], data0=ones[32:64, :],
                                                            data1=raw[32:64, ch * 128:(ch + 1) * 128], initial=0.0,
                                                            op0=ALU.mult, op1=ALU.add),
                 reads=["g", "ones"], writes=["gc%d" % ch])
        S.dma("sp", c.dtd[0:32, :], t1[0:32, :], reads=["beta"], writes=["dtd"])
        S.dma("sp", c.acd[0:32, :], gc[32:64, :], reads=["gc%d" % ch for ch in range(L // 128)], writes=["acd"])
        S.barrier()


def phase_gdn_scan(c, l, HB=2):
    nc, S = c.nc, c.S
    j = l // 2
    NCH = L // 128
    NH = 32 // HB
    NK = 16 // HB
    NG = NH // 4
    phase_gdn_pre(c, j)
    with ExitStack() as es:
        def sb(name, shape, dt=F32):
            return es.enter_context(nc.sbuf_tensor(name, shape, dt))
        St = sb("gs_S", [128, NH, 128])
        Sb = sb("gs_Sb", [128, NH, 128], BF16)
        maskT = sb("gs_mask", [128, 128])
        lmask = sb("gs_lmask", [128, 7, 128], BF16)
        maskS = sb("gs_maskS", [128, 128])
        identb = sb("gs_idb", [128, 128], BF16)
        nwv = sb("gs_nw", [128, 1])
        qT = [sb("gs_q%d" % i, [128, NK, 128], BF16) for i in range(2)]
        kT = [sb("gs_k%d" % i, [128, NK, 128], BF16) for i in range(2)]
        vT = [sb("gs_v%d" % i, [128, NH, 128], BF16) for i in range(2)]
        zT = [sb("gs_z%d" % i, [128, NH, 128], BF16) for i in range(3)]
        dta = [sb("gs_dta%d" % i, [NH, 2, 128]) for i in range(2)]
        Gbc = sb("gs_gbc", [128, NH, 128])
        gk = sb("gs_gk", [128, 2 * NH])
        eg = sb("gs_eg", [128, NH])
        neg = sb("gs_neg", [128, NH])
        nbt = sb("gs_nbt", [128, NH])
        Wt = [sb("gs_W%d" % i, [128, 4, 128], BF16) for i in range(4)]
        lmT0 = sb("gs_lmT0", [128, 128], BF16)
        dl = sb("gs_dl", [128, NH])
        EGL = sb("gs_EGL", [128, NH])
        sqk = sb("gs_sqk", [128, NK, 128], BF16)
        rq = sb("gs_rq", [128, NK, 128])
        qn = sb("gs_qn", [128, NK, 128], BF16)
        kn = sb("gs_kn", [128, NK, 128], BF16)
        ktok = sb("gs_ktok", [128, NK, 128], BF16)
        vtok = sb("gs_vtok", [128, NH, 128], BF16)
        seg = [sb("gs_seg%d" % i, [128, 4, 128]) for i in range(2)]
        tm4 = [sb("gs_tm4%d" % i, [128, 4, 128]) for i in range(2)]
        dm4 = [sb("gs_dm4%d" % i, [128, 4, 128]) for i in range(2)]
        eg4 = [sb("gs_eg4%d" % i, [128, 4, 128]) for i in range(2)]
        At = sb("gs_At", [128, NH, 128], BF16)
        attnT = sb("gs_attn", [128, NH, 128], BF16)
        qg = sb("gs_qg", [128, NH, 128], BF16)
        Xn = sb("gs_Xn", [128, NH, 128], BF16)
        XT = sb("gs_XT", [128, NH, 128], BF16)
        tmpk = [sb("gs_tmpk%d" % i, [128, 4, 128]) for i in range(2)]
        vdn = sb("gs_vdn", [128, NH, 128], BF16)
        vnew = sb("gs_vnew", [128, NH, 128], BF16)
        o4 = [sb("gs_o4%d" % i, [128, 4, 128]) for i in range(4)]
        og = [sb("gs_og%d" % i, [128, 4, 128]) for i in range(2)]
        sq2 = [sb("gs_sq2%d" % i, [128, 4, 128], BF16) for i in range(2)]
        rstd = [sb("gs_rstd%d" % i, [128, 4, 128]) for i in range(2)]
        tmpS = [sb("gs_tmpS%d" % i, [128, 4, 128]) for i in range(2)]
        ygT = [sb("gs_ygT%d" % i, [128, NH, 128], BF16) for i in range(2)]

        S.dma("sp", maskT[:], c.maskT_d, writes=["mask"])
        S.dma("pool", lmask[:], c.lmask_d.rearrange("p (l i) -> p l i", i=128), writes=["lmask"])
        S.dma("sp", maskS[:], c.maskS_d, writes=["maskS"])
        S.dma("sp", lmT0[:], c.lmT0_d, writes=["lmT0"])
        STG = Stager(S)
        TG = Stager(S)
        BG = Background(S)
        S.dma("sp", identb[:], c.identb_d, writes=["idb"])
        S.dma("sp", nwv[:], c.gdn_nw[j], writes=["nw"])

        def cols(ch):
            return slice(ch * 128, (ch + 1) * 128)

        def g4(t, grp):
            return t[:, grp * 4:(grp + 1) * 4, :]

        for hb in range(HB):
            h0 = hb * NH
            k0 = hb * NK
            Sn = ["S%d" % g for g in range(NG)]
            Sbn = ["Sb%d" % g for g in range(NG)]
            S.op("pool", lambda e: e.memset(St[:], 0.0), writes=Sn)
            S.op("pool", lambda e: e.memset(Sb[:], 0.0), writes=Sbn)

            def load(ch):
                b = ch % 2
                rr = "(cc p) t -> p cc t"
                S.dma("sp", qT[b][:], c.pbuf[k0 * 128:(k0 + NK) * 128, cols(ch)].rearrange(rr, p=128), writes=["q%d" % b])
                S.dma("sp", kT[b][:], c.pbuf[2048 + k0 * 128:2048 + (k0 + NK) * 128, cols(ch)].rearrange(rr, p=128), writes=["k%d" % b])
                S.dma("act", vT[b][:], c.pbuf[4096 + h0 * 128:4096 + (h0 + NH) * 128, cols(ch)].rearrange(rr, p=128), writes=["v%d" % b])
                S.dma("act", zT[ch % 3][:], c.pbuf[8192 + h0 * 128:8192 + (h0 + NH) * 128, cols(ch)].rearrange(rr, p=128), writes=["z%d" % (ch % 3)])
                S.dma("sp", dta[b][:, 0, :], c.dtd[h0:h0 + NH, cols(ch)], writes=["dta%d" % b])
                S.dma("sp", dta[b][:, 1, :], c.acd[h0:h0 + NH, cols(ch)], writes=["dta%d" % b])

            load(0)
            for ch in range(NCH):
                b = ch % 2
                def gbc_load(ch_):
                    src = bass.AP(tensor=c.acd.tensor, offset=h0 * L + ch_ * 128, ap=[[0, 128], [L, NH], [1, 128]])
                    S.dma("sp", Gbc[:], src, writes=["gbc"])
                if ch == 0:
                    gbc_load(0)
                if ch + 1 < NCH:
                    load(ch + 1)
                def t1(e, b=b):
                    e.transpose(out=c.ps[:, 0, 0:NH], in_=dta[b][0:NH, 0, :], identity=c.identf[0:NH, 0:NH])
                    return e.transpose(out=c.ps[:, 0, NH:2 * NH], in_=dta[b][0:NH, 1, :], identity=c.identf[0:NH, 0:NH])
                S.op("pe", t1, reads=["dta%d" % b, "identf"], writes=["ps0"])
                S.op("dve", lambda e: e.tensor_copy(out=gk[:], in_=c.ps[:, 0, 0:2 * NH]), reads=["ps0"], writes=["gk"])
                S.op("act", lambda e: e.activation(out=eg[:], in_=gk[:, NH:2 * NH], func=AF.Exp), reads=["gk"], writes=["eg"])
                S.op("dve", lambda e: e.tensor_scalar(out=neg[:], in0=eg[:], scalar1=-1.0, scalar2=None, op0=ALU.mult), reads=["eg"], writes=["neg"])
                S.op("dve", lambda e: e.tensor_scalar(out=nbt[:], in0=gk[:, 0:NH], scalar1=-1.0, scalar2=None, op0=ALU.mult), reads=["gk"], writes=["nbt"])
                S.op("dve", lambda e: e.tensor_tensor(out=dl[:], in0=Gbc[:, :, 127], in1=gk[:, NH:2 * NH], op=ALU.subtract),
                     reads=["gbc", "gk"], writes=["dl"])
                S.op("act", lambda e: e.activation(out=dl[:], in_=dl[:], func=AF.Exp), reads=["dl"], writes=["dl"])
                S.op("act", lambda e: e.activation(out=EGL[:], in_=Gbc[:, :, 127], func=AF.Exp), reads=["gbc"], writes=["EGL"])
                BG.pump(3)
                for which, srcT, dst, scale, bias in ((0, qT[b], qn, 128.0, 128.0e-6), (1, kT[b], kn, 1.0, 1.0e-6)):
                    sname = ("q%d" if which == 0 else "k%d") % b
                    dname = "qn" if which == 0 else "kn"
                    S.op("act", lambda e, srcT=srcT: e.activation(out=sqk[:], in_=srcT[:], func=AF.Square), reads=[sname], writes=["sqk"])
                    nb = (NK * 128) // 512
                    for bb in range(nb):
                        bank = 1 + bb

                        def mmn(e, bb=bb, bank=bank):
                            ins = None
                            for q in range(4):
                                ins = e.matmul(c.ps[:, bank, q * 128:(q + 1) * 128], lhsT=c.onesb[:], rhs=sqk[:, bb * 4 + q, :], start=True, stop=True)
                            return ins
                        S.op("pe", mmn, reads=["sqk", "onesb"], writes=["ps%d" % bank])
                        S.op("act", lambda e, bb=bb, bank=bank, scale=scale, bias=bias: e.activation(
                            out=rq[:, bb * 4:(bb + 1) * 4, :], in_=c.ps[:, bank, :].rearrange("p (k i) -> p k i", i=128),
                            func=AF.Ln, scale=scale, bias=c.cb[:, which:which + 1]), reads=["ps%d" % bank, "cb"], writes=["rq%d" % bb])
                    rqn = ["rq%d" % bb for bb in range(nb)]
                    S.op("act", lambda e: e.activation(out=rq[:], in_=rq[:], func=AF.Exp, scale=-0.5), reads=rqn, writes=rqn)
                    BG.pump(3)
                    S.op("dve", lambda e, srcT=srcT, dst=dst: e.tensor_tensor(out=dst[:], in0=srcT[:], in1=rq[:], op=ALU.mult),
                         reads=rqn + [sname], writes=[dname])
                for bb in range((NK + 7) // 8):
                    bank = 3 + bb
                    nq = min(8, NK - bb * 8)

                    def t3(e, bb=bb, bank=bank, nq=nq):
                        ins = None
                        for q in range(nq):
                            ins = e.transpose(out=psbf(c, bank)[:, q * 128:(q + 1) * 128], in_=kn[:, bb * 8 + q, :], identity=identb[:])
                        return ins
                    S.op("pe", t3, reads=["kn", "idb"], writes=["ps%d" % bank])
                    S.op("act", lambda e, bb=bb, bank=bank, nq=nq: e.copy(
                        out=ktok[:, bb * 8:bb * 8 + nq, :], in_=psbf(c, bank)[:, 0:nq * 128].rearrange("p (g n) -> p g n", n=128)),
                        reads=["ps%d" % bank], writes=["ktok"])
                for bb in range(NH // 8):
                    bank = 4 + (bb % 2)

                    def t3v(e, bb=bb, bank=bank, b=b):
                        ins = None
                        for q in range(8):
                            ins = e.transpose(out=psbf(c, bank)[:, q * 128:(q + 1) * 128], in_=vT[b][:, bb * 8 + q, :], identity=identb[:])
                        return ins
                    S.op("pe", t3v, reads=["v%d" % b, "idb"], writes=["ps%d" % bank])
                    S.op("act", lambda e, bb=bb, bank=bank: e.copy(
                        out=vtok[:, bb * 8:(bb + 1) * 8, :], in_=psbf(c, bank).rearrange("p (g n) -> p g n", n=128)),
                        reads=["ps%d" % bank], writes=["vtok%d" % bb])
                BG.pump(5)
                for grp in range(NG):
                    STG.begin()
                    k2 = grp % 2
                    bkk, bqk = 1 + k2 * 2, 2 + k2 * 2

                    def t4(e, grp=grp, bkk=bkk, bqk=bqk):
                        ins = None
                        for r in range(2):
                            kh = grp * 2 + r
                            e.matmul(c.ps[:, bkk, r * 128:(r + 1) * 128], lhsT=kn[:, kh, :], rhs=kn[:, kh, :], start=True, stop=True)
                            ins = e.matmul(c.ps[:, bqk, r * 128:(r + 1) * 128], lhsT=kn[:, kh, :], rhs=qn[:, kh, :], start=True, stop=True)
                        return ins
                    STG.op("pe", t4, reads=["kn", "qn"], writes=["ps%d" % bkk, "ps%d" % bqk])
                    STG.op("dve", lambda e, grp=grp, k2=k2: e.tensor_tensor(out=seg[k2][:], in0=g4(Gbc, grp),
                                                                         in1=bcast(gk[:, NH + grp * 4:NH + grp * 4 + 4], 2, 128), op=ALU.subtract),
                         reads=["gbc", "gk"], writes=["seg%d" % k2])
                    STG.op("act", lambda e, k2=k2: e.activation(out=seg[k2][:], in_=seg[k2][:], func=AF.Relu, scale=-1.0), reads=["seg%d" % k2], writes=["seg%d" % k2])
                    STG.op("act", lambda e, k2=k2: e.activation(out=seg[k2][:], in_=seg[k2][:], func=AF.Exp, scale=-1.0), reads=["seg%d" % k2], writes=["seg%d" % k2])
                    STG.op("pool", lambda e, k2=k2: e.tensor_tensor(out=tm4[k2][:], in0=seg[k2][:], in1=bcast(maskS[:], 1, 4), op=ALU.mult),
                         reads=["seg%d" % k2, "maskS"], writes=["tm4%d" % k2])
                    STG.op("pool", lambda e, k2=k2: e.tensor_tensor(out=dm4[k2][:], in0=seg[k2][:], in1=bcast(maskT[:], 1, 4), op=ALU.mult),
                         reads=["seg%d" % k2, "mask"], writes=["dm4%d" % k2])

                    def v22(ap):
                        return ap.rearrange("p (k r) i -> p k r i", r=2)
                    kkv = bcast(c.ps[:, bkk, 0:256].rearrange("p (k i) -> p k i", i=128), 2, 2)
                    qkv = bcast(c.ps[:, bqk, 0:256].rearrange("p (k i) -> p k i", i=128), 2, 2)
                    def mkAt(e, grp=grp, k2=k2, bkk=bkk):
                        ins = None
                        for q in range(4):
                            h = grp * 4 + q
                            r = q // 2
                            ins = e.scalar_tensor_tensor(out=At[:, h, :], in0=c.ps[:, bkk, r * 128:(r + 1) * 128], scalar=nbt[:, h:h + 1],
                                                         in1=tm4[k2][:, q, :], op0=ALU.mult, op1=ALU.mult)
                        return ins
                    STG.op("dve", mkAt, reads=["ps%d" % bkk, "tm4%d" % k2, "nbt"], writes=["At%d" % grp])
                    STG.op("dve", lambda e, grp=grp: e.tensor_tensor(out=g4(At, grp), in0=g4(At, grp), in1=bcast(identb[:], 1, 4), op=ALU.add),
                         reads=["At%d" % grp, "idb"], writes=["At%d" % grp])
                    STG.op("dve", lambda e, grp=grp, k2=k2, qkv=qkv: e.tensor_tensor(out=v22(g4(attnT, grp)), in0=qkv, in1=v22(dm4[k2][:]), op=ALU.mult),
                         reads=["ps%d" % bqk, "dm4%d" % k2], writes=["attn%d" % grp])
                    STG.op("act", lambda e, grp=grp, k2=k2: e.activation(out=eg4[k2][:], in_=g4(Gbc, grp), func=AF.Exp), reads=["gbc"], writes=["eg4%d" % k2])
                    STG.op("pool", lambda e, grp=grp, k2=k2: e.tensor_tensor(out=v22(g4(qg, grp)), in0=bcast(qn[:, grp * 2:grp * 2 + 2, :], 2, 2),
                                                                          in1=v22(eg4[k2][:]), op=ALU.mult),
                         reads=["qn", "eg4%d" % k2], writes=["qg%d" % grp])
                    if grp % 2 == 1:
                        STG.flush()
                if ch + 1 < NCH:
                    gbc_load(ch + 1)
                BG.drain()
                vr = lambda ap: ap.rearrange("p (q i) -> p q i", i=128)
                for grp in range(NG):
                    S.op("pool", lambda e, grp=grp: e.tensor_tensor(out=g4(XT, grp), in0=g4(At, grp), in1=bcast(lmT0[:], 1, 4), op=ALU.mult),
                         reads=["At%d" % grp, "lmT0"], writes=["XT%d" % grp])
                for grp in range(NG):
                    def mt0(e, grp=grp):
                        ins = None
                        for q in range(4):
                            h = grp * 4 + q
                            ins = e.matmul(c.ps[:, grp % 4, q * 128:(q + 1) * 128], lhsT=XT[:, h, :], rhs=identb[:], start=True, stop=True)
                        return ins
                    S.op("pe", mt0, reads=["XT%d" % grp, "idb"], writes=["ps%d" % (grp % 4)])
                for grp in range(NG):
                    S.op("act", lambda e, grp=grp: e.copy(out=g4(Xn, grp), in_=vr(c.ps[:, grp % 4, :])),
                         reads=["ps%d" % (grp % 4)], writes=["Xn%d" % grp])
                for lv in range(1, 7):
                    for grp in range(NG):
                        def mp(e, grp=grp):
                            ins = None
                            for q in range(4):
                                h = grp * 4 + q
                                ins = e.matmul(c.ps[:, grp % 4, q * 128:(q + 1) * 128], lhsT=At[:, h, :], rhs=Xn[:, h, :], start=True, stop=True)
                            return ins
                        S.op("pe", mp, reads=["At%d" % grp, "Xn%d" % grp], writes=["ps%d" % (grp % 4)])
                    for grp in range(NG):
                        S.op("dve", lambda e, grp=grp, lv=lv: e.tensor_tensor(out=Wt[grp % 4][:], in0=vr(c.ps[:, grp % 4, :]),
                                                                             in1=bcast(lmask[:, lv, :], 1, 4), op=ALU.mult),
                             reads=["ps%d" % (grp % 4), "lmask"], writes=["W%d" % (grp % 4)])
                    for half in range(NG // 2):
                        for grp in (2 * half, 2 * half + 1):
                            k2 = grp % 2
                            px, pt = 4 + k2 * 2, 5 + k2 * 2
                            if lv < 6:
                                def mx(e, grp=grp, px=px):
                                    ins = None
                                    for q in range(4):
                                        h = grp * 4 + q
                                        ins = e.matmul(c.ps[:, px, q * 128:(q + 1) * 128], lhsT=XT[:, h, :], rhs=Wt[grp % 4][:, q, :], start=True, stop=True)
                                    return ins
                                S.op("pe", mx, reads=["XT%d" % grp, "W%d" % (grp % 4)], writes=["ps%d" % px])

                            def mt(e, grp=grp, pt=pt):
                                ins = None
                                for q in range(4):
                                    h = grp * 4 + q
                                    ins = e.matmul(c.ps[:, pt, q * 128:(q + 1) * 128], lhsT=Wt[grp % 4][:, q, :], rhs=XT[:, h, :], start=True, stop=True)
                                return ins
                            S.op("pe", mt, reads=["XT%d" % grp, "W%d" % (grp % 4)], writes=["ps%d" % pt])
                        for grp in (2 * half, 2 * half + 1):
                            k2 = grp % 2
                            px, pt = 4 + k2 * 2, 5 + k2 * 2
                            if lv < 6:
                                S.op("act", lambda e, grp=grp, px=px: e.copy(out=g4(Xn, grp), in_=vr(c.ps[:, px, :])),
                                     reads=["ps%d" % px], writes=["Xn%d" % grp])
                            S.op("act", lambda e, grp=grp, pt=pt: e.copy(out=g4(XT, grp), in_=vr(c.ps[:, pt, :])),
                                 reads=["ps%d" % pt], writes=["XT%d" % grp])
                BG.drain()
                ob = ch % 2
                for grp in range(NG):
                    STG.begin()
                    k2 = grp % 2
                    b1, b2, b3, b4 = k2 * 3, k2 * 3, k2 * 3 + 1, k2 * 3 + 2
                    bt = 6 + k2
                    o4i = grp % 4
                    zb = ch % 3
                    vr4 = lambda ap: ap.rearrange("p (q i) -> p q i", i=128)

                    def mks(e, grp=grp, b1=b1):
                        ins = None
                        for q in range(4):
                            h = grp * 4 + q
                            ins = e.matmul(c.ps[:, b1, q * 128:(q + 1) * 128], lhsT=kn[:, h // 2, :], rhs=Sb[:, h, :], start=True, stop=True)
                        return ins
                    STG.op("pe", mks, reads=["kn", "Sb%d" % grp], writes=["ps%d" % b1])
                    STG.op("dve", lambda e, grp=grp, k2=k2, b1=b1: e.tensor_tensor(out=tmpk[k2][:], in0=vr4(c.ps[:, b1, :]),
                                                                                in1=bcast(neg[:, grp * 4:grp * 4 + 4], 2, 128), op=ALU.mult),
                         reads=["ps%d" % b1, "neg"], writes=["tmpk%d" % k2])
                    STG.op("pool", lambda e, grp=grp, k2=k2: e.tensor_tensor(out=g4(vdn, grp), in0=tmpk[k2][:], in1=g4(vtok, grp), op=ALU.add),
                         reads=["tmpk%d" % k2, "vtok%d" % (grp // 2)], writes=["vdn%d" % grp])

                    def mtv(e, grp=grp, b2=b2):
                        ins = None
                        for q in range(4):
                            h = grp * 4 + q
                            ins = e.matmul(c.ps[:, b2, q * 128:(q + 1) * 128], lhsT=XT[:, h, :], rhs=vdn[:, h, :], start=True, stop=True)
                        return ins
                    STG.op("pe", mtv, reads=["XT%d" % grp, "vdn%d" % grp], writes=["ps%d" % b2])
                    STG.op("dve", lambda e, grp=grp, b2=b2: e.tensor_tensor(out=g4(vnew, grp), in0=vr4(c.ps[:, b2, :]),
                                                                         in1=bcast(gk[:, grp * 4:grp * 4 + 4], 2, 128), op=ALU.mult),
                         reads=["ps%d" % b2, "gk"], writes=["vnew%d" % grp])
                    STG.op("pool", lambda e, grp=grp: e.tensor_tensor(out=g4(vdn, grp), in0=g4(vnew, grp),
                                                                   in1=bcast(dl[:, grp * 4:grp * 4 + 4], 2, 128), op=ALU.mult),
                         reads=["vnew%d" % grp, "dl"], writes=["vdn%d" % grp])

                    def mo(e, grp=grp, b3=b3):
                        ins = None
                        for q in range(4):
                            h = grp * 4 + q
                            e.matmul(c.ps[:, b3, q * 128:(q + 1) * 128], lhsT=qg[:, h, :], rhs=Sb[:, h, :], start=True, stop=False)
                            ins = e.matmul(c.ps[:, b3, q * 128:(q + 1) * 128], lhsT=attnT[:, h, :], rhs=vnew[:, h, :], start=False, stop=True)
                        return ins
                    STG.op("pe", mo, reads=["qg%d" % grp, "Sb%d" % grp, "attn%d" % grp, "vnew%d" % grp], writes=["ps%d" % b3])
                    def mst(e, grp=grp, b4=b4):
                        ins = None
                        for q in range(4):
                            h = grp * 4 + q
                            ins = e.matmul(c.ps[:, b4, q * 128:(q + 1) * 128], lhsT=ktok[:, h // 2, :], rhs=vdn[:, h, :], start=True, stop=True)
                        return ins
                    STG.op("pe", mst, reads=["ktok", "vdn%d" % grp], writes=["ps%d" % b4])
                    STG.op("pool", lambda e, grp=grp, k2=k2: e.tensor_tensor(out=tmpS[k2][:], in0=g4(St, grp),
                                                                          in1=bcast(EGL[:, grp * 4:grp * 4 + 4], 2, 128), op=ALU.mult),
                         reads=["S%d" % grp, "EGL"], writes=["tmpS%d" % k2])
                    STG.op("dve", lambda e, grp=grp, k2=k2, b4=b4: e.tensor_tensor(out=g4(St, grp), in0=tmpS[k2][:], in1=vr4(c.ps[:, b4, :]), op=ALU.add),
                         reads=["tmpS%d" % k2, "ps%d" % b4], writes=["S%d" % grp])
                    STG.op("act", lambda e, grp=grp: e.copy(out=g4(Sb, grp), in_=g4(St, grp)), reads=["S%d" % grp], writes=["Sb%d" % grp])
                    STG.op("act", lambda e, o4i=o4i, b3=b3: e.copy(out=o4[o4i][:], in_=vr4(c.ps[:, b3, :])), reads=["ps%d" % b3], writes=["o4%d" % o4i])
                    TG.begin()

                    def mtr(e, o4i=o4i, bt=bt):
                        ins = None
                        for q in range(4):
                            ins = e.transpose(out=c.ps[:, bt, q * 128:(q + 1) * 128], in_=o4[o4i][:, q, :], identity=c.identf[:])
                        return ins
                    TG.op("pe", mtr, reads=["o4%d" % o4i, "identf"], writes=["ps%d" % bt])
                    TG.op("act", lambda e, k2=k2, bt=bt: e.copy(out=og[k2][:], in_=vr4(c.ps[:, bt, :])), reads=["ps%d" % bt], writes=["og%d" % k2])
                    TG.op("act", lambda e, k2=k2, bt=bt: e.activation(out=sq2[k2][:], in_=vr4(c.ps[:, bt, :]), func=AF.Square), reads=["ps%d" % bt], writes=["sq2%d" % k2])

                    def mn2(e, k2=k2, bt=bt):
                        ins = None
                        for q in range(4):
                            ins = e.matmul(c.ps[:, bt, q * 128:(q + 1) * 128], lhsT=c.onesb[:], rhs=sq2[k2][:, q, :], start=True, stop=True)
                        return ins
                    TG.op("pe", mn2, reads=["sq2%d" % k2, "onesb"], writes=["ps%d" % bt])
                    TG.op("act", lambda e, k2=k2, bt=bt: e.activation(out=rstd[k2][:], in_=vr4(c.ps[:, bt, :]), func=AF.Ln, scale=1.0 / 128, bias=c.cb[:, 1:2]),
                          reads=["ps%d" % bt, "cb"], writes=["rstd%d" % k2])
                    TG.op("act", lambda e, k2=k2: e.activation(out=rstd[k2][:], in_=rstd[k2][:], func=AF.Exp, scale=-0.5), reads=["rstd%d" % k2], writes=["rstd%d" % k2])
                    TG.op("dve", lambda e, k2=k2: e.tensor_tensor(out=og[k2][:], in0=og[k2][:], in1=rstd[k2][:], op=ALU.mult),
                          reads=["og%d" % k2, "rstd%d" % k2], writes=["og%d" % k2])
                    TG.op("pool", lambda e, grp=grp, k2=k2, zb=zb, ob=ob: e.tensor_tensor(out=g4(ygT[ob], grp), in0=og[k2][:], in1=g4(zT[zb], grp), op=ALU.mult),
                          reads=["og%d" % k2, "z%d" % zb], writes=["ygT%d_%d" % (ob, grp)])
                    if grp % 2 == 1:
                        STG.flush()
                        BG.add(TG)
                BG.add_dma("sp", c.ybuf[h0 * 128:(h0 + NH) * 128, cols(ch)].rearrange("(cc p) t -> p cc t", p=128), ygT[ob][:],
                           reads=["ygT%d_%d" % (ob, grp) for grp in range(NG)], writes=["ybuf%d_%d" % (hb, ch)])
            BG.drain()
        S.barrier()


def gdn_chunks():
    ch = []
    for i in range(64):
        ch.append((i * 128, 128, "conv", i * 128, i))
    for i in range(32):
        ch.append((8192 + i * 128, 128, "silu", 8192 + i * 128, 0))
    ch.append((12288, 64, "raw", 0, 0))
    return ch


def phase_gdn_layer(c, l):
    j = l // 2
    phase_inproj(c, l, c.gdn_w_in[j], gdn_chunks(), c.gdn_cw[j], 64, c.gain[:, l, 0, :], c.modv[:, l, 0:KC])
    phase_gdn_scan(c, l)
    phase_down(c, c.gdn_w_out[j], 32, c.ybuf, "ybuf", c.modv[:, l, 2 * KC:3 * KC], rowscale_dram=c.gdn_nwx[j])
```
